# Optimizing a Trainium2 kernel written in Bass

```python
import math
import jax, jax.numpy as jnp
from jax import lax
import numpy as np

D_MODEL = 1024
BATCH = 8
SEQ = 2048
DEPTH = 1
DEC_BATCH = 16
DEC_SEQ = 64
PAST_LEN = 1024

CHUNK = 64
Q_BLOCK = 128
ATTN_WIDTH = D_MODEL // 2
SSM_WIDTH = D_MODEL - ATTN_WIDTH
N_HEADS = 8
HEAD_DIM = ATTN_WIDTH // N_HEADS
HALF_DIM = HEAD_DIM // 2
ROT_DIM = HALF_DIM // 4
ROPE_THETA = 500000.0
SSM_GROUP = 16
N_SSM_GROUPS = SSM_WIDTH // SSM_GROUP
SSM_STATE = 64
D_FF = 4 * D_MODEL
IN_WIDTH = 3 * ATTN_WIDTH + SSM_WIDTH
LN_EPS = 1e-5
SUBLN_EPS = 1e-5
DEEPNORM_ALPHA = (2 * DEPTH) ** 0.25
DEEPNORM_BETA = (8 * DEPTH) ** -0.25

kernel_name = "hymba_diffattn_s5_streaming_step"


def lambda_init_fn(layer_idx):
    return 0.8 - 0.6 * math.exp(-0.3 * layer_idx)


def layer_norm(x, g, b):
    xf = x.astype(jnp.float32)
    mu = jnp.mean(xf, axis=-1, keepdims=True)
    var = jnp.mean(jnp.square(xf - mu), axis=-1, keepdims=True)
    return ((xf - mu) * lax.rsqrt(var + LN_EPS) * g + b).astype(x.dtype)


def partial_rope(x, pos):
    inv = ROPE_THETA ** (-jnp.arange(0, ROT_DIM, 2, dtype=jnp.float32) / ROT_DIM)
    ang = pos.astype(jnp.float32)[:, None] * inv[None, :]
    cos = jnp.cos(ang)[None, :, None, :]
    sin = jnp.sin(ang)[None, :, None, :]
    xr = x[..., :ROT_DIM].astype(jnp.float32)
    x1, x2 = xr[..., :ROT_DIM // 2], xr[..., ROT_DIM // 2:]
    rot = jnp.concatenate([x1 * cos - x2 * sin, x2 * cos + x1 * sin], axis=-1).astype(x.dtype)
    return jnp.concatenate([rot, x[..., ROT_DIM:]], axis=-1)


def diff_attn_core(q, k, v, q_pos, k_pos, lam, subln_g, lam_init):
    s = jnp.einsum('bqhd,bkhd->bhqk', q, k).astype(jnp.float32) * (HALF_DIM ** -0.5)
    mask = (q_pos // CHUNK)[:, None] >= (k_pos // CHUNK)[None, :]
    s = jnp.where(mask[None, None], s, -jnp.inf)
    a = jax.nn.softmax(s, axis=-1)
    bsz, _, lq, sk = a.shape
    a = a.reshape(bsz, N_HEADS, 2, lq, sk)
    w = (a[:, :, 0] - lam * a[:, :, 1]).astype(v.dtype)
    o = jnp.einsum('bhqk,bkhd->bqhd', w, v).astype(jnp.float32)
    o = o * lax.rsqrt(jnp.mean(o * o, axis=-1, keepdims=True) + SUBLN_EPS) * subln_g
    return (o * (1.0 - lam_init)).astype(v.dtype)


def diff_attn_blocked(q, k, v, pos, lam, subln_g, lam_init):
    bsz, seq = q.shape[0], q.shape[1]
    nb = seq // Q_BLOCK
    qb = q.reshape(bsz, nb, Q_BLOCK, 2 * N_HEADS, HALF_DIM).transpose(1, 0, 2, 3, 4)
    pb = pos.reshape(nb, Q_BLOCK)
    ob = lax.map(lambda qp: diff_attn_core(qp[0], k, v, qp[1], pos, lam, subln_g, lam_init), (qb, pb))
    return ob.transpose(1, 0, 2, 3, 4).reshape(bsz, seq, N_HEADS, HEAD_DIM)


def s5_discretise(a_re, a_im, log_dt, b_re, b_im):
    dt = jnp.exp(log_dt.astype(jnp.float32))[:, None]
    ar = a_re.astype(jnp.float32)
    ai = a_im.astype(jnp.float32)
    mag = jnp.exp(ar * dt)
    lb_re = mag * jnp.cos(ai * dt)
    lb_im = mag * jnp.sin(ai * dt)
    nr = lb_re - 1.0
    ni = lb_im
    den = ar * ar + ai * ai
    f_re = ((nr * ar + ni * ai) / den)[..., None]
    f_im = ((ni * ar - nr * ai) / den)[..., None]
    br = b_re.astype(jnp.float32)
    bi = b_im.astype(jnp.float32)
    bb_re = f_re * br - f_im * bi
    bb_im = f_re * bi + f_im * br
    return lb_re, lb_im, bb_re, bb_im


def _complex_affine_combine(e1, e2):
    a1r, a1i, b1r, b1i = e1
    a2r, a2i, b2r, b2i = e2
    return (a2r * a1r - a2i * a1i,
            a2r * a1i + a2i * a1r,
            a2r * b1r - a2i * b1i + b2r,
            a2r * b1i + a2i * b1r + b2i)


def s5_ssm(u, h0_re, h0_im, a_re, a_im, log_dt, b_re, b_im, c_re, c_im, d):
    lb_re, lb_im, bb_re, bb_im = s5_discretise(a_re, a_im, log_dt, b_re, b_im)
    uf = u.astype(jnp.float32)
    bu_re = jnp.einsum('blgc,gpc->blgp', uf, bb_re)
    bu_im = jnp.einsum('blgc,gpc->blgp', uf, bb_im)
    shape = bu_re.shape
    acum_re, acum_im, h_re, h_im = lax.associative_scan(
        _complex_affine_combine,
        (jnp.broadcast_to(lb_re, shape), jnp.broadcast_to(lb_im, shape), bu_re, bu_im),
        axis=1)
    if h0_re is not None:
        r0 = h0_re.astype(jnp.float32)[:, None]
        i0 = h0_im.astype(jnp.float32)[:, None]
        h_re, h_im = (h_re + acum_re * r0 - acum_im * i0,
                      h_im + acum_re * i0 + acum_im * r0)
    y = (jnp.einsum('blgp,gcp->blgc', h_re, c_re.astype(jnp.float32))
         - jnp.einsum('blgp,gcp->blgc', h_im, c_im.astype(jnp.float32))
         + d.astype(jnp.float32) * uf)
    return y.astype(u.dtype), h_re[:, -1], h_im[:, -1]


def hybrid_layer(x, pos, cache_k, cache_v, h0_re, h0_im, p, lam_init):
    bsz, seq, _ = x.shape
    proj = x @ p['w_in']
    q, k, v, u = jnp.split(proj, [ATTN_WIDTH, 2 * ATTN_WIDTH, 3 * ATTN_WIDTH], axis=-1)
    q = partial_rope(q.reshape(bsz, seq, 2 * N_HEADS, HALF_DIM), pos)
    k = partial_rope(k.reshape(bsz, seq, 2 * N_HEADS, HALF_DIM), pos)
    v = v.reshape(bsz, seq, N_HEADS, HEAD_DIM)
    new_k = k.reshape(bsz, seq, N_HEADS, HEAD_DIM)
    lam = (jnp.exp(jnp.sum(p['lambda_q1'].astype(jnp.float32) * p['lambda_k1'].astype(jnp.float32)))
           - jnp.exp(jnp.sum(p['lambda_q2'].astype(jnp.float32) * p['lambda_k2'].astype(jnp.float32)))
           + lam_init)
    if cache_k is None:
        attn = diff_attn_blocked(q, k, v, pos, lam, p['subln_g'], lam_init)
    else:
        past = cache_k.shape[1]
        k_all = jnp.concatenate([cache_k.reshape(bsz, past, 2 * N_HEADS, HALF_DIM), k], axis=1)
        v_all = jnp.concatenate([cache_v, v], axis=1)
        k_pos = jnp.arange(past + seq, dtype=jnp.int32)
        attn = diff_attn_core(q, k_all, v_all, pos, k_pos, lam, p['subln_g'], lam_init)
    y_ssm, h_re, h_im = s5_ssm(u.reshape(bsz, seq, N_SSM_GROUPS, SSM_GROUP), h0_re, h0_im,
                               p['ssm_a_re'], p['ssm_a_im'], p['ssm_log_dt'], p['ssm_b_re'],
                               p['ssm_b_im'], p['ssm_c_re'], p['ssm_c_im'], p['ssm_d'])
    z = jax.nn.gelu(y_ssm.reshape(bsz, seq, SSM_WIDTH)) @ p['w_glu'] + p['b_glu']
    ssm_out = z[..., :SSM_WIDTH] * jax.nn.sigmoid(z[..., SSM_WIDTH:])
    mix = jnp.concatenate([attn.reshape(bsz, seq, ATTN_WIDTH), ssm_out], axis=-1) @ p['w_out']
    x1 = layer_norm(DEEPNORM_ALPHA * x + mix, p['ln1_g'], p['ln1_b'])
    ff = jnp.square(jax.nn.relu(x1 @ p['w_up'])) @ p['w_down']
    x2 = layer_norm(DEEPNORM_ALPHA * x1 + ff, p['ln2_g'], p['ln2_b'])
    return x2, new_k, v, h_re, h_im


def setup_inputs(seed: int = 0) -> dict:
    key = jax.random.key(seed)
    ks = jax.random.split(key, 32)
    f32 = jnp.float32
    nrm = lambda k, s, sc: jax.random.normal(k, s, f32) * sc
    G, P, C = N_SSM_GROUPS, SSM_STATE, SSM_GROUP
    n_idx = jnp.arange(P, dtype=f32)
    return {
        'x_prompt': nrm(ks[0], (BATCH, SEQ, D_MODEL), 1.0),
        'x_sample': nrm(ks[1], (DEC_BATCH, DEC_SEQ, D_MODEL), 1.0),
        'cache_k': nrm(ks[2], (DEPTH, DEC_BATCH, PAST_LEN, N_HEADS, HEAD_DIM), 1.0),
        'cache_v': nrm(ks[3], (DEPTH, DEC_BATCH, PAST_LEN, N_HEADS, HEAD_DIM), 1.0),
        'state_ssm_re': nrm(ks[4], (DEPTH, DEC_BATCH, G, P), 0.1),
        'state_ssm_im': nrm(ks[5], (DEPTH, DEC_BATCH, G, P), 0.1),
        'w_in': nrm(ks[6], (DEPTH, D_MODEL, IN_WIDTH), D_MODEL ** -0.5),
        'lambda_q1': nrm(ks[7], (DEPTH, HALF_DIM), 0.1),
        'lambda_k1': nrm(ks[8], (DEPTH, HALF_DIM), 0.1),
        'lambda_q2': nrm(ks[9], (DEPTH, HALF_DIM), 0.1),
        'lambda_k2': nrm(ks[10], (DEPTH, HALF_DIM), 0.1),
        'subln_g': 1.0 + nrm(ks[11], (DEPTH, HEAD_DIM), 0.01),
        'ssm_a_re': -0.5 + nrm(ks[12], (DEPTH, G, P), 0.01),
        'ssm_a_im': math.pi * n_idx + nrm(ks[13], (DEPTH, G, P), 0.01),
        'ssm_log_dt': jax.random.uniform(ks[14], (DEPTH, G), f32, math.log(0.001), math.log(0.1)),
        'ssm_b_re': nrm(ks[15], (DEPTH, G, P, C), (2 * C) ** -0.5),
        'ssm_b_im': nrm(ks[16], (DEPTH, G, P, C), (2 * C) ** -0.5),
        'ssm_c_re': nrm(ks[17], (DEPTH, G, C, P), (2 * P) ** -0.5),
        'ssm_c_im': nrm(ks[18], (DEPTH, G, C, P), (2 * P) ** -0.5),
        'ssm_d': nrm(ks[19], (DEPTH, G, C), 1.0),
        'w_glu': nrm(ks[20], (DEPTH, SSM_WIDTH, 2 * SSM_WIDTH), SSM_WIDTH ** -0.5),
        'b_glu': nrm(ks[21], (DEPTH, 2 * SSM_WIDTH), 0.01),
        'w_out': nrm(ks[22], (DEPTH, D_MODEL, D_MODEL), DEEPNORM_BETA * D_MODEL ** -0.5),
        'ln1_g': 1.0 + nrm(ks[23], (DEPTH, D_MODEL), 0.01),
        'ln1_b': nrm(ks[24], (DEPTH, D_MODEL), 0.01),
        'w_up': nrm(ks[25], (DEPTH, D_MODEL, D_FF), D_MODEL ** -0.5),
        'w_down': nrm(ks[26], (DEPTH, D_FF, D_MODEL), DEEPNORM_BETA * D_FF ** -0.5),
        'ln2_g': 1.0 + nrm(ks[27], (DEPTH, D_MODEL), 0.01),
        'ln2_b': nrm(ks[28], (DEPTH, D_MODEL), 0.01),
    }


def reference(x_prompt, x_sample, cache_k, cache_v, state_ssm_re, state_ssm_im,
              w_in, lambda_q1, lambda_k1, lambda_q2, lambda_k2, subln_g,
              ssm_a_re, ssm_a_im, ssm_log_dt, ssm_b_re, ssm_b_im, ssm_c_re, ssm_c_im, ssm_d,
              w_glu, b_glu, w_out, ln1_g, ln1_b, w_up, w_down, ln2_g, ln2_b):
    seq = x_prompt.shape[1]
    past = cache_k.shape[2]
    dec_seq = x_sample.shape[1]
    pos_prompt = jnp.arange(seq, dtype=jnp.int32)
    pos_sample = past + jnp.arange(dec_seq, dtype=jnp.int32)
    hp, hs = x_prompt, x_sample
    kp_l, vp_l, rp_l, ip_l, ks_l, vs_l, rs_l, is_l = [], [], [], [], [], [], [], []
    for l in range(DEPTH):
        p = dict(w_in=w_in[l], lambda_q1=lambda_q1[l], lambda_k1=lambda_k1[l],
                 lambda_q2=lambda_q2[l], lambda_k2=lambda_k2[l], subln_g=subln_g[l],
                 ssm_a_re=ssm_a_re[l], ssm_a_im=ssm_a_im[l], ssm_log_dt=ssm_log_dt[l],
                 ssm_b_re=ssm_b_re[l], ssm_b_im=ssm_b_im[l], ssm_c_re=ssm_c_re[l],
                 ssm_c_im=ssm_c_im[l], ssm_d=ssm_d[l], w_glu=w_glu[l], b_glu=b_glu[l],
                 w_out=w_out[l], ln1_g=ln1_g[l], ln1_b=ln1_b[l], w_up=w_up[l],
                 w_down=w_down[l], ln2_g=ln2_g[l], ln2_b=ln2_b[l])
        lam_init = lambda_init_fn(l)
        hp, kp, vp, rp, ip = hybrid_layer(hp, pos_prompt, None, None, None, None, p, lam_init)
        hs, ksn, vsn, rsn, isn = hybrid_layer(hs, pos_sample, cache_k[l], cache_v[l],
                                              state_ssm_re[l], state_ssm_im[l], p, lam_init)
        kp_l.append(kp); vp_l.append(vp); rp_l.append(rp); ip_l.append(ip)
        ks_l.append(ksn); vs_l.append(vsn); rs_l.append(rsn); is_l.append(isn)
    return (hp, hs,
            jnp.stack(kp_l), jnp.stack(vp_l), jnp.stack(rp_l), jnp.stack(ip_l),
            jnp.stack(ks_l), jnp.stack(vs_l), jnp.stack(rs_l), jnp.stack(is_l))
```

```python
import math
from contextlib import ExitStack
import numpy as np
import concourse.bass as bass
import concourse.mybir as mybir
from concourse.bass_utils import run_bass_kernel_spmd

F32 = mybir.dt.float32
BF16 = mybir.dt.bfloat16
I32 = mybir.dt.int32
U8 = mybir.dt.uint8
ALU = mybir.AluOpType
AF = mybir.ActivationFunctionType
AX = mybir.AxisListType

ENG = ("pe", "act", "dve", "pool", "sp")

NT = 17
NTOK = NT * 128
KC = 272
LAM_INIT = 0.8 - 0.6 * math.exp(-0.3 * 0)
ALPHA = 2.0 ** 0.25
ROPE_INV = [500000.0 ** (-(2 * i) / 8.0) for i in range(4)]
TWO_PI = 2.0 * math.pi


class Res:
    __slots__ = ("w", "rd")

    def __init__(self):
        self.w = None
        self.rd = []


class Buf:
    def __init__(self, t, nslots=1):
        self.t = t
        self.res = [Res() for _ in range(nslots)]

    def __getitem__(self, k):
        return self.t[k]

    def s(self, slot=None):
        if slot is None:
            return list(self.res)
        if isinstance(slot, (list, tuple, range)):
            return [self.res[i] for i in slot]
        return [self.res[slot]]


class Op:
    __slots__ = ("eng", "fn", "deps", "signal", "semval", "dma", "dsem", "dval", "prev_same_sem")

    def __init__(self, eng, fn, dma):
        self.eng = eng
        self.fn = fn
        self.deps = []
        self.signal = False
        self.semval = None
        self.dma = dma
        self.dsem = None
        self.dval = None
        self.prev_same_sem = None


class Prog:
    def __init__(self, nc, n_dma_sems=(12, 4, 40)):
        self.nc = nc
        self.ops = {e: [] for e in ENG}
        self.n_dma = {"sp": n_dma_sems[0], "act": n_dma_sems[1], "pool": n_dma_sems[2]}
        self.dma_cnt = {"sp": 0, "act": 0, "pool": 0}
        self.dma_last = {}
        self.dma_uses = {}
        self.bar = {e: [] for e in ENG}

    def _add(self, eng, fn, reads, writes, dma=False):
        op = Op(eng, fn, dma)
        deps = list(self.bar[eng])
        self.bar[eng] = []
        for r in reads:
            if r.w is not None:
                deps.append(r.w)
        for w in writes:
            if w.w is not None:
                deps.append(w.w)
            deps.extend(w.rd)
        seen = set()
        for d in deps:
            if d is op or id(d) in seen:
                continue
            seen.add(id(d))
            if (not d.dma) and d.eng == "pe" and eng == "pe" and not dma:
                continue
            op.deps.append(d)
            d.signal = True
        for r in reads:
            r.rd.append(op)
        for w in writes:
            w.w = op
            w.rd = []
        self.ops[eng].append(op)
        if dma:
            k = self.dma_cnt[eng] % self.n_dma[eng]
            self.dma_cnt[eng] += 1
            key = (eng, k)
            op.prev_same_sem = self.dma_last.get(key)
            self.dma_uses[key] = self.dma_uses.get(key, 0) + 1
            op.dsem = key
            op.dval = 16 * self.dma_uses[key]
            self.dma_last[key] = op
        return op

    rec = None

    def op(self, eng, fn, reads=(), writes=()):
        if self.rec is not None:
            self.rec.append((eng, fn, list(reads), list(writes), False))
            return None
        return self._add(eng, fn, list(reads), list(writes), False)

    def dma(self, eng, out, in_, reads=(), writes=(), **kw):
        def fn(e):
            return e.dma_start(out=out, in_=in_, **kw)
        if self.rec is not None:
            self.rec.append((eng, fn, list(reads), list(writes), True))
            return None
        return self._add(eng, fn, list(reads), list(writes), True)

    def flush(self, lst, n):
        for _ in range(min(n, len(lst))):
            self._add(*lst.pop(0))

    def barrier(self, include_dma=True):
        lasts = []
        for e in ENG:
            if self.ops[e]:
                lasts.append(self.ops[e][-1])
        if include_dma:
            for key, op in self.dma_last.items():
                if include_dma is True or key[0] in include_dma:
                    lasts.append(op)
        for e in ENG:
            self.bar[e] = list(lasts)

    def emit(self, stack):
        nc = self.nc
        sems = {e: stack.enter_context(nc.semaphore("s_" + e)) for e in ENG}
        dsems = {}
        for e, n in self.n_dma.items():
            for k in range(n):
                dsems[(e, k)] = stack.enter_context(nc.semaphore("d_%s%d" % (e, k)))
        for e in ENG:
            c = 0
            for op in self.ops[e]:
                if op.dma:
                    continue
                if op.signal:
                    c += 1
                    op.semval = c
        block = stack.enter_context(nc.Block())
        prog = self

        def body(ename):
            def f(eng):
                waited = {}

                def wait(sem, key, val):
                    if waited.get(key, 0) >= val:
                        return
                    waited[key] = val
                    eng.wait_ge(sem, val)

                for op in prog.ops[ename]:
                    for d in op.deps:
                        if d.dma:
                            wait(dsems[d.dsem], d.dsem, d.dval)
                        else:
                            wait(sems[d.eng], d.eng, d.semval)
                    if op.dma:
                        p = op.prev_same_sem
                        if p is not None:
                            wait(dsems[p.dsem], p.dsem, p.dval)
                        inst = op.fn(eng)
                        inst.then_inc(dsems[op.dsem], 16)
                    else:
                        inst = op.fn(eng)
                        if op.signal:
                            inst.then_inc(sems[ename], 1)
                if ename == "sp":
                    for key, last in prog.dma_last.items():
                        wait(dsems[key], key, last.dval)
            return f

        block.tensor(body("pe"))
        block.scalar(body("act"))
        block.vector(body("dve"))
        block.gpsimd(body("pool"))
        block.sync(body("sp"))


class K:
    pass


def build_nc(stage=99, debug=False):
    nc = bass.Bass("TRN2", target_bir_lowering=False)
    st = ExitStack()
    with st:
        _build(nc, st, stage, debug)
    return nc


def _build(nc, st, stage, debug):
    P = Prog(nc)
    D = {}

    def din(name, shape, dt=F32):
        D[name] = Buf(nc.dram_tensor(name, list(shape), dt, kind="ExternalInput").ap())
        return D[name]

    def dout(name, shape, dt=F32):
        D[name] = Buf(nc.dram_tensor(name, list(shape), dt, kind="ExternalOutput").ap())
        return D[name]

    x_all = din("x_all", [NTOK, 1024])
    cache_k = din("cache_k", [2, 1024, 512])
    cache_v = din("cache_v", [2, 1024, 512])
    st_re = din("st_re", [2, 32, 64])
    st_im = din("st_im", [2, 32, 64])
    w_in = din("w_in", [1024, 2048])
    lam_p = din("lam_p", [1, 4, 32])
    subln_g = din("subln_g", [1, 64])
    a_re = din("a_re", [32, 64])
    a_im = din("a_im", [32, 64])
    log_dt = din("log_dt", [1, 32])
    b_re = din("b_re", [32, 64, 16])
    b_im = din("b_im", [32, 64, 16])
    c_re = din("c_re", [32, 16, 64])
    c_im = din("c_im", [32, 16, 64])
    ssm_d = din("ssm_d", [32, 16])
    w_glu = din("w_glu", [512, 1024])
    b_glu = din("b_glu", [1, 1024])
    w_out = din("w_out", [1024, 1024])
    ln1_g = din("ln1_g", [1, 1024])
    ln1_b = din("ln1_b", [1, 1024])
    w_up = din("w_up", [1024, 4096])
    w_down = din("w_down", [4096, 1024])
    ln2_g = din("ln2_g", [1, 1024])
    ln2_b = din("ln2_b", [1, 1024])

    y_out = dout("y_out", [NTOK, 1024])
    k_out = dout("k_out", [NTOK, 512])
    v_out = dout("v_out", [NTOK, 512])
    h_re_out = dout("h_re_out", [3, 32, 64])
    h_im_out = dout("h_im_out", [3, 32, 64])
    x1_scr = Buf(nc.dram_tensor("x1_scr", [NTOK, 1024], F32, kind="Internal").ap(), NT)

    ARENA_BYTES = 206 * 1024
    arena = st.enter_context(nc.sbuf_tensor("arena", [128, ARENA_BYTES], U8))
    psum = st.enter_context(nc.psum_tensor("psum", [128, 8, 512], F32))
    PS = Buf(psum, 8)

    cur = [0]

    def carve(shape, dt, nslots=1, at=None):
        esz = {F32: 4, BF16: 2, I32: 4, U8: 1}[dt]
        n = 1
        for d in shape[1:]:
            n *= d
        nb = n * esz
        nb_al = (nb + 31) // 32 * 32
        if at is None:
            off = cur[0]
            cur[0] += nb_al
        else:
            off = at
        assert off + nb <= ARENA_BYTES, ("arena overflow", off, nb)
        v = arena[:, off:off + nb]
        if dt != U8:
            v = v.bitcast(dt)
        if len(shape) > 2:
            names = " ".join("d%d" % i for i in range(1, len(shape)))
            kw = {"d%d" % i: shape[i] for i in range(2, len(shape))}
            v = v.rearrange("p (%s) -> p %s" % (names, names), **kw)
        return Buf(v, nslots)

    def psv(bank, shape, dt=F32, nb=1):
        if nb == 1:
            v = psum[:, bank, :]
        else:
            v = psum[:, bank:bank + nb, :].rearrange("p a b -> p (a b)")
        if dt != F32:
            v = v.bitcast(dt)
        n = 1
        for d in shape[1:]:
            n *= d
        v = v[:, 0:n]
        if len(shape) > 2:
            names = " ".join("d%d" % i for i in range(1, len(shape)))
            kw = {"d%d" % i: shape[i] for i in range(2, len(shape))}
            v = v.rearrange("p (%s) -> p %s" % (names, names), **kw)
        return v

    identf = carve([128, 128], F32)
    ident = carve([128, 128], BF16)
    P.op("pool", lambda e: e.memset(identf[:], 1.0), [], identf.s())
    P.op("pool", lambda e: e.affine_select(out=identf[:], in_=identf[:], pattern=[[-1, 128]], base=0,
                                           channel_multiplier=1, compare_op=ALU.is_equal, fill=0.0),
         identf.s(), identf.s())
    P.op("dve", lambda e: e.tensor_copy(out=ident[:], in_=identf[:]), identf.s(), ident.s())
    ones_bf = carve([128, 128], BF16)
    P.op("pool", lambda e: e.memset(ones_bf[:], 1.0), [], ones_bf.s())
    ones_f = carve([128, 128], F32)
    P.op("pool", lambda e: e.memset(ones_f[:], 1.0), [], ones_f.s())

    def sincos(ang, n, sin_out, cos_out, eng="dve"):
        tmp = carve([128, n], F32)
        ki = carve([128, n], I32)
        kf = carve([128, n], F32)
        r = carve([128, n], F32)
        C1 = 6.28125
        C2 = TWO_PI - C1
        for which, outb, shift in (("s", sin_out, 0.0), ("c", cos_out, math.pi / 2)):
            P.op(eng, lambda e, shift=shift: e.tensor_scalar(out=tmp[:], in0=ang[:], scalar1=shift, scalar2=1.0 / TWO_PI,
                                                             op0=ALU.add, op1=ALU.mult), ang.s(), tmp.s())
            P.op(eng, lambda e: e.tensor_copy(out=ki[:], in_=tmp[:]), tmp.s(), ki.s())
            P.op(eng, lambda e: e.tensor_copy(out=kf[:], in_=ki[:]), ki.s(), kf.s())
            P.op(eng, lambda e, shift=shift: e.tensor_scalar(out=tmp[:], in0=ang[:], scalar1=shift, scalar2=None,
                                                             op0=ALU.add), ang.s(), tmp.s())
            P.op(eng, lambda e: e.scalar_tensor_tensor(out=r[:], in0=kf[:], scalar=-C1, in1=tmp[:],
                                                       op0=ALU.mult, op1=ALU.add), kf.s() + tmp.s(), r.s())
            P.op(eng, lambda e: e.scalar_tensor_tensor(out=tmp[:], in0=kf[:], scalar=-C2, in1=r[:],
                                                       op0=ALU.mult, op1=ALU.add), kf.s() + r.s(), tmp.s())
            P.op(eng, lambda e: e.tensor_scalar(out=r[:], in0=tmp[:], scalar1=-math.pi, scalar2=math.pi,
                                                op0=ALU.max, op1=ALU.min), tmp.s(), r.s())
            P.op("act", lambda e, outb=outb: e.activation(out=outb[:], in_=r[:], func=AF.Sin), r.s(), outb.s())

    posi = carve([128, NT], I32)
    posf = carve([128, NT], F32)
    P.op("pool", lambda e: e.iota(posi[:], pattern=[[128, NT]], base=0, channel_multiplier=1), [], posi.s())
    P.op("pool", lambda e: e.iota(posi[0:64, 16:17], pattern=[[0, 1]], base=1024, channel_multiplier=1), posi.s(), posi.s())
    P.op("pool", lambda e: e.iota(posi[64:128, 16:17], pattern=[[0, 1]], base=1024, channel_multiplier=1), posi.s(), posi.s())
    P.op("dve", lambda e: e.tensor_copy(out=posf[:], in_=posi[:]), posi.s(), posf.s())
    rang = carve([128, NT, 4], F32)
    for i in range(4):
        P.op("dve", lambda e, i=i: e.tensor_scalar(out=rang[:, :, i], in0=posf[:], scalar1=float(ROPE_INV[i]), scalar2=None,
                                                   op0=ALU.mult), posf.s(), rang.s())
    rsin = carve([128, NT, 4], F32)
    rcos = carve([128, NT, 4], F32)
    rang2 = Buf(rang[:].rearrange("p a b -> p (a b)"))
    rang2.res = rang.res
    rsin2 = Buf(rsin[:].rearrange("p a b -> p (a b)")); rsin2.res = rsin.res
    rcos2 = Buf(rcos[:].rearrange("p a b -> p (a b)")); rcos2.res = rcos.res
    sincos(rang2, NT * 4, rsin2, rcos2)

    uD_start = cur[0]
    uD = carve([128, 4, 8, KC], BF16, NT)
    attnT = carve([128, 4, NTOK], BF16, NT)
    gT = carve([128, 4, NTOK], BF16, 1)
    keep_end = cur[0]
    qT = carve([128, 4, NTOK], BF16, NT)
    kT = carve([128, 4, NTOK], BF16, NT)
    Vaug = carve([128, NT, 8, 65], BF16, NT)
    kTc = carve([128, 4, 2048], BF16, 16)
    Vaugc = carve([128, 16, 8, 65], BF16, 16)
    pers_end = cur[0]

    P.op("pool", lambda e: e.memset(Vaug[:, :, :, 64:65], 1.0), [], Vaug.s())
    P.op("pool", lambda e: e.memset(Vaugc[:, :, :, 64:65], 1.0), [], Vaugc.s())

    w_in_bf = carve([128, 8, 2048], BF16, 32)
    xb = [carve([128, 1024], BF16) for _ in range(3)]
    xT = [carve([128, 8, 128], BF16) for _ in range(3)]
    qf = [carve([128, 512], F32) for _ in range(2)] * 2
    kf = [carve([128, 512], F32) for _ in range(2)] * 2
    vf = [carve([128, 512], F32) for _ in range(2)] * 2
    qb = [carve([128, 512], BF16) for _ in range(3)]
    kb = [carve([128, 512], BF16) for _ in range(3)]
    rt = [carve([128, 16, 4], F32) for _ in range(4)]

    def rope(buf, tile):
        v = buf[:].rearrange("p (h d) -> p h d", d=32)
        x1 = v[:, :, 0:4]
        x2 = v[:, :, 4:8]
        cs = rcos[:, tile, :].unsqueeze(1).to_broadcast([128, 16, 4])
        sn = rsin[:, tile, :].unsqueeze(1).to_broadcast([128, 16, 4])
        rd = buf.s() + rcos.s() + rsin.s()
        P.op("dve", lambda e: e.tensor_tensor(out=rt[0][:], in0=x1, in1=cs, op=ALU.mult), rd, rt[0].s())
        P.op("dve", lambda e: e.tensor_tensor(out=rt[1][:], in0=x2, in1=sn, op=ALU.mult), rd, rt[1].s())
        P.op("dve", lambda e: e.tensor_tensor(out=rt[2][:], in0=x2, in1=cs, op=ALU.mult), rd, rt[2].s())
        P.op("dve", lambda e: e.tensor_tensor(out=rt[3][:], in0=x1, in1=sn, op=ALU.mult), rd, rt[3].s())
        P.op("dve", lambda e: e.tensor_tensor(out=x1, in0=rt[0][:], in1=rt[1][:], op=ALU.subtract),
             rt[0].s() + rt[1].s() + rt[2].s() + rt[3].s(), buf.s())
        P.op("dve", lambda e: e.tensor_tensor(out=x2, in0=rt[2][:], in1=rt[3][:], op=ALU.add),
             rt[2].s() + rt[3].s(), buf.s())

    def transpose4(src, dstT, col0, bank, ncols=128):
        pv = psv(bank, [128, 4, 128], BF16)
        for blk in range(4):
            P.op("pe", lambda e, blk=blk: e.transpose(out=pv[:, blk, :], in_=src[:, blk * 128:(blk + 1) * 128], identity=ident[:]),
                 src.s() + ident.s(), PS.s(bank))
        return pv

    NB = 3

    def stageL(t):
        b = t % NB
        P.dma("pool", xb[b][:], x_all[t * 128:(t + 1) * 128, :], writes=xb[b].s())

    def stageX(t):
        b = t % NB
        bank = t % 2
        pxT = psv(bank, [128, 8, 128], BF16)
        for dc in range(8):
            P.op("pe", lambda e, dc=dc, b=b: e.transpose(out=pxT[:, dc, :], in_=xb[b][:, dc * 128:(dc + 1) * 128], identity=ident[:]),
                 xb[b].s() + ident.s(), PS.s(bank))
        P.op("act", lambda e, b=b: e.copy(out=xT[b][:], in_=pxT), PS.s(bank), xT[b].s())

    def stageM(t):
        b = t % NB
        b2 = t % 2
        for j, bank in ((0, 2), (1, 3), (2, 4)):
            pv_ = psv(bank, [128, 512])
            for dc in range(8):
                P.op("pe", lambda e, dc=dc, b=b, j=j, pv_=pv_: e.matmul(pv_, lhsT=xT[b][:, dc, :], rhs=w_in_bf[:, dc, j * 512:(j + 1) * 512],
                                                                       start=(dc == 0), stop=(dc == 7)),
                     xT[b].s() + w_in_bf.s(j * 8 + dc), PS.s(bank))
        pu = psv(5, [128, 4, 128])
        for gb in range(4):
            for dc in range(8):
                P.op("pe", lambda e, dc=dc, b=b, gb=gb: e.matmul(pu[:, gb, :], lhsT=w_in_bf[:, dc, 1536 + gb * 128:1536 + (gb + 1) * 128],
                                                                 rhs=xT[b][:, dc, :], start=(dc == 0), stop=(dc == 7)),
                     xT[b].s() + w_in_bf.s(24 + dc), PS.s(5))
        pq, pk, pvv = psv(2, [128, 512]), psv(3, [128, 512]), psv(4, [128, 512])
        P.op("act", lambda e, b2=b2: e.copy(out=qf[b2][:], in_=pq), PS.s(2), qf[b2].s())
        P.op("act", lambda e, b2=b2: e.copy(out=kf[b2][:], in_=pk), PS.s(3), kf[b2].s())
        P.op("dve", lambda e, b2=b2: e.tensor_copy(out=vf[b2][:], in_=pvv), PS.s(4), vf[b2].s())
        P.op("dve", lambda e, t=t: e.tensor_copy(out=uD[:, :, :, 16 * t:16 * (t + 1)],
                                                 in_=pu.rearrange("p g (k j) -> p g j k", j=8)), PS.s(5), uD.s(t))
        rope(qf[b2], t)
        rope(kf[b2], t)
        P.dma("sp", k_out[t * 128:(t + 1) * 128, :], kf[b2][:], reads=kf[b2].s(), writes=k_out.s())
        P.dma("sp", v_out[t * 128:(t + 1) * 128, :], vf[b2][:], reads=vf[b2].s(), writes=v_out.s())
        P.op("pool", lambda e, b=b, b2=b2: e.tensor_copy(out=qb[b][:], in_=qf[b2][:]), qf[b2].s(), qb[b].s())
        P.op("act", lambda e, b=b, b2=b2: e.copy(out=kb[b][:], in_=kf[b2][:]), kf[b2].s(), kb[b].s())
        P.op("pool", lambda e, b2=b2, t=t: e.tensor_copy(out=Vaug[:, t, :, 0:64], in_=vf[b2][:].rearrange("p (h d) -> p h d", d=64)),
             vf[b2].s(), Vaug.s(t))

    def stageT(t):
        b = t % NB
        pqT = transpose4(qb[b], qT, t * 128, 6)
        P.op("dve", lambda e, t=t, pqT=pqT: e.tensor_copy(out=qT[:, :, t * 128:(t + 1) * 128], in_=pqT), PS.s(6), qT.s(t))
        pkT = transpose4(kb[b], kT, t * 128, 7)
        P.op("act", lambda e, t=t, pkT=pkT: e.copy(out=kT[:, :, t * 128:(t + 1) * 128], in_=pkT), PS.s(7), kT.s(t))

    stageL(0)
    stageL(1)
    stageL(2)
    for j in range(4):
        for dc in range(0, 8, 2):
            P.dma("pool", w_in_bf[:, dc:dc + 2, j * 512:(j + 1) * 512],
                  w_in[dc * 128:(dc + 2) * 128, j * 512:(j + 1) * 512].rearrange("(a p) n -> p a n", p=128),
                  writes=w_in_bf.s(range(j * 8 + dc, j * 8 + dc + 2)))
    stageX(0)
    for t in range(NT):
        if t >= 1 and t + 2 < NT:
            stageL(t + 2)
        if t + 1 < NT:
            stageX(t + 1)
        stageM(t)
        if t >= 1:
            stageT(t - 1)
    stageT(NT - 1)

    if stage <= 1:
        P.emit(st)
        return

    rr = [0]

    def alt():
        rr[0] += 1
        return "dve" if rr[0] % 2 else "pool"

    def TT(eng, o, a, b, op, rd, wr):
        P.op(eng, lambda e: e.tensor_tensor(out=o, in0=a, in1=b, op=op), rd, wr)

    def TS(eng, o, a, s1, s2, op0, op1, rd, wr):
        P.op(eng, lambda e: e.tensor_scalar(out=o, in0=a, scalar1=s1, scalar2=s2, op0=op0, op1=op1), rd, wr)

    def STT(eng, o, a, sc, b, op0, op1, rd, wr):
        P.op(eng, lambda e: e.scalar_tensor_tensor(out=o, in0=a, scalar=sc, in1=b, op0=op0, op1=op1), rd, wr)

    MUL, ADD, SUB = ALU.mult, ALU.add, ALU.subtract
    P.barrier()
    cur[0] = pers_end
    lamb = carve([128, 128], F32)
    P.dma("sp", lamb[:], lam_p[0:1, :, :].rearrange("a b c -> a (b c)").to_broadcast([128, 128]), writes=lamb.s())
    lt = carve([128, 64], F32)
    ls = carve([128, 2], F32)
    le = carve([128, 2], F32)
    neglam = carve([128, 1], F32)
    P.op("dve", lambda e: e.tensor_tensor(out=lt[:].rearrange("p (a b) -> p a b", a=2), in0=lamb[:].rearrange("p (a k b) -> p a k b", a=2, k=2)[:, :, 0, :],
                                          in1=lamb[:].rearrange("p (a k b) -> p a k b", a=2, k=2)[:, :, 1, :], op=ALU.mult), lamb.s(), lt.s())
    P.op("dve", lambda e: e.tensor_reduce(out=ls[:], in_=lt[:].rearrange("p (a b) -> p a b", a=2), op=ALU.add, axis=AX.X), lt.s(), ls.s())
    P.op("act", lambda e: e.activation(out=le[:], in_=ls[:], func=AF.Exp), ls.s(), le.s())
    P.op("dve", lambda e: e.scalar_tensor_tensor(out=neglam[:], in0=le[:, 1:2], scalar=-LAM_INIT, in1=le[:, 0:1],
                                                 op0=ALU.add, op1=ALU.subtract), le.s(), neglam.s())
    gsub = carve([128, 64], F32)
    P.dma("sp", gsub[:], subln_g[0:1, :].to_broadcast([128, 64]), writes=gsub.s())
    P.op("dve", lambda e: e.tensor_scalar(out=gsub[:], in0=gsub[:], scalar1=1.0 - LAM_INIT, scalar2=None, op0=ALU.mult), gsub.s(), gsub.s())
    mhalf = carve([128, 8], F32)
    P.op("pool", lambda e: e.memset(mhalf[:], -0.5), [], mhalf.s())
    Vs1 = carve([64, 8, 65], BF16)
    P.dma("sp", Vs1[0:64], Vaug[64:128, 16, :, :], reads=Vaug.s(16), writes=Vs1.s())

    TOP_OFF = ARENA_BYTES - 21 * 1024
    assert cur[0] <= TOP_OFF, cur[0]
    b_cur = cur[0]
    cur[0] = TOP_OFF
    prm = carve([128, 3, 16], F32)
    h0 = carve([128, 4, 16], F32)
    Dcol = carve([128, 4], F32)
    sm = carve([128, 24, 16], F32)
    angb = carve([128, 16], F32)
    snb = carve([128, 16], F32)
    csb = carve([128, 16], F32)
    PWr = carve([128, 9, 16], F32)
    PWi = carve([128, 9, 16], F32)
    P.rec = []
    An = carve([128, 3, 128], F32)
    ldn = carve([128, 2], F32)
    P.dma("sp", An[0:16, 0, :], a_re[:, :].rearrange("(pair g2) p -> pair (g2 p)", g2=2), writes=An.s())
    P.dma("sp", An[0:16, 1, :], a_im[:, :].rearrange("(pair g2) p -> pair (g2 p)", g2=2), writes=An.s())
    P.dma("sp", ldn[0:16, :], log_dt[0:1, :].rearrange("a (pair g2) -> (a pair) g2", g2=2), writes=ldn.s())
    P.op("dve", lambda e: e.tensor_copy(out=An[0:16, 2, :].rearrange("p (g q) -> p g q", g=2),
                                        in_=ldn[0:16, :].unsqueeze(2).to_broadcast([16, 2, 64])), ldn.s() + An.s(), An.s())
    pA = psv(7, [128, 3, 16])
    for i in range(3):
        P.op("pe", lambda e, i=i: e.transpose(out=pA[:, i, :], in_=An[0:16, i, :], identity=identf[0:16, 0:16]), An.s() + identf.s(), PS.s(7))
    P.op("dve", lambda e: e.tensor_copy(out=prm[:], in_=pA), PS.s(7), prm.s())
    are, aim, ldt = prm[:, 0, :], prm[:, 1, :], prm[:, 2, :]
    Hn = carve([128, 4, 128], F32)
    for s in range(2):
        P.dma("sp", Hn[0:16, 2 * s, :], st_re[s].rearrange("(pair g2) p -> pair (g2 p)", g2=2), writes=Hn.s())
        P.dma("sp", Hn[0:16, 2 * s + 1, :], st_im[s].rearrange("(pair g2) p -> pair (g2 p)", g2=2), writes=Hn.s())
    pH0 = psv(7, [128, 4, 16])
    for i in range(4):
        P.op("pe", lambda e, i=i: e.transpose(out=pH0[:, i, :], in_=Hn[0:16, i, :], identity=identf[0:16, 0:16]), Hn.s() + identf.s(), PS.s(7))
    P.op("dve", lambda e: e.tensor_copy(out=h0[:], in_=pH0), PS.s(7), h0.s())
    Bre = carve([128, 16, 16], F32)
    Bim = carve([128, 16, 16], F32)
    for g2 in range(2):
        P.dma("sp", Bre[64 * g2:64 * (g2 + 1), :, :], b_re[:].rearrange("(pair g2) p c -> g2 p pair c", g2=2)[g2], writes=Bre.s())
        P.dma("sp", Bim[64 * g2:64 * (g2 + 1), :, :], b_im[:].rearrange("(pair g2) p c -> g2 p pair c", g2=2)[g2], writes=Bim.s())
    Cn = carve([128, 2, 4, 64], F32)
    P.dma("sp", Cn[:, 0, :, :], c_re[:].rearrange("(gb gl) c p -> (gl c) gb p", gb=4), writes=Cn.s())
    P.dma("sp", Cn[:, 1, :, :], c_im[:].rearrange("(gb gl) c p -> (gl c) gb p", gb=4), writes=Cn.s())
    CT = carve([128, 2, 512], F32)
    for ri in range(2):
        pC = psv(7, [128, 4, 128])
        for gb in range(4):
            P.op("pe", lambda e, gb=gb, ri=ri, pC=pC: e.transpose(out=pC[0:64, gb, :], in_=Cn[:, ri, gb, :], identity=identf[:]),
                 Cn.s() + identf.s(), PS.s(7))
        P.op("dve", lambda e, ri=ri, pC=pC: e.tensor_copy(out=CT[0:64, ri, :], in_=pC[0:64].rearrange("p a b -> p (a b)")), PS.s(7), CT.s())
    Cre = carve([128, 16, 16], F32)
    Cim = carve([128, 16, 16], F32)
    for g2 in range(2):
        P.dma("sp", Cre[64 * g2:64 * (g2 + 1), :, :], CT[0:64, 0, :].rearrange("p (pair g2 c) -> p g2 pair c", g2=2, c=16)[:, g2], reads=CT.s(), writes=Cre.s())
        P.dma("sp", Cim[64 * g2:64 * (g2 + 1), :, :], CT[0:64, 1, :].rearrange("p (pair g2 c) -> p g2 pair c", g2=2, c=16)[:, g2], reads=CT.s(), writes=Cim.s())
    Dn = carve([128, 128], F32)
    P.dma("sp", Dn[0:4, :], ssm_d[:].rearrange("(gb gl) c -> gb (gl c)", gb=4), writes=Dn.s())
    pD = psv(7, [128, 4])
    P.op("pe", lambda e: e.transpose(out=pD, in_=Dn[0:4, :], identity=identf[0:4, 0:4]), Dn.s() + identf.s(), PS.s(7))
    P.op("dve", lambda e: e.tensor_copy(out=Dcol[:], in_=pD), PS.s(7), Dcol.s())

    SMs = sm.s()
    dtt, adt, ang, mag, sn_, cs_, lr, li, den, nr, fre, fim, r8, rr8, er, ei, t1s, t2s = [sm[:, i, :] for i in range(18)]
    P.op("act", lambda e: e.activation(out=dtt, in_=ldt, func=AF.Exp), prm.s(), SMs)
    TT("dve", adt, are, dtt, MUL, prm.s() + SMs, SMs)
    TT("dve", ang, aim, dtt, MUL, prm.s() + SMs, SMs)
    P.op("act", lambda e: e.activation(out=mag, in_=adt, func=AF.Exp), SMs, SMs)
    P.op("act", lambda e: e.activation(out=r8, in_=adt, func=AF.Exp, scale=8.0), SMs, SMs)
    P.op("act", lambda e: e.activation(out=rr8, in_=adt, func=AF.Exp, scale=-8.0), SMs, SMs)
    P.op("dve", lambda e: e.tensor_copy(out=angb[:], in_=ang), SMs, angb.s())
    sincos(angb, 16, snb, csb)
    TT("dve", lr, mag, csb[:], MUL, SMs + csb.s(), SMs)
    TT("dve", li, mag, snb[:], MUL, SMs + snb.s(), SMs)
    TT("dve", t1s, are, are, MUL, prm.s(), SMs)
    TT("dve", t2s, aim, aim, MUL, prm.s(), SMs)
    TT("dve", den, t1s, t2s, ADD, SMs, SMs)
    P.op("dve", lambda e: e.reciprocal(out=den, in_=den), SMs, SMs)
    TS("dve", nr, lr, -1.0, None, ADD, ALU.bypass, SMs, SMs)
    TT("dve", t1s, nr, are, MUL, SMs + prm.s(), SMs)
    TT("dve", t2s, li, aim, MUL, SMs + prm.s(), SMs)
    TT("dve", t1s, t1s, t2s, ADD, SMs, SMs)
    TT("dve", fre, t1s, den, MUL, SMs, SMs)
    TT("dve", t1s, li, are, MUL, SMs + prm.s(), SMs)
    TT("dve", t2s, nr, aim, MUL, SMs + prm.s(), SMs)
    TT("dve", t1s, t1s, t2s, SUB, SMs, SMs)
    TT("dve", fim, t1s, den, MUL, SMs, SMs)
    PWs = PWr.s() + PWi.s()
    P.op("dve", lambda e: e.memset(PWr[:, 0, :], 1.0), [], PWs)
    P.op("dve", lambda e: e.memset(PWi[:, 0, :], 0.0), [], PWs)
    for m in range(1, 9):
        TT("dve", t1s, PWr[:, m - 1, :], lr, MUL, PWs + SMs, SMs)
        TT("dve", t2s, PWi[:, m - 1, :], li, MUL, PWs + SMs, SMs)
        TT("dve", PWr[:, m, :], t1s, t2s, SUB, SMs, PWs)
        TT("dve", t1s, PWr[:, m - 1, :], li, MUL, PWs + SMs, SMs)
        TT("dve", t2s, PWi[:, m - 1, :], lr, MUL, PWs + SMs, SMs)
        TT("dve", PWi[:, m, :], t1s, t2s, ADD, SMs, PWs)
    TT("dve", er, PWr[:, 8, :], rr8, MUL, PWs + SMs, SMs)
    TT("dve", ei, PWi[:, 8, :], rr8, MUL, PWs + SMs, SMs)

    early_ops = P.rec
    P.rec = None
    hoist = [o for o in early_ops if o[4] and not o[2]]
    early_ops = [o for o in early_ops if not (o[4] and not o[2])]
    for o in hoist:
        P._add(*o)
    top_end = cur[0]
    cur[0] = b_cur
    ckb = [carve([128, 512], BF16) for _ in range(4)]

    def cache_load(idx):
        s, c = idx // 8, idx % 8
        P.dma("pool", ckb[idx % 4][:], cache_k[s, c * 128:(c + 1) * 128, :], writes=ckb[idx % 4].s())
        P.dma("pool", Vaugc[:, idx, :, 0:64], cache_v[s, c * 128:(c + 1) * 128, :].rearrange("p (h d) -> p h d", d=64),
              writes=Vaugc.s(idx))

    def cache_tr(idx):
        pT = transpose4(ckb[idx % 4], kTc, idx * 128, 6)
        P.op("act", lambda e, idx=idx, pT=pT: e.copy(out=kTc[:, :, idx * 128:(idx + 1) * 128], in_=pT), PS.s(6), kTc.s(idx))

    PT = [carve([128, 8, 128], BF16, 2) for _ in range(3)]
    attn_tm = [carve([128, 512], BF16) for _ in range(5)]
    qBD = [carve([128, 4, 512], BF16) for _ in range(2)]
    qBDs = [carve([128, 4, 256], BF16) for _ in range(2)]
    for qb_ in qBD + qBDs:
        P.op("pool", lambda e, qb_=qb_: e.memset(qb_[:], 0.0), [], qb_.s())
    Osb = [carve([128, 16, 65], F32, 2) for _ in range(3)]
    rl = carve([128, 16], F32)
    rl2n = carve([128, 8], F32)
    o1 = carve([128, 8, 64], F32)
    o2 = carve([128, 8, 64], F32)
    osq = o2
    ms = carve([128, 8], F32)
    rstd = carve([128, 8], F32)
    rti = carve([128, 8], I32)
    rtn = carve([128, 8], F32)
    SCALE = 32.0 ** -0.5
    assert cur[0] <= TOP_OFF, (cur[0], TOP_OFF)
    Oall = psv(4, [128, 8, 128], nb=2)

    qtiles = []
    for i in [15, 0, 14, 1, 13, 2, 12, 3, 11, 4, 10, 5, 9, 6, 8, 7]:
        specs = []
        for j in range(i + 1):
            specs.append(dict(nk=128, mask=(j == i), deps=kT.s(j) + Vaug.s(j),
                              kT=(lambda blk, j=j: kT[:, blk, j * 128:(j + 1) * 128]),
                              V=(lambda h, j=j: Vaug[:, j, h, :])))
        qtiles.append(dict(qcol0=i * 128, nq=128, qdeps=qT.s(i), specs=specs, dst=attnT.s(i)))
    for s in range(2):
        specs = []
        for c in range(8):
            idx = 8 * s + c
            specs.append(dict(nk=128, mask=False, deps=kTc.s(idx) + Vaugc.s(idx),
                              kT=(lambda blk, idx=idx: kTc[:, blk, idx * 128:(idx + 1) * 128]),
                              V=(lambda h, idx=idx: Vaugc[:, idx, h, :])))
        if s == 0:
            Vf = (lambda h: Vaug[0:64, 16, h, :])
            vd = Vaug.s(16)
        else:
            Vf = (lambda h: Vs1[0:64, h, :])
            vd = Vs1.s()
        specs.append(dict(nk=64, mask=False, deps=kT.s(16) + vd,
                          kT=(lambda blk, s=s: kT[:, blk, 2048 + 64 * s:2048 + 64 * (s + 1)]), V=Vf))
        qtiles.append(dict(qcol0=2048 + 64 * s, nq=64, qdeps=qT.s(16), specs=specs, dst=attnT.s(16)))

    steps = []
    for qi, qt in enumerate(qtiles):
        for hs in range(2):
            n = len(qt["specs"])
            for ji, ks in enumerate(qt["specs"]):
                steps.append(dict(qi=qi, qt=qt, hs=hs, ji=ji, ks=ks, first=(ji == 0), last=(ji == n - 1)))

    built_q = set()

    def build_q(qi):
        if qi in built_q:
            return
        built_q.add(qi)
        qt = qtiles[qi]
        nq = qt["nq"]
        qb_ = qBD[qi % 2] if nq == 128 else qBDs[qi % 2]
        for r in range(4):
            eng = "pool" if r % 2 else "dve"
            P.op(eng, lambda e, r=r, qb_=qb_, qt=qt, nq=nq: e.tensor_copy(
                out=qb_[32 * r:32 * r + 32, :, r * nq:(r + 1) * nq], in_=qT[32 * r:32 * r + 32, :, qt["qcol0"]:qt["qcol0"] + nq]),
                qt["qdeps"] + qb_.s(), qb_.s())

    def emit_qk(si, sp):
        qt, hs, ks = sp["qt"], sp["hs"], sp["ks"]
        nq, nk, qi = qt["nq"], ks["nk"], sp["qi"]
        qb_ = qBD[qi % 2] if nq == 128 else qBDs[qi % 2]
        build_q(qi)
        if sp["first"] and sp["hs"] == 0 and qi + 1 < len(qtiles):
            build_q(qi + 1)
        sb0 = 2 * (si % 2)
        for bp in range(2):
            blk = 2 * hs + bp
            P.op("pe", lambda e, ks=ks, blk=blk, bp=bp, nk=nk, nq=nq, qb_=qb_, sb0=sb0: e.matmul(
                psum[0:nk, sb0 + bp, 0:4 * nq], lhsT=ks["kT"](blk), rhs=qb_[:, blk, 0:4 * nq], start=True, stop=True),
                qb_.s() + ks["deps"], PS.s(sb0 + bp))
        pt = PT[si % 3]
        sp["pt"] = pt
        for bp in range(2):
            Sv = psum[0:nk, sb0 + bp, 0:4 * nq].rearrange("p (r q) -> p r q", r=4)
            P.op("act", lambda e, pt=pt, nk=nk, nq=nq, Sv=Sv, bp=bp: e.activation(
                out=pt[0:nk, 4 * bp:4 * bp + 4, 0:nq], in_=Sv, func=AF.Exp, scale=SCALE),
                PS.s(sb0 + bp) + pt.s(bp), pt.s(bp))
            if ks["mask"]:
                P.op("pool", lambda e, pt=pt, bp=bp: e.memset(pt[64:128, 4 * bp:4 * bp + 4, 0:64], 0.0), pt.s(bp), pt.s(bp))

    def emit_av(si, sp):
        qt, hs, ks, pt = sp["qt"], sp["hs"], sp["ks"], sp["pt"]
        nq, nk = qt["nq"], ks["nk"]
        for e8 in range(8):
            hh = 8 * hs + e8
            h = hh // 2
            P.op("pe", lambda e, ks=ks, pt=pt, h=h, e8=e8, nk=nk, nq=nq, sp=sp: e.matmul(
                Oall[0:nq, e8, 0:65], lhsT=pt[0:nk, e8, 0:nq], rhs=ks["V"](h),
                start=(sp["first"] and e8 % 4 == 0), stop=sp["last"], skip_group_check=True), pt.s(e8 // 4) + ks["deps"], PS.s(range(4, 6)))
        if sp["last"]:
            finalize(sp)

    fcnt = [0]

    def finalize(sp):
        qt, hs, qi = sp["qt"], sp["hs"], sp["qi"]
        nq = qt["nq"]
        am = attn_tm[qi % 5]
        ob = Osb[qi % 3]
        P.op("act", lambda e, hs=hs: e.copy(out=ob[0:nq, 8 * hs:8 * hs + 8, :], in_=Oall[0:nq, :, 0:65]), PS.s(range(4, 6)) + ob.s(hs), ob.s(hs))
        if hs == 0:
            return
        Ov = ob[:].rearrange("p (h m) d -> p h m d", m=2)
        P.op("dve", lambda e: e.reciprocal(out=rl[0:nq, :], in_=ob[0:nq, :, 64]), ob.s(), rl.s())
        rlv = rl[:].rearrange("p (h m) -> p h m", m=2)
        P.op("dve", lambda e: e.tensor_tensor(out=rl2n[0:nq, :], in0=rlv[0:nq, :, 1], in1=neglam[0:nq, 0:1].to_broadcast([nq, 8]), op=ALU.mult),
             rl.s() + neglam.s(), rl2n.s())
        P.op("dve", lambda e: e.tensor_tensor(out=o1[0:nq], in0=Ov[0:nq, :, 0, 0:64], in1=rlv[0:nq, :, 0].unsqueeze(2).to_broadcast([nq, 8, 64]), op=ALU.mult),
             ob.s() + rl.s(), o1.s())
        P.op("dve", lambda e: e.tensor_tensor(out=o2[0:nq], in0=Ov[0:nq, :, 1, 0:64], in1=rl2n[0:nq, :].unsqueeze(2).to_broadcast([nq, 8, 64]), op=ALU.mult),
             ob.s() + rl2n.s(), o2.s())
        P.op("dve", lambda e: e.tensor_tensor(out=o1[0:nq], in0=o1[0:nq], in1=o2[0:nq], op=ALU.add), o1.s() + o2.s(), o1.s())
        P.op("dve", lambda e: e.tensor_tensor(out=osq[0:nq], in0=o1[0:nq], in1=o1[0:nq], op=ALU.mult), o1.s(), osq.s())
        P.op("dve", lambda e: e.tensor_reduce(out=ms[0:nq, :], in_=osq[0:nq], op=ALU.add, axis=AX.X), osq.s(), ms.s())
        P.op("dve", lambda e: e.tensor_scalar(out=ms[0:nq, :], in0=ms[0:nq, :], scalar1=1.0 / 64.0, scalar2=1e-5, op0=ALU.mult, op1=ALU.add), ms.s(), ms.s())
        P.op("dve", lambda e: e.tensor_single_scalar(out=rti[0:nq, :], in_=ms[0:nq, :].bitcast(I32), scalar=1, op=ALU.logical_shift_right), ms.s(), rti.s())
        P.op("dve", lambda e: e.tensor_scalar(out=rti[0:nq, :], in0=rti[0:nq, :], scalar1=-1.0, scalar2=1597463007.0, op0=ALU.mult, op1=ALU.add), rti.s(), rti.s())
        for it in range(2):
            ysrc = rti[0:nq, :].bitcast(F32) if it == 0 else rstd[0:nq, :]
            P.op("dve", lambda e, ysrc=ysrc: e.tensor_tensor(out=rtn[0:nq, :], in0=ysrc, in1=ysrc, op=ALU.mult), rti.s() + rstd.s(), rtn.s())
            P.op("dve", lambda e: e.tensor_tensor(out=rtn[0:nq, :], in0=rtn[0:nq, :], in1=ms[0:nq, :], op=ALU.mult), rtn.s() + ms.s(), rtn.s())
            P.op("dve", lambda e: e.tensor_scalar(out=rtn[0:nq, :], in0=rtn[0:nq, :], scalar1=-0.5, scalar2=1.5, op0=ALU.mult, op1=ALU.add), rtn.s(), rtn.s())
            P.op("dve", lambda e, ysrc=ysrc: e.tensor_tensor(out=rstd[0:nq, :], in0=ysrc, in1=rtn[0:nq, :], op=ALU.mult), rtn.s() + rti.s() + rstd.s(), rstd.s())
        P.op("dve", lambda e: e.tensor_tensor(out=o1[0:nq], in0=o1[0:nq], in1=rstd[0:nq, :].unsqueeze(2).to_broadcast([nq, 8, 64]), op=ALU.mult),
             o1.s() + rstd.s(), o1.s())
        P.op("dve", lambda e: e.tensor_tensor(out=am[0:nq, :].rearrange("p (h d) -> p h d", d=64), in0=o1[0:nq],
                                              in1=gsub[0:nq, :].unsqueeze(1).to_broadcast([nq, 8, 64]), op=ALU.mult),
             o1.s() + gsub.s() + am.s(), am.s())
        if hs == 1:
            pendT.append((qt, am, nq, qi + 3))

    pendT = []
    cur_done = [-1]

    def emit_T(force=False):
        keep = []
        for item in pendT:
            qt, am, nq, due = item
            if not force and not (cur_done[0] >= due):
                keep.append(item)
                continue
            pT = psv(6, [128, 4, 128], BF16)
            for blk in range(4):
                P.op("pe", lambda e, blk=blk, am=am, nq=nq, pT=pT: e.transpose(out=pT[:, blk, 0:nq], in_=am[0:nq, blk * 128:(blk + 1) * 128], identity=ident[0:nq, 0:nq]),
                     am.s() + ident.s(), PS.s(6))
            P.op("act", lambda e, qt=qt, nq=nq, pT=pT: e.copy(out=attnT[:, :, qt["qcol0"]:qt["qcol0"] + nq], in_=pT[:, :, 0:nq]), PS.s(6), qt["dst"])
        pendT[:] = keep

    for idx in range(3):
        cache_load(idx)
    cdone = [0]
    for si, sp in enumerate(steps):
        emit_qk(si, sp)
        if si >= 1:
            emit_av(si - 1, steps[si - 1])
            pv_ = steps[si - 1]
            if pv_["last"] and pv_["hs"] == 0:
                cur_done[0] = pv_["qi"]
        emit_T()
        if si >= 2:
            P.flush(early_ops, 1)
        if sp["first"] and sp["hs"] == 0 and cdone[0] < 16 and sp["qi"] >= 1:
            cache_tr(cdone[0])
            if cdone[0] + 3 < 16:
                cache_load(cdone[0] + 3)
            cdone[0] += 1
    emit_av(len(steps) - 1, steps[-1])
    emit_T(force=True)
    P.flush(early_ops, len(early_ops))
    assert cdone[0] == 16

    if debug:
        dbg_attn = dout("dbg_attnT", [128, 4, NTOK], BF16)
        P.dma("sp", dbg_attn[:], attnT[:], reads=attnT.s(), writes=dbg_attn.s())
    if stage <= 2:
        P.emit(st)
        return

    P.barrier()
    cur[0] = keep_end
    BB = carve([128, 16, 2, 32], BF16)
    Cint = carve([128, 16, 2, 9, 32], BF16)
    Tz = carve([128, 4, 8, 128], BF16)
    Wb_off = cur[0]
    Wb = carve([128, 4, 2, 8, 128], BF16)
    lvl2_start = cur[0]

    def bc16(v):
        return v.unsqueeze(2).to_broadcast([128, 16, 16])

    Bbr = carve([128, 16, 16], F32)
    Bbi = carve([128, 16, 16], F32)
    T1 = carve([128, 16, 16], F32)
    T2 = carve([128, 16, 16], F32)
    TT("dve", T1[:], Bre[:], bc16(fre), MUL, Bre.s() + SMs, T1.s())
    TT("pool", T2[:], Bim[:], bc16(fim), MUL, Bim.s() + SMs, T2.s())
    TT("dve", Bbr[:], T1[:], T2[:], SUB, T1.s() + T2.s(), Bbr.s())
    TT("dve", T1[:], Bim[:], bc16(fre), MUL, Bim.s() + SMs, T1.s())
    TT("pool", T2[:], Bre[:], bc16(fim), MUL, Bre.s() + SMs, T2.s())
    TT("dve", Bbi[:], T1[:], T2[:], ADD, T1.s() + T2.s(), Bbi.s())
    P.op("pool", lambda e: e.memset(BB[:], 0.0), [], BB.s())
    for g2 in range(2):
        sl = slice(64 * g2, 64 * (g2 + 1))
        P.op("dve", lambda e, sl=sl, g2=g2: e.tensor_copy(out=BB[sl, :, 0, 16 * g2:16 * (g2 + 1)], in_=Bbr[sl]), Bbr.s() + BB.s(), BB.s())
        P.op("dve", lambda e, sl=sl, g2=g2: e.tensor_copy(out=BB[sl, :, 1, 16 * g2:16 * (g2 + 1)], in_=Bbi[sl]), Bbi.s() + BB.s(), BB.s())
    P.op("pool", lambda e: e.memset(Cint[:], 0.0), [], Cint.s())
    CW1 = carve([128, 16, 9, 16], F32)
    CW2 = carve([128, 16, 9, 16], F32)

    def bcC(v):
        return v.unsqueeze(2).to_broadcast([128, 16, 9, 16])

    def bcP(v):
        return v.rearrange("p m q -> p q m").unsqueeze(3).to_broadcast([128, 16, 9, 16])

    TT("dve", CW1[:], bcC(Cre[:]), bcP(PWr[:]), MUL, Cre.s() + PWs, CW1.s())
    TT("dve", CW2[:], bcC(Cim[:]), bcP(PWi[:]), MUL, Cim.s() + PWs, CW2.s())
    TT("dve", CW1[:], CW1[:], CW2[:], SUB, CW1.s() + CW2.s(), CW1.s())
    for g2 in range(2):
        sl = slice(64 * g2, 64 * (g2 + 1))
        P.op("act", lambda e, sl=sl, g2=g2: e.copy(out=Cint[sl, :, 0, :, 16 * g2:16 * (g2 + 1)], in_=CW1[sl]), CW1.s() + Cint.s(), Cint.s())
    CW3 = carve([128, 16, 9, 16], F32)
    TT("dve", CW3[:], bcC(Cre[:]), bcP(PWi[:]), MUL, Cre.s() + PWs, CW3.s())
    TT("dve", CW2[:], bcC(Cim[:]), bcP(PWr[:]), MUL, Cim.s() + PWs + CW2.s(), CW2.s())
    STT("dve", CW3[:], CW3[:], -1.0, CW2[:], MUL, SUB, CW3.s() + CW2.s(), CW3.s())
    for g2 in range(2):
        sl = slice(64 * g2, 64 * (g2 + 1))
        P.op("act", lambda e, sl=sl, g2=g2: e.copy(out=Cint[sl, :, 1, :, 16 * g2:16 * (g2 + 1)], in_=CW3[sl]), CW3.s() + Cint.s(), Cint.s())
    Tzf = carve([128, 4, 128], F32)
    P.op("pool", lambda e: e.memset(Tz[:], 0.0), [], Tz.s())
    P.op("pool", lambda e: e.memset(Tzf[:], 0.0), [], Tzf.s())
    pz = psv(5, [128, 32, 32], nb=2)
    for pair in range(16):
        gb, m = pair // 4, pair % 4
        for tau in range(8):
            for ri in range(2):
                P.op("pe", lambda e, pair=pair, gb=gb, m=m, tau=tau, ri=ri: e.matmul(
                    pz[32 * m:32 * (m + 1), gb * 8 + tau, :], lhsT=BB[:, pair, ri, :], rhs=Cint[:, pair, ri, tau, :],
                    start=(ri == 0), stop=(ri == 1), tile_position=(0, 32 * m), skip_group_check=True), BB.s() + Cint.s(), PS.s(range(5, 7)))
    pzv = pz.rearrange("p (gb t) c -> p gb t c", t=8)
    for m in range(4):
        sl = slice(32 * m, 32 * (m + 1))
        P.op("dve", lambda e, sl=sl: e.tensor_copy(out=Tz[sl, :, 1:8, sl], in_=pzv[sl, :, 1:8, :]), PS.s(range(5, 7)) + Tz.s(), Tz.s())
        P.op("dve", lambda e, sl=sl: e.tensor_copy(out=Tzf[sl, :, sl], in_=pzv[sl, :, 0, :]), PS.s(range(5, 7)) + Tzf.s(), Tzf.s())
    for gb in range(4):
        STT("dve", Tz[:, gb, 0, :], identf[:], Dcol[:, gb:gb + 1], Tzf[:, gb, :], MUL, ADD, identf.s() + Dcol.s() + Tzf.s() + Tz.s(), Tz.s())
    Nat = carve([128, 4, 2, 8, 128], F32)
    P.op("pool", lambda e: e.memset(Nat[:], 0.0), [], Nat.s())
    Natv = Nat[:].rearrange("p gb r j (m g c) -> p gb r j m g c", m=4, g=2)
    NW1 = carve([128, 16, 8, 16], F32)
    NW2 = carve([128, 16, 8, 16], F32)

    def bcB(v):
        return v.unsqueeze(2).to_broadcast([128, 16, 8, 16])

    def bcP8(v):
        return v[:, 0:8, :].rearrange("p m q -> p q m").unsqueeze(3).to_broadcast([128, 16, 8, 16])

    for ri in range(2):
        if ri == 0:
            TT("dve", NW1[:], bcB(Bbr[:]), bcP8(PWr), MUL, Bbr.s() + PWs + NW1.s(), NW1.s())
            TT("dve", NW2[:], bcB(Bbi[:]), bcP8(PWi), MUL, Bbi.s() + PWs + NW2.s(), NW2.s())
            TT("dve", NW1[:], NW1[:], NW2[:], SUB, NW1.s() + NW2.s(), NW1.s())
        else:
            TT("dve", NW1[:], bcB(Bbr[:]), bcP8(PWi), MUL, Bbr.s() + PWs + NW1.s(), NW1.s())
            TT("dve", NW2[:], bcB(Bbi[:]), bcP8(PWr), MUL, Bbi.s() + PWs + NW2.s(), NW2.s())
            TT("dve", NW1[:], NW1[:], NW2[:], ADD, NW1.s() + NW2.s(), NW1.s())
        for g2 in range(2):
            sl = slice(64 * g2, 64 * (g2 + 1))
            for gb in range(4):
                P.op("act", lambda e, sl=sl, g2=g2, gb=gb, ri=ri: e.copy(out=Natv[sl, gb, ri, :, :, g2, :],
                                                                        in_=NW1[sl, 4 * gb:4 * gb + 4, :, :].rearrange("p m q c -> p q m c")),
                     NW1.s() + Nat.s(), Nat.s())
    tcnt = 0
    for gb in range(4):
        for ri in range(2):
            for jh in range(2):
                bank = tcnt % 2
                tcnt += 1
                pW = psv(bank, [128, 4, 128])
                for jj in range(4):
                    j = jh * 4 + jj
                    P.op("pe", lambda e, gb=gb, ri=ri, j=j, jj=jj, pW=pW: e.transpose(out=pW[:, jj, :], in_=Nat[:, gb, ri, 7 - j, :], identity=identf[:]),
                         Nat.s() + identf.s(), PS.s(bank))
                P.op("act" if tcnt % 2 else "dve",
                     (lambda e, gb=gb, ri=ri, jh=jh, pW=pW: e.copy(out=Wb[:, gb, ri, 4 * jh:4 * jh + 4, :], in_=pW)) if tcnt % 2 else
                     (lambda e, gb=gb, ri=ri, jh=jh, pW=pW: e.tensor_copy(out=Wb[:, gb, ri, 4 * jh:4 * jh + 4, :], in_=pW)),
                     PS.s(bank), Wb.s())

    assert cur[0] <= TOP_OFF, (cur[0], TOP_OFF)
    P.barrier()
    cur[0] = lvl2_start
    Ec = carve([128, 8, KC], F32)
    Es = carve([128, 8, KC], F32)
    Sr = carve([128, 8, KC], F32)
    Si = carve([128, 8, KC], F32)
    Xr = carve([128, 8, KC], F32)
    Xi = carve([128, 8, KC], F32)
    U1 = carve([128, 8, KC], F32)
    U2 = carve([128, 8, KC], F32)
    Hbr = carve([128, 16, KC], BF16)
    Hbi = carve([128, 16, KC], BF16)
    Hfin = carve([128, 2, 3, 16], F32)
    assert cur[0] <= TOP_OFF, (cur[0], TOP_OFF)
    segs = [(0, 256, None), (256, 8, 0), (264, 8, 1)]
    for hf in range(2):
        ps8 = slice(8 * hf, 8 * hf + 8)
        Es_ = Ec.s() + Es.s()
        P.op("dve", lambda e, ps8=ps8: e.tensor_copy(out=Ec[:, :, 0], in_=er[:, ps8]), SMs + Es_, Es_)
        P.op("dve", lambda e, ps8=ps8: e.tensor_copy(out=Es[:, :, 0], in_=ei[:, ps8]), SMs + Es_, Es_)
        n = 1
        while n < 256:
            bc_c = Ec[:, :, n - 1:n].to_broadcast([128, 8, n])
            bc_s = Es[:, :, n - 1:n].to_broadcast([128, 8, n])
            a_c, a_s = Ec[:, :, 0:n], Es[:, :, 0:n]
            u1, u2 = U1[:, :, 0:n], U2[:, :, 0:n]
            TT("dve", u1, a_c, bc_c, MUL, Es_, U1.s())
            TT("dve", u2, a_s, bc_s, MUL, Es_, U2.s())
            TT("dve", Ec[:, :, n:2 * n], u1, u2, SUB, U1.s() + U2.s() + Es_, Es_)
            TT("dve", u1, a_c, bc_s, MUL, Es_, U1.s())
            TT("dve", u2, a_s, bc_c, MUL, Es_, U2.s())
            TT("dve", Es[:, :, n:2 * n], u1, u2, ADD, U1.s() + U2.s() + Es_, Es_)
            n *= 2
        for c0 in (256, 264):
            P.op("dve", lambda e, c0=c0: e.tensor_copy(out=Ec[:, :, c0:c0 + 8], in_=Ec[:, :, 0:8]), Es_, Es_)
            P.op("dve", lambda e, c0=c0: e.tensor_copy(out=Es[:, :, c0:c0 + 8], in_=Es[:, :, 0:8]), Es_, Es_)
        for gbl in range(2):
            gb = 2 * hf + gbl
            for ri in range(2):
                b0 = 4 * ((gbl * 2 + ri) % 2)
                for j in range(8):
                    for m in range(4):
                        P.op("pe", lambda e, gb=gb, ri=ri, j=j, m=m, b0=b0: e.matmul(
                            psum[:, b0 + m, 0:KC], lhsT=Wb[32 * m:32 * (m + 1), gb, ri, j, :], rhs=uD[32 * m:32 * (m + 1), gb, j, :],
                            start=(j == 0), stop=(j == 7), tile_position=(32 * m, 0)), Wb.s() + uD.s(), PS.s(range(b0, b0 + 4)))
                dst = Sr if ri == 0 else Si
                P.op("act", lambda e, dst=dst, gbl=gbl, b0=b0: e.copy(out=dst[:, 4 * gbl:4 * gbl + 4, :], in_=psum[:, b0:b0 + 4, 0:KC]),
                     PS.s(range(b0, b0 + 4)), dst.s())
        TT("dve", U1[:], Ec[:], Sr[:], MUL, Es_ + Sr.s(), U1.s())
        TT("dve", U2[:], Es[:], Si[:], MUL, Es_ + Si.s(), U2.s())
        TT("dve", Xr[:], U1[:], U2[:], ADD, U1.s() + U2.s(), Xr.s())
        TT("dve", U1[:], Ec[:], Si[:], MUL, Es_ + Si.s(), U1.s())
        TT("dve", U2[:], Es[:], Sr[:], MUL, Es_ + Sr.s(), U2.s())
        TT("dve", Xi[:], U1[:], U2[:], SUB, U1.s() + U2.s(), Xi.s())
        for pl in range(8):
            pair = 8 * hf + pl
            for (c0, n, sidx) in segs:
                for part, (src, dst) in enumerate(((Xr, Sr), (Xi, Si))):
                    if sidx is None:
                        init = 0.0
                        rdi = []
                    else:
                        init = h0[:, 2 * sidx + part, pair:pair + 1]
                        rdi = h0.s()
                    P.op("dve", lambda e, src=src, dst=dst, pl=pl, pair=pair, c0=c0, n=n, init=init: e.tensor_tensor_scan(
                        out=dst[:, pl, c0:c0 + n], data0=r8[:, pair:pair + 1].to_broadcast([128, n]), data1=src[:, pl, c0:c0 + n],
                        initial=init, op0=MUL, op1=ADD), src.s() + SMs + rdi + dst.s(), dst.s())
        TT("dve", U1[:], Ec[:], Sr[:], MUL, Es_ + Sr.s(), U1.s())
        TT("dve", U2[:], Es[:], Si[:], MUL, Es_ + Si.s(), U2.s())
        TT("dve", Xr[:], U1[:], U2[:], SUB, U1.s() + U2.s() + Xr.s(), Xr.s())
        TT("dve", U1[:], Ec[:], Si[:], MUL, Es_ + Si.s(), U1.s())
        TT("dve", U2[:], Es[:], Sr[:], MUL, Es_ + Sr.s(), U2.s())
        TT("dve", Xi[:], U1[:], U2[:], ADD, U1.s() + U2.s() + Xi.s(), Xi.s())
        for part, (X_, Hb) in enumerate(((Xr, Hbr), (Xi, Hbi))):
            for si, (c0, n, sidx) in enumerate(segs):
                P.op("pool", lambda e, X_=X_, part=part, si=si, c0=c0, n=n, ps8=ps8: e.tensor_copy(out=Hfin[:, part, si, ps8], in_=X_[:, :, c0 + n - 1]),
                     X_.s() + Hfin.s(), Hfin.s())
                P.op("act", lambda e, X_=X_, Hb=Hb, c0=c0, n=n, ps8=ps8: e.copy(out=Hb[:, ps8, c0 + 1:c0 + n], in_=X_[:, :, c0:c0 + n - 1]),
                     X_.s() + Hb.s(), Hb.s())
                if sidx is None:
                    P.op("pool", lambda e, Hb=Hb, c0=c0, ps8=ps8: e.memset(Hb[:, ps8, c0:c0 + 1], 0.0), Hb.s(), Hb.s())
                else:
                    P.op("pool", lambda e, Hb=Hb, c0=c0, ps8=ps8, sidx=sidx, part=part: e.tensor_copy(out=Hb[:, ps8, c0], in_=h0[:, 2 * sidx + part, ps8]),
                         h0.s() + Hb.s(), Hb.s())
    P.barrier()
    wglu_bf = carve([128, 4, 1024], BF16, at=Wb_off)
    wout_bf = carve([128, 8, 1024], BF16, at=lvl2_start)
    for gc in range(4):
        P.dma("pool", wglu_bf[:, gc, :], w_glu[gc * 128:(gc + 1) * 128, :], writes=wglu_bf.s())
    for kc in range(8):
        P.dma("pool", wout_bf[:, kc, :], w_out[kc * 128:(kc + 1) * 128, :], writes=wout_bf.s())
    pHf = psv(6, [128, 6, 128], nb=2)
    Hn2 = carve([128, 6, 128], F32)
    for part in range(2):
        for si in range(3):
            P.op("pe", lambda e, part=part, si=si: e.transpose(out=pHf[0:16, part * 3 + si, :], in_=Hfin[:, part, si, :], identity=identf[:]),
                 Hfin.s() + identf.s(), PS.s(7))
    P.op("dve", lambda e: e.tensor_copy(out=Hn2[0:16], in_=pHf[0:16]), PS.s(range(6, 8)), Hn2.s())
    for si in range(3):
        P.dma("sp", h_re_out[si].rearrange("(pair g2) p -> pair (g2 p)", g2=2), Hn2[0:16, si, :], reads=Hn2.s(), writes=h_re_out.s())
        P.dma("sp", h_im_out[si].rearrange("(pair g2) p -> pair (g2 p)", g2=2), Hn2[0:16, 3 + si, :], reads=Hn2.s(), writes=h_im_out.s())

    G = [carve([128, KC], F32) for _ in range(4)]
    gTv = gT[:].rearrange("p g (k b) -> p g k b", b=8)
    ycnt = 0
    for gb in range(4):
        for b in range(8):
            bank = ycnt % 4
            ycnt += 1
            py = psum[:, bank, 0:KC]
            for j in range(b + 1):
                P.op("pe", lambda e, gb=gb, b=b, j=j, py=py: e.matmul(py, lhsT=Tz[:, gb, b - j, :], rhs=uD[:, gb, j, :], start=(j == 0), stop=False,
                                                                    skip_group_check=True), Tz.s() + uD.s(), PS.s(bank))
            for m in range(4):
                pair = 4 * gb + m
                for ri, Hb in enumerate((Hbr, Hbi)):
                    P.op("pe", lambda e, m=m, pair=pair, ri=ri, Hb=Hb, b=b, bank=bank: e.matmul(
                        psum[32 * m:32 * (m + 1), bank, 0:KC], lhsT=Cint[:, pair, ri, b + 1, :], rhs=Hb[:, pair, :], start=False, stop=(ri == 1),
                        tile_position=(0, 32 * m), skip_group_check=True), Cint.s() + Hb.s(), PS.s(bank))
            if b % 2 == 1:
                b0 = bank - 1
                P.op("act", lambda e, gb=gb, b=b, b0=b0: e.activation(out=gTv[:, gb, :, b - 1:b + 1],
                                                                     in_=psum[:, b0:b0 + 2, 0:KC].rearrange("p j k -> p k j"),
                                                                     func=AF.Gelu_apprx_tanh), PS.s(range(b0, b0 + 2)), gT.s())

    if debug:
        dbg_g = dout("dbg_gT", [128, 4, NTOK], BF16)
        P.dma("sp", dbg_g[:], gT[:], reads=gT.s(), writes=dbg_g.s())
    if stage <= 3:
        P.emit(st)
        return

    P.barrier()
    cur[0] = keep_end
    W_UP_OFF = ARENA_BYTES - 64 * 1024
    w_up_bf = carve([128, 8, 4096], BF16, 8, at=W_UP_OFF)
    bgn = carve([128, 128], F32)
    P.dma("sp", bgn[0:8, :], b_glu[0:1, :].rearrange("a (f p) -> (a f) p", p=128), writes=bgn.s())
    pbg = psv(0, [128, 8])
    P.op("pe", lambda e: e.transpose(out=pbg, in_=bgn[0:8, :], identity=identf[0:8, 0:8]), bgn.s() + identf.s(), PS.s(0))
    bglu = carve([128, 8], F32)
    nbglu = carve([128, 8], F32)
    P.op("dve", lambda e: e.tensor_copy(out=bglu[:], in_=pbg), PS.s(0), bglu.s())
    TS("dve", nbglu[:], bglu[:], -1.0, None, MUL, ALU.bypass, bglu.s(), nbglu.s())
    def make_ln(gd, bd):
        L = {}
        L["g"] = carve([128, 1024], F32)
        L["b"] = carve([128, 1024], F32)
        P.dma("sp", L["g"][:], gd[0:1, :].to_broadcast([128, 1024]), writes=L["g"].s())
        P.dma("sp", L["b"][:], bd[0:1, :].to_broadcast([128, 1024]), writes=L["b"].s())
        L["stats"] = carve([128, 2, 6], F32)
        L["mv"] = carve([128, 2], F32)
        L["rstd"] = carve([128, 1], F32)
        L["nmr"] = carve([128, 1], F32)
        L["ti"] = carve([128, 1], I32)
        L["tn"] = carve([128, 2], F32)
        return L
    LN1 = make_ln(ln1_g, ln1_b)
    xs2 = [carve([128, 1024], F32) for _ in range(2)]
    eg = [carve([128, 4, 128], F32) for _ in range(2)]
    soT = [carve([128, 4, 128], BF16) for _ in range(2)]
    assert cur[0] <= Wb_off, (cur[0], Wb_off)
    cur[0] = Wb_off + 8 * 1024
    y1 = [carve([128, 1024], F32, 2) for _ in range(2)]
    assert cur[0] <= lvl2_start, (cur[0], lvl2_start)
    NFA = 12
    cur[0] = lvl2_start + 16 * 1024
    wdnA_off = cur[0]
    wdnA = carve([128, NFA, 1024], BF16, NFA)
    assert cur[0] <= W_UP_OFF, cur[0]

    def layer_norm(src, dst, L):
        stats, mv, rstd1, nmr, ti, tn = L["stats"], L["mv"], L["rstd"], L["nmr"], L["ti"], L["tn"]
        for hh_ in range(2):
            P.op("dve", lambda e, hh_=hh_: e.bn_stats(out=stats[:, hh_, :], in_=src[:, 512 * hh_:512 * (hh_ + 1)]), src.s(hh_) + stats.s(), stats.s())
        P.op("dve", lambda e: e.bn_aggr(out=mv[:], in_=stats[:].rearrange("p a b -> p (a b)")), stats.s(), mv.s())
        TS("dve", tn[:, 0:1], mv[:, 1:2], 1e-5, None, ADD, ALU.bypass, mv.s(), tn.s())
        P.op("dve", lambda e: e.tensor_single_scalar(out=ti[:], in_=tn[:, 0:1].bitcast(I32), scalar=1, op=ALU.logical_shift_right), tn.s(), ti.s())
        TS("dve", ti[:], ti[:], -1.0, 1597463007.0, MUL, ADD, ti.s(), ti.s())
        for it in range(2):
            ysrc = ti[:].bitcast(F32) if it == 0 else rstd1[:]
            TT("dve", tn[:, 1:2], ysrc, ysrc, MUL, ti.s() + rstd1.s(), tn.s())
            TT("dve", tn[:, 1:2], tn[:, 1:2], tn[:, 0:1], MUL, tn.s(), tn.s())
            TS("dve", tn[:, 1:2], tn[:, 1:2], -0.5, 1.5, MUL, ADD, tn.s(), tn.s())
            TT("dve", rstd1[:], ysrc, tn[:, 1:2], MUL, tn.s() + ti.s() + rstd1.s(), rstd1.s())
        STT("dve", nmr[:], mv[:, 0:1], -1.0, rstd1[:], MUL, MUL, mv.s() + rstd1.s(), nmr.s())
        for hh_ in range(2):
            sl = slice(512 * hh_, 512 * (hh_ + 1))
            P.op("act", lambda e, sl=sl: e.activation(out=dst[:, sl], in_=src[:, sl], func=AF.Identity, scale=rstd1[:, 0:1], bias=nmr[:, 0:1]),
                 src.s(hh_) + rstd1.s() + nmr.s() + dst.s(hh_), dst.s(hh_))
            TT("dve", dst[:, sl], dst[:, sl], L["g"][:, sl], MUL, dst.s(hh_) + L["g"].s(), dst.s(hh_))
            TT("dve", dst[:, sl], dst[:, sl], L["b"][:, sl], ADD, dst.s(hh_) + L["b"].s(), dst.s(hh_))

    def c1_G(t):
        b = t % 2
        cols = slice(t * 128, (t + 1) * 128)
        P.dma("sp", xs2[b][:], x_all[t * 128:(t + 1) * 128, :], writes=xs2[b].s())
        zb = 2 * b
        pz_ = psv(zb, [128, 8, 128], nb=2)
        for fb in range(8):
            for gc in range(4):
                P.op("pe", lambda e, fb=fb, gc=gc, cols=cols, pz_=pz_: e.matmul(pz_[:, fb, :], lhsT=wglu_bf[:, gc, fb * 128:(fb + 1) * 128], rhs=gT[:, gc, cols],
                                                                               start=(gc == 0), stop=(gc == 3), skip_group_check=True), wglu_bf.s() + gT.s(), PS.s(range(zb, zb + 2)))
        for f4 in range(4):
            P.op("act", lambda e, f4=f4, b=b, pz_=pz_: e.activation(out=eg[b][:, f4, :], in_=pz_[:, 4 + f4, :], func=AF.Sigmoid, bias=bglu[:, 4 + f4:5 + f4]),
                 PS.s(range(zb, zb + 2)) + bglu.s() + eg[b].s(), eg[b].s())
        for f4 in range(4):
            STT("dve", soT[b][:, f4, :], pz_[:, f4, :], bglu[:, f4:f4 + 1], eg[b][:, f4, :], ADD, MUL, PS.s(range(zb, zb + 2)) + bglu.s() + eg[b].s() + soT[b].s(), soT[b].s())

    def c1_M(t):
        b = t % 2
        cols = slice(t * 128, (t + 1) * 128)
        mb = 4 + 2 * b
        pm_ = psv(mb, [128, 1024], nb=2)
        for hh_ in range(2):
            for kc in range(8):
                lh = attnT[:, kc, cols] if kc < 4 else soT[b][:, kc - 4, :]
                rd = attnT.s(t) if kc < 4 else soT[b].s()
                P.op("pe", lambda e, hh_=hh_, kc=kc, lh=lh, pm_=pm_: e.matmul(pm_[:, 512 * hh_:512 * (hh_ + 1)], lhsT=lh, rhs=wout_bf[:, kc, 512 * hh_:512 * (hh_ + 1)],
                                                                             start=(kc == 0), stop=(kc == 7)), rd + wout_bf.s(), PS.s(mb + hh_))
        for hh_ in range(2):
            STT("dve", y1[b][:, 512 * hh_:512 * (hh_ + 1)], xs2[b][:, 512 * hh_:512 * (hh_ + 1)], ALPHA, pm_[:, 512 * hh_:512 * (hh_ + 1)], MUL, ADD,
                xs2[b].s() + PS.s(mb + hh_) + y1[b].s(hh_), y1[b].s(hh_))
        layer_norm(y1[b], y1[b], LN1)
        P.dma("sp", x1_scr[t * 128:(t + 1) * 128, :], y1[b][:], reads=y1[b].s(), writes=x1_scr.s(t))

    c1_G(0)
    for t in range(NT):
        if t + 1 < NT:
            c1_G(t + 1)
        c1_M(t)
        if t == 1:
            for dc in range(8):
                for hh_ in range(2):
                    P.dma("pool", w_up_bf[:, dc, 2048 * hh_:2048 * (hh_ + 1)], w_up[dc * 128:(dc + 1) * 128, 2048 * hh_:2048 * (hh_ + 1)], writes=w_up_bf.s(dc))
            for fc in range(0, NFA, 2):
                P.dma("pool", wdnA[:, fc:fc + 2, :], w_down[fc * 128:(fc + 2) * 128, :].rearrange("(a p) n -> p a n", p=128), writes=wdnA.s(range(fc, fc + 2)))


    if debug:
        dbg_x1 = dout("dbg_x1", [NTOK, 1024], F32)
        P.dma("sp", dbg_x1[:], x1_scr[:], reads=x1_scr.s(), writes=dbg_x1.s())

    P.barrier(include_dma=("sp",))
    cur[0] = uD_start
    LN2 = make_ln(ln2_g, ln2_b)
    wdnB = carve([128, 32 - NFA, 1024], BF16, 32 - NFA)

    def wdn(fc):
        if fc < NFA:
            return wdnA[:, fc, :], wdnA.s(fc)
        return wdnB[:, fc - NFA, :], wdnB.s(fc - NFA)

    x1g = [carve([128, 2, 1024], F32) for _ in range(2)]
    x1b = [carve([128, 2, 1024], BF16) for _ in range(2)]
    x1T = [carve([128, 8, 256], BF16) for _ in range(2)]
    hT = carve([128, 32, 256], BF16)
    sq = [carve([128, 256], F32) for _ in range(2)]
    y2 = [carve([128, 1024], F32, 2) for _ in range(2)]
    assert cur[0] <= wdnA_off, (cur[0], wdnA_off)
    groups = [(2 * g, 2) for g in range(8)] + [(16, 1)]
    ucnt = [0]
    ycnt2 = [0]

    def c2_X(gi):
        t0, ntl = groups[gi]
        b = gi % 2
        for tl in range(ntl):
            P.dma("sp", x1g[b][:, tl, :], x1_scr[(t0 + tl) * 128:(t0 + tl + 1) * 128, :], reads=x1_scr.s(t0 + tl), writes=x1g[b].s())
            P.dma("pool", x1b[b][:, tl, :], x1_scr[(t0 + tl) * 128:(t0 + tl + 1) * 128, :], reads=x1_scr.s(t0 + tl), writes=x1b[b].s())
        for tl in range(ntl):
            pxT = psv(7, [128, 8, 128], BF16)
            for dc in range(8):
                P.op("pe", lambda e, dc=dc, tl=tl, pxT=pxT, b=b: e.transpose(out=pxT[:, dc, :], in_=x1b[b][:, tl, dc * 128:(dc + 1) * 128], identity=ident[:]),
                     x1b[b].s() + ident.s(), PS.s(7))
            P.op("act", lambda e, b=b, tl=tl, pxT=pxT: e.copy(out=x1T[b][:, :, tl * 128:(tl + 1) * 128], in_=pxT), PS.s(7) + x1T[b].s(), x1T[b].s())

    def c2_U(gi):
        t0, ntl = groups[gi]
        b = gi % 2
        ntok = 128 * ntl
        for fc in range(32):
            bank = ucnt[0] % 3
            ucnt[0] += 1
            ph = psum[:, bank, 0:ntok]
            for dc in range(8):
                P.op("pe", lambda e, fc=fc, dc=dc, b=b, ph=ph, ntok=ntok: e.matmul(ph, lhsT=w_up_bf[:, dc, fc * 128:(fc + 1) * 128], rhs=x1T[b][:, dc, 0:ntok],
                                                                                  start=(dc == 0), stop=(dc == 7)), w_up_bf.s(dc) + x1T[b].s(), PS.s(bank))
            sqb = sq[fc % 2]
            P.op("act", lambda e, sqb=sqb, ph=ph, ntok=ntok: e.activation(out=sqb[:, 0:ntok], in_=ph, func=AF.Square), PS.s(bank), sqb.s())
            STT("dve", hT[:, fc, 0:ntok], ph, 0.0, sqb[:, 0:ntok], ALU.is_gt, MUL, PS.s(bank) + sqb.s() + hT.s(), hT.s())
            if fc in (5, 17) and pend_ln:
                pend_ln.pop(0)()

    pend_ln = []

    def c2_D(gi):
        t0, ntl = groups[gi]
        b = gi % 2
        for tl in range(ntl):
            t = t0 + tl
            yb = y2[ycnt2[0] % 2]
            ycnt2[0] += 1
            for hh_ in range(2):
                bank = 3 + 2 * tl + hh_
                pd = psv(bank, [128, 512])
                for fc in range(32):
                    wv, wr = wdn(fc)
                    P.op("pe", lambda e, fc=fc, tl=tl, hh_=hh_, pd=pd, wv=wv: e.matmul(pd, lhsT=hT[:, fc, tl * 128:(tl + 1) * 128], rhs=wv[:, 512 * hh_:512 * (hh_ + 1)],
                                                                                       start=(fc == 0), stop=(fc == 31)), hT.s() + wr, PS.s(bank))
                STT("dve", yb[:, 512 * hh_:512 * (hh_ + 1)], x1g[b][:, tl, 512 * hh_:512 * (hh_ + 1)], ALPHA, pd, MUL, ADD,
                    x1g[b].s() + PS.s(bank) + yb.s(hh_), yb.s(hh_))
            def tail(yb=yb, t=t):
                layer_norm(yb, yb, LN2)
                P.dma("sp", y_out[t * 128:(t + 1) * 128, :], yb[:], reads=yb.s(), writes=y_out.s())
            pend_ln.append(tail)

    c2_X(0)
    for fc in range(NFA, 32, 2):
        P.dma("pool", wdnB[:, fc - NFA:fc - NFA + 2, :], w_down[fc * 128:(fc + 2) * 128, :].rearrange("(a p) n -> p a n", p=128),
              writes=wdnB.s(range(fc - NFA, fc - NFA + 2)))
    for gi in range(len(groups)):
        c2_U(gi)
        if gi + 1 < len(groups):
            c2_X(gi + 1)
        c2_D(gi)
    while pend_ln:
        pend_ln.pop(0)()

    P.emit(st)


def make_in_maps(inp):
    f = lambda a: np.ascontiguousarray(a, dtype=np.float32)
    maps = []
    lam_p = np.stack([inp["lambda_q1"][0], inp["lambda_k1"][0], inp["lambda_q2"][0], inp["lambda_k2"][0]])[None]
    for c in range(8):
        xa = np.concatenate([inp["x_prompt"][c], inp["x_sample"][2 * c], inp["x_sample"][2 * c + 1]], axis=0)
        m = {
            "x_all": f(xa),
            "cache_k": f(inp["cache_k"][0, 2 * c:2 * c + 2].reshape(2, 1024, 512)),
            "cache_v": f(inp["cache_v"][0, 2 * c:2 * c + 2].reshape(2, 1024, 512)),
            "st_re": f(inp["state_ssm_re"][0, 2 * c:2 * c + 2]),
            "st_im": f(inp["state_ssm_im"][0, 2 * c:2 * c + 2]),
            "w_in": f(inp["w_in"][0]),
            "lam_p": f(lam_p),
            "subln_g": f(inp["subln_g"]),
            "a_re": f(inp["ssm_a_re"][0]),
            "a_im": f(inp["ssm_a_im"][0]),
            "log_dt": f(inp["ssm_log_dt"]),
            "b_re": f(inp["ssm_b_re"][0]),
            "b_im": f(inp["ssm_b_im"][0]),
            "c_re": f(inp["ssm_c_re"][0]),
            "c_im": f(inp["ssm_c_im"][0]),
            "ssm_d": f(inp["ssm_d"][0]),
            "w_glu": f(inp["w_glu"][0]),
            "b_glu": f(inp["b_glu"]),
            "w_out": f(inp["w_out"][0]),
            "ln1_g": f(inp["ln1_g"]),
            "ln1_b": f(inp["ln1_b"]),
            "w_up": f(inp["w_up"][0]),
            "w_down": f(inp["w_down"][0]),
            "ln2_g": f(inp["ln2_g"]),
            "ln2_b": f(inp["ln2_b"]),
        }
        maps.append(m)
    return maps


_NC_CACHE = {}


def kernel(**inp):
    inp = {k: np.asarray(v) for k, v in inp.items()}
    if "nc" not in _NC_CACHE:
        _NC_CACHE["nc"] = build_nc()
    nc = _NC_CACHE["nc"]
    maps = make_in_maps(inp)
    res = run_bass_kernel_spmd(nc, maps, core_ids=list(range(8)))
    R = res.results
    y_prompt = np.stack([R[c]["y_out"][:2048] for c in range(8)])
    y_sample = np.stack([R[c]["y_out"][2048 + 64 * s:2048 + 64 * (s + 1)] for c in range(8) for s in range(2)])
    k_prompt = np.stack([R[c]["k_out"][:2048].reshape(2048, 8, 64) for c in range(8)])[None]
    v_prompt = np.stack([R[c]["v_out"][:2048].reshape(2048, 8, 64) for c in range(8)])[None]
    k_sample = np.stack([R[c]["k_out"][2048 + 64 * s:2048 + 64 * (s + 1)].reshape(64, 8, 64) for c in range(8) for s in range(2)])[None]
    v_sample = np.stack([R[c]["v_out"][2048 + 64 * s:2048 + 64 * (s + 1)].reshape(64, 8, 64) for c in range(8) for s in range(2)])[None]
    re_p = np.stack([R[c]["h_re_out"][0] for c in range(8)])[None]
    im_p = np.stack([R[c]["h_im_out"][0] for c in range(8)])[None]
    re_s = np.stack([R[c]["h_re_out"][1 + s] for c in range(8) for s in range(2)])[None]
    im_s = np.stack([R[c]["h_im_out"][1 + s] for c in range(8) for s in range(2)])[None]
    f = lambda a: np.ascontiguousarray(a, dtype=np.float32)
    return (f(y_prompt), f(y_sample), f(k_prompt), f(v_prompt), f(re_p), f(im_p),
            f(k_sample), f(v_sample), f(re_s), f(im_s))
```

```python
import math
from contextlib import ExitStack
import numpy as np
import concourse.bass as bass
import concourse.mybir as mybir
from concourse.bass_utils import run_bass_kernel_spmd

F32 = mybir.dt.float32
BF16 = mybir.dt.bfloat16
I32 = mybir.dt.int32
U8 = mybir.dt.uint8
ALU = mybir.AluOpType
AF = mybir.ActivationFunctionType
AX = mybir.AxisListType

ENG = ("pe", "act", "dve", "pool", "sp")

NT = 17
NTOK = NT * 128
KC = 272
LAM_INIT = 0.8 - 0.6 * math.exp(-0.3 * 0)
ALPHA = 2.0 ** 0.25
ROPE_INV = [500000.0 ** (-(2 * i) / 8.0) for i in range(4)]
TWO_PI = 2.0 * math.pi


class Res:
    __slots__ = ("w", "rd")

    def __init__(self):
        self.w = None
        self.rd = []


class Buf:
    def __init__(self, t, nslots=1):
        self.t = t
        self.res = [Res() for _ in range(nslots)]

    def __getitem__(self, k):
        return self.t[k]

    def s(self, slot=None):
        if slot is None:
            return list(self.res)
        if isinstance(slot, (list, tuple, range)):
            return [self.res[i] for i in slot]
        return [self.res[slot]]


class Op:
    __slots__ = ("eng", "fn", "deps", "signal", "semval", "dma", "dsem", "dval", "prev_same_sem")

    def __init__(self, eng, fn, dma):
        self.eng = eng
        self.fn = fn
        self.deps = []
        self.signal = False
        self.semval = None
        self.dma = dma
        self.dsem = None
        self.dval = None
        self.prev_same_sem = None


class Prog:
    def __init__(self, nc, n_dma_sems=(12, 4, 40)):
        self.nc = nc
        self.ops = {e: [] for e in ENG}
        self.n_dma = {"sp": n_dma_sems[0], "act": n_dma_sems[1], "pool": n_dma_sems[2]}
        self.dma_cnt = {"sp": 0, "act": 0, "pool": 0}
        self.dma_last = {}
        self.dma_uses = {}
        self.bar = {e: [] for e in ENG}

    def _add(self, eng, fn, reads, writes, dma=False):
        op = Op(eng, fn, dma)
        deps = list(self.bar[eng])
        self.bar[eng] = []
        for r in reads:
            if r.w is not None:
                deps.append(r.w)
        for w in writes:
            if w.w is not None:
                deps.append(w.w)
            deps.extend(w.rd)
        seen = set()
        for d in deps:
            if d is op or id(d) in seen:
                continue
            seen.add(id(d))
            if (not d.dma) and d.eng == "pe" and eng == "pe" and not dma:
                continue
            op.deps.append(d)
            d.signal = True
        for r in reads:
            r.rd.append(op)
        for w in writes:
            w.w = op
            w.rd = []
        self.ops[eng].append(op)
        if dma:
            k = self.dma_cnt[eng] % self.n_dma[eng]
            self.dma_cnt[eng] += 1
            key = (eng, k)
            op.prev_same_sem = self.dma_last.get(key)
            self.dma_uses[key] = self.dma_uses.get(key, 0) + 1
            op.dsem = key
            op.dval = 16 * self.dma_uses[key]
            self.dma_last[key] = op
        return op

    rec = None

    def op(self, eng, fn, reads=(), writes=()):
        if self.rec is not None:
            self.rec.append((eng, fn, list(reads), list(writes), False))
            return None
        return self._add(eng, fn, list(reads), list(writes), False)

    def dma(self, eng, out, in_, reads=(), writes=(), **kw):
        def fn(e):
            return e.dma_start(out=out, in_=in_, **kw)
        if self.rec is not None:
            self.rec.append((eng, fn, list(reads), list(writes), True))
            return None
        return self._add(eng, fn, list(reads), list(writes), True)

    def flush(self, lst, n):
        for _ in range(min(n, len(lst))):
            self._add(*lst.pop(0))

    def barrier(self, include_dma=True):
        lasts = []
        for e in ENG:
            if self.ops[e]:
                lasts.append(self.ops[e][-1])
        if include_dma:
            for key, op in self.dma_last.items():
                if include_dma is True or key[0] in include_dma:
                    lasts.append(op)
        for e in ENG:
            self.bar[e] = list(lasts)

    def emit(self, stack):
        nc = self.nc
        sems = {e: stack.enter_context(nc.semaphore("s_" + e)) for e in ENG}
        dsems = {}
        for e, n in self.n_dma.items():
            for k in range(n):
                dsems[(e, k)] = stack.enter_context(nc.semaphore("d_%s%d" % (e, k)))
        for e in ENG:
            c = 0
            for op in self.ops[e]:
                if op.dma:
                    continue
                if op.signal:
                    c += 1
                    op.semval = c
        block = stack.enter_context(nc.Block())
        prog = self

        def body(ename):
            def f(eng):
                waited = {}

                def wait(sem, key, val):
                    if waited.get(key, 0) >= val:
                        return
                    waited[key] = val
                    eng.wait_ge(sem, val)

                for op in prog.ops[ename]:
                    for d in op.deps:
                        if d.dma:
                            wait(dsems[d.dsem], d.dsem, d.dval)
                        else:
                            wait(sems[d.eng], d.eng, d.semval)
                    if op.dma:
                        p = op.prev_same_sem
                        if p is not None:
                            wait(dsems[p.dsem], p.dsem, p.dval)
                        inst = op.fn(eng)
                        inst.then_inc(dsems[op.dsem], 16)
                    else:
                        inst = op.fn(eng)
                        if op.signal:
                            inst.then_inc(sems[ename], 1)
                if ename == "sp":
                    for key, last in prog.dma_last.items():
                        wait(dsems[key], key, last.dval)
            return f

        block.tensor(body("pe"))
        block.scalar(body("act"))
        block.vector(body("dve"))
        block.gpsimd(body("pool"))
        block.sync(body("sp"))


class K:
    pass


def build_nc(stage=99, debug=False):
    nc = bass.Bass("TRN2", target_bir_lowering=False)
    st = ExitStack()
    with st:
        _build(nc, st, stage, debug)
    return nc


def _build(nc, st, stage, debug):
    P = Prog(nc)
    D = {}

    def din(name, shape, dt=F32):
        D[name] = Buf(nc.dram_tensor(name, list(shape), dt, kind="ExternalInput").ap())
        return D[name]

    def dout(name, shape, dt=F32):
        D[name] = Buf(nc.dram_tensor(name, list(shape), dt, kind="ExternalOutput").ap())
        return D[name]

    x_all = din("x_all", [NTOK, 1024])
    cache_k = din("cache_k", [2, 1024, 512])
    cache_v = din("cache_v", [2, 1024, 512])
    st_re = din("st_re", [2, 32, 64])
    st_im = din("st_im", [2, 32, 64])
    w_in = din("w_in", [1024, 2048])
    lam_p = din("lam_p", [1, 4, 32])
    subln_g = din("subln_g", [1, 64])
    a_re = din("a_re", [32, 64])
    a_im = din("a_im", [32, 64])
    log_dt = din("log_dt", [1, 32])
    b_re = din("b_re", [32, 64, 16])
    b_im = din("b_im", [32, 64, 16])
    c_re = din("c_re", [32, 16, 64])
    c_im = din("c_im", [32, 16, 64])
    ssm_d = din("ssm_d", [32, 16])
    w_glu = din("w_glu", [512, 1024])
    b_glu = din("b_glu", [1, 1024])
    w_out = din("w_out", [1024, 1024])
    ln1_g = din("ln1_g", [1, 1024])
    ln1_b = din("ln1_b", [1, 1024])
    w_up = din("w_up", [1024, 4096])
    w_down = din("w_down", [4096, 1024])
    ln2_g = din("ln2_g", [1, 1024])
    ln2_b = din("ln2_b", [1, 1024])

    y_out = dout("y_out", [NTOK, 1024])
    k_out = dout("k_out", [NTOK, 512])
    v_out = dout("v_out", [NTOK, 512])
    h_re_out = dout("h_re_out", [3, 32, 64])
    h_im_out = dout("h_im_out", [3, 32, 64])
    x1_scr = Buf(nc.dram_tensor("x1_scr", [NTOK, 1024], F32, kind="Internal").ap(), NT)

    ARENA_BYTES = 206 * 1024
    arena = st.enter_context(nc.sbuf_tensor("arena", [128, ARENA_BYTES], U8))
    psum = st.enter_context(nc.psum_tensor("psum", [128, 8, 512], F32))
    PS = Buf(psum, 8)

    cur = [0]

    def carve(shape, dt, nslots=1, at=None):
        esz = {F32: 4, BF16: 2, I32: 4, U8: 1}[dt]
        n = 1
        for d in shape[1:]:
            n *= d
        nb = n * esz
        nb_al = (nb + 31) // 32 * 32
        if at is None:
            off = cur[0]
            cur[0] += nb_al
        else:
            off = at
        assert off + nb <= ARENA_BYTES, ("arena overflow", off, nb)
        v = arena[:, off:off + nb]
        if dt != U8:
            v = v.bitcast(dt)
        if len(shape) > 2:
            names = " ".join("d%d" % i for i in range(1, len(shape)))
            kw = {"d%d" % i: shape[i] for i in range(2, len(shape))}
            v = v.rearrange("p (%s) -> p %s" % (names, names), **kw)
        return Buf(v, nslots)

    def psv(bank, shape, dt=F32, nb=1):
        if nb == 1:
            v = psum[:, bank, :]
        else:
            v = psum[:, bank:bank + nb, :].rearrange("p a b -> p (a b)")
        if dt != F32:
            v = v.bitcast(dt)
        n = 1
        for d in shape[1:]:
            n *= d
        v = v[:, 0:n]
        if len(shape) > 2:
            names = " ".join("d%d" % i for i in range(1, len(shape)))
            kw = {"d%d" % i: shape[i] for i in range(2, len(shape))}
            v = v.rearrange("p (%s) -> p %s" % (names, names), **kw)
        return v

    identf = carve([128, 128], F32)
    ident = carve([128, 128], BF16)
    P.op("pool", lambda e: e.memset(identf[:], 1.0), [], identf.s())
    P.op("pool", lambda e: e.affine_select(out=identf[:], in_=identf[:], pattern=[[-1, 128]], base=0,
                                           channel_multiplier=1, compare_op=ALU.is_equal, fill=0.0),
         identf.s(), identf.s())
    P.op("dve", lambda e: e.tensor_copy(out=ident[:], in_=identf[:]), identf.s(), ident.s())
    ones_bf = carve([128, 128], BF16)
    P.op("pool", lambda e: e.memset(ones_bf[:], 1.0), [], ones_bf.s())
    ones_f = carve([128, 128], F32)
    P.op("pool", lambda e: e.memset(ones_f[:], 1.0), [], ones_f.s())

    def sincos(ang, n, sin_out, cos_out, eng="dve"):
        tmp = carve([128, n], F32)
        ki = carve([128, n], I32)
        kf = carve([128, n], F32)
        r = carve([128, n], F32)
        C1 = 6.28125
        C2 = TWO_PI - C1
        for which, outb, shift in (("s", sin_out, 0.0), ("c", cos_out, math.pi / 2)):
            P.op(eng, lambda e, shift=shift: e.tensor_scalar(out=tmp[:], in0=ang[:], scalar1=shift, scalar2=1.0 / TWO_PI,
                                                             op0=ALU.add, op1=ALU.mult), ang.s(), tmp.s())
            P.op(eng, lambda e: e.tensor_copy(out=ki[:], in_=tmp[:]), tmp.s(), ki.s())
            P.op(eng, lambda e: e.tensor_copy(out=kf[:], in_=ki[:]), ki.s(), kf.s())
            P.op(eng, lambda e, shift=shift: e.tensor_scalar(out=tmp[:], in0=ang[:], scalar1=shift, scalar2=None,
                                                             op0=ALU.add), ang.s(), tmp.s())
            P.op(eng, lambda e: e.scalar_tensor_tensor(out=r[:], in0=kf[:], scalar=-C1, in1=tmp[:],
                                                       op0=ALU.mult, op1=ALU.add), kf.s() + tmp.s(), r.s())
            P.op(eng, lambda e: e.scalar_tensor_tensor(out=tmp[:], in0=kf[:], scalar=-C2, in1=r[:],
                                                       op0=ALU.mult, op1=ALU.add), kf.s() + r.s(), tmp.s())
            P.op(eng, lambda e: e.tensor_scalar(out=r[:], in0=tmp[:], scalar1=-math.pi, scalar2=math.pi,
                                                op0=ALU.max, op1=ALU.min), tmp.s(), r.s())
            P.op("act", lambda e, outb=outb: e.activation(out=outb[:], in_=r[:], func=AF.Sin), r.s(), outb.s())

    posi = carve([128, NT], I32)
    posf = carve([128, NT], F32)
    P.op("pool", lambda e: e.iota(posi[:], pattern=[[128, NT]], base=0, channel_multiplier=1), [], posi.s())
    P.op("pool", lambda e: e.iota(posi[0:64, 16:17], pattern=[[0, 1]], base=1024, channel_multiplier=1), posi.s(), posi.s())
    P.op("pool", lambda e: e.iota(posi[64:128, 16:17], pattern=[[0, 1]], base=1024, channel_multiplier=1), posi.s(), posi.s())
    P.op("dve", lambda e: e.tensor_copy(out=posf[:], in_=posi[:]), posi.s(), posf.s())
    rang = carve([128, NT, 4], F32)
    for i in range(4):
        P.op("dve", lambda e, i=i: e.tensor_scalar(out=rang[:, :, i], in0=posf[:], scalar1=float(ROPE_INV[i]), scalar2=None,
                                                   op0=ALU.mult), posf.s(), rang.s())
    rsin = carve([128, NT, 4], F32)
    rcos = carve([128, NT, 4], F32)
    rang2 = Buf(rang[:].rearrange("p a b -> p (a b)"))
    rang2.res = rang.res
    rsin2 = Buf(rsin[:].rearrange("p a b -> p (a b)")); rsin2.res = rsin.res
    rcos2 = Buf(rcos[:].rearrange("p a b -> p (a b)")); rcos2.res = rcos.res
    sincos(rang2, NT * 4, rsin2, rcos2)

    uD_start = cur[0]
    uD = carve([128, 4, 8, KC], BF16, NT)
    attnT = carve([128, 4, NTOK], BF16, NT)
    gT = carve([128, 4, NTOK], BF16, 1)
    keep_end = cur[0]
    qT = carve([128, 4, NTOK], BF16, NT)
    kT = carve([128, 4, NTOK], BF16, NT)
    Vaug = carve([128, NT, 8, 65], BF16, NT)
    kTc = carve([128, 4, 2048], BF16, 16)
    Vaugc = carve([128, 16, 8, 65], BF16, 16)
    pers_end = cur[0]

    P.op("pool", lambda e: e.memset(Vaug[:, :, :, 64:65], 1.0), [], Vaug.s())
    P.op("pool", lambda e: e.memset(Vaugc[:, :, :, 64:65], 1.0), [], Vaugc.s())

    w_in_bf = carve([128, 8, 2048], BF16, 32)
    xb = [carve([128, 1024], BF16) for _ in range(3)]
    xT = [carve([128, 8, 128], BF16) for _ in range(3)]
    qf = [carve([128, 512], F32) for _ in range(2)] * 2
    kf = [carve([128, 512], F32) for _ in range(2)] * 2
    vf = [carve([128, 512], F32) for _ in range(2)] * 2
    qb = [carve([128, 512], BF16) for _ in range(3)]
    kb = [carve([128, 512], BF16) for _ in range(3)]
    rt = [carve([128, 16, 4], F32) for _ in range(4)]

    def rope(buf, tile):
        v = buf[:].rearrange("p (h d) -> p h d", d=32)
        x1 = v[:, :, 0:4]
        x2 = v[:, :, 4:8]
        cs = rcos[:, tile, :].unsqueeze(1).to_broadcast([128, 16, 4])
        sn = rsin[:, tile, :].unsqueeze(1).to_broadcast([128, 16, 4])
        rd = buf.s() + rcos.s() + rsin.s()
        P.op("dve", lambda e: e.tensor_tensor(out=rt[0][:], in0=x1, in1=cs, op=ALU.mult), rd, rt[0].s())
        P.op("dve", lambda e: e.tensor_tensor(out=rt[1][:], in0=x2, in1=sn, op=ALU.mult), rd, rt[1].s())
        P.op("dve", lambda e: e.tensor_tensor(out=rt[2][:], in0=x2, in1=cs, op=ALU.mult), rd, rt[2].s())
        P.op("dve", lambda e: e.tensor_tensor(out=rt[3][:], in0=x1, in1=sn, op=ALU.mult), rd, rt[3].s())
        P.op("dve", lambda e: e.tensor_tensor(out=x1, in0=rt[0][:], in1=rt[1][:], op=ALU.subtract),
             rt[0].s() + rt[1].s() + rt[2].s() + rt[3].s(), buf.s())
        P.op("dve", lambda e: e.tensor_tensor(out=x2, in0=rt[2][:], in1=rt[3][:], op=ALU.add),
             rt[2].s() + rt[3].s(), buf.s())

    def transpose4(src, dstT, col0, bank, ncols=128):
        pv = psv(bank, [128, 4, 128], BF16)
        for blk in range(4):
            P.op("pe", lambda e, blk=blk: e.transpose(out=pv[:, blk, :], in_=src[:, blk * 128:(blk + 1) * 128], identity=ident[:]),
                 src.s() + ident.s(), PS.s(bank))
        return pv

    NB = 3

    def stageL(t):
        b = t % NB
        P.dma("pool", xb[b][:], x_all[t * 128:(t + 1) * 128, :], writes=xb[b].s())

    def stageX(t):
        b = t % NB
        bank = t % 2
        pxT = psv(bank, [128, 8, 128], BF16)
        for dc in range(8):
            P.op("pe", lambda e, dc=dc, b=b: e.transpose(out=pxT[:, dc, :], in_=xb[b][:, dc * 128:(dc + 1) * 128], identity=ident[:]),
                 xb[b].s() + ident.s(), PS.s(bank))
        P.op("act", lambda e, b=b: e.copy(out=xT[b][:], in_=pxT), PS.s(bank), xT[b].s())

    def stageM(t):
        b = t % NB
        b2 = t % 2
        for j, bank in ((0, 2), (1, 3), (2, 4)):
            pv_ = psv(bank, [128, 512])
            for dc in range(8):
                P.op("pe", lambda e, dc=dc, b=b, j=j, pv_=pv_: e.matmul(pv_, lhsT=xT[b][:, dc, :], rhs=w_in_bf[:, dc, j * 512:(j + 1) * 512],
                                                                       start=(dc == 0), stop=(dc == 7)),
                     xT[b].s() + w_in_bf.s(j * 8 + dc), PS.s(bank))
        pu = psv(5, [128, 4, 128])
        for gb in range(4):
            for dc in range(8):
                P.op("pe", lambda e, dc=dc, b=b, gb=gb: e.matmul(pu[:, gb, :], lhsT=w_in_bf[:, dc, 1536 + gb * 128:1536 + (gb + 1) * 128],
                                                                 rhs=xT[b][:, dc, :], start=(dc == 0), stop=(dc == 7)),
                     xT[b].s() + w_in_bf.s(24 + dc), PS.s(5))
        pq, pk, pvv = psv(2, [128, 512]), psv(3, [128, 512]), psv(4, [128, 512])
        P.op("act", lambda e, b2=b2: e.copy(out=qf[b2][:], in_=pq), PS.s(2), qf[b2].s())
        P.op("act", lambda e, b2=b2: e.copy(out=kf[b2][:], in_=pk), PS.s(3), kf[b2].s())
        P.op("dve", lambda e, b2=b2: e.tensor_copy(out=vf[b2][:], in_=pvv), PS.s(4), vf[b2].s())
        P.op("dve", lambda e, t=t: e.tensor_copy(out=uD[:, :, :, 16 * t:16 * (t + 1)],
                                                 in_=pu.rearrange("p g (k j) -> p g j k", j=8)), PS.s(5), uD.s(t))
        rope(qf[b2], t)
        rope(kf[b2], t)
        P.dma("sp", k_out[t * 128:(t + 1) * 128, :], kf[b2][:], reads=kf[b2].s(), writes=k_out.s())
        P.dma("sp", v_out[t * 128:(t + 1) * 128, :], vf[b2][:], reads=vf[b2].s(), writes=v_out.s())
        P.op("pool", lambda e, b=b, b2=b2: e.tensor_copy(out=qb[b][:], in_=qf[b2][:]), qf[b2].s(), qb[b].s())
        P.op("act", lambda e, b=b, b2=b2: e.copy(out=kb[b][:], in_=kf[b2][:]), kf[b2].s(), kb[b].s())
        P.op("pool", lambda e, b2=b2, t=t: e.tensor_copy(out=Vaug[:, t, :, 0:64], in_=vf[b2][:].rearrange("p (h d) -> p h d", d=64)),
             vf[b2].s(), Vaug.s(t))

    def stageT(t):
        b = t % NB
        pqT = transpose4(qb[b], qT, t * 128, 6)
        P.op("dve", lambda e, t=t, pqT=pqT: e.tensor_copy(out=qT[:, :, t * 128:(t + 1) * 128], in_=pqT), PS.s(6), qT.s(t))
        pkT = transpose4(kb[b], kT, t * 128, 7)
        P.op("act", lambda e, t=t, pkT=pkT: e.copy(out=kT[:, :, t * 128:(t + 1) * 128], in_=pkT), PS.s(7), kT.s(t))

    stageL(0)
    stageL(1)
    stageL(2)
    for j in range(4):
        for dc in range(0, 8, 2):
            P.dma("pool", w_in_bf[:, dc:dc + 2, j * 512:(j + 1) * 512],
                  w_in[dc * 128:(dc + 2) * 128, j * 512:(j + 1) * 512].rearrange("(a p) n -> p a n", p=128),
                  writes=w_in_bf.s(range(j * 8 + dc, j * 8 + dc + 2)))
    stageX(0)
    for t in range(NT):
        if t >= 1 and t + 2 < NT:
            stageL(t + 2)
        if t + 1 < NT:
            stageX(t + 1)
        stageM(t)
        if t >= 1:
            stageT(t - 1)
    stageT(NT - 1)

    if stage <= 1:
        P.emit(st)
        return

    rr = [0]

    def alt():
        rr[0] += 1
        return "dve" if rr[0] % 2 else "pool"

    def TT(eng, o, a, b, op, rd, wr):
        P.op(eng, lambda e: e.tensor_tensor(out=o, in0=a, in1=b, op=op), rd, wr)

    def TS(eng, o, a, s1, s2, op0, op1, rd, wr):
        P.op(eng, lambda e: e.tensor_scalar(out=o, in0=a, scalar1=s1, scalar2=s2, op0=op0, op1=op1), rd, wr)

    def STT(eng, o, a, sc, b, op0, op1, rd, wr):
        P.op(eng, lambda e: e.scalar_tensor_tensor(out=o, in0=a, scalar=sc, in1=b, op0=op0, op1=op1), rd, wr)

    MUL, ADD, SUB = ALU.mult, ALU.add, ALU.subtract
    P.barrier()
    cur[0] = pers_end
    lamb = carve([128, 128], F32)
    P.dma("sp", lamb[:], lam_p[0:1, :, :].rearrange("a b c -> a (b c)").to_broadcast([128, 128]), writes=lamb.s())
    lt = carve([128, 64], F32)
    ls = carve([128, 2], F32)
    le = carve([128, 2], F32)
    neglam = carve([128, 1], F32)
    P.op("dve", lambda e: e.tensor_tensor(out=lt[:].rearrange("p (a b) -> p a b", a=2), in0=lamb[:].rearrange("p (a k b) -> p a k b", a=2, k=2)[:, :, 0, :],
                                          in1=lamb[:].rearrange("p (a k b) -> p a k b", a=2, k=2)[:, :, 1, :], op=ALU.mult), lamb.s(), lt.s())
    P.op("dve", lambda e: e.tensor_reduce(out=ls[:], in_=lt[:].rearrange("p (a b) -> p a b", a=2), op=ALU.add, axis=AX.X), lt.s(), ls.s())
    P.op("act", lambda e: e.activation(out=le[:], in_=ls[:], func=AF.Exp), ls.s(), le.s())
    P.op("dve", lambda e: e.scalar_tensor_tensor(out=neglam[:], in0=le[:, 1:2], scalar=-LAM_INIT, in1=le[:, 0:1],
                                                 op0=ALU.add, op1=ALU.subtract), le.s(), neglam.s())
    gsub = carve([128, 64], F32)
    P.dma("sp", gsub[:], subln_g[0:1, :].to_broadcast([128, 64]), writes=gsub.s())
    P.op("dve", lambda e: e.tensor_scalar(out=gsub[:], in0=gsub[:], scalar1=1.0 - LAM_INIT, scalar2=None, op0=ALU.mult), gsub.s(), gsub.s())
    mhalf = carve([128, 8], F32)
    P.op("pool", lambda e: e.memset(mhalf[:], -0.5), [], mhalf.s())
    Vs1 = carve([64, 8, 65], BF16)
    P.dma("sp", Vs1[0:64], Vaug[64:128, 16, :, :], reads=Vaug.s(16), writes=Vs1.s())

    TOP_OFF = ARENA_BYTES - 21 * 1024
    assert cur[0] <= TOP_OFF, cur[0]
    b_cur = cur[0]
    cur[0] = TOP_OFF
    prm = carve([128, 3, 16], F32)
    h0 = carve([128, 4, 16], F32)
    Dcol = carve([128, 4], F32)
    sm = carve([128, 24, 16], F32)
    angb = carve([128, 16], F32)
    snb = carve([128, 16], F32)
    csb = carve([128, 16], F32)
    PWr = carve([128, 9, 16], F32)
    PWi = carve([128, 9, 16], F32)
    P.rec = []
    An = carve([128, 3, 128], F32)
    ldn = carve([128, 2], F32)
    P.dma("sp", An[0:16, 0, :], a_re[:, :].rearrange("(pair g2) p -> pair (g2 p)", g2=2), writes=An.s())
    P.dma("sp", An[0:16, 1, :], a_im[:, :].rearrange("(pair g2) p -> pair (g2 p)", g2=2), writes=An.s())
    P.dma("sp", ldn[0:16, :], log_dt[0:1, :].rearrange("a (pair g2) -> (a pair) g2", g2=2), writes=ldn.s())
    P.op("dve", lambda e: e.tensor_copy(out=An[0:16, 2, :].rearrange("p (g q) -> p g q", g=2),
                                        in_=ldn[0:16, :].unsqueeze(2).to_broadcast([16, 2, 64])), ldn.s() + An.s(), An.s())
    pA = psv(7, [128, 3, 16])
    for i in range(3):
        P.op("pe", lambda e, i=i: e.transpose(out=pA[:, i, :], in_=An[0:16, i, :], identity=identf[0:16, 0:16]), An.s() + identf.s(), PS.s(7))
    P.op("dve", lambda e: e.tensor_copy(out=prm[:], in_=pA), PS.s(7), prm.s())
    are, aim, ldt = prm[:, 0, :], prm[:, 1, :], prm[:, 2, :]
    Hn = carve([128, 4, 128], F32)
    for s in range(2):
        P.dma("sp", Hn[0:16, 2 * s, :], st_re[s].rearrange("(pair g2) p -> pair (g2 p)", g2=2), writes=Hn.s())
        P.dma("sp", Hn[0:16, 2 * s + 1, :], st_im[s].rearrange("(pair g2) p -> pair (g2 p)", g2=2), writes=Hn.s())
    pH0 = psv(7, [128, 4, 16])
    for i in range(4):
        P.op("pe", lambda e, i=i: e.transpose(out=pH0[:, i, :], in_=Hn[0:16, i, :], identity=identf[0:16, 0:16]), Hn.s() + identf.s(), PS.s(7))
    P.op("dve", lambda e: e.tensor_copy(out=h0[:], in_=pH0), PS.s(7), h0.s())
    Bre = carve([128, 16, 16], F32)
    Bim = carve([128, 16, 16], F32)
    for g2 in range(2):
        P.dma("sp", Bre[64 * g2:64 * (g2 + 1), :, :], b_re[:].rearrange("(pair g2) p c -> g2 p pair c", g2=2)[g2], writes=Bre.s())
        P.dma("sp", Bim[64 * g2:64 * (g2 + 1), :, :], b_im[:].rearrange("(pair g2) p c -> g2 p pair c", g2=2)[g2], writes=Bim.s())
    Cn = carve([128, 2, 4, 64], F32)
    P.dma("sp", Cn[:, 0, :, :], c_re[:].rearrange("(gb gl) c p -> (gl c) gb p", gb=4), writes=Cn.s())
    P.dma("sp", Cn[:, 1, :, :], c_im[:].rearrange("(gb gl) c p -> (gl c) gb p", gb=4), writes=Cn.s())
    CT = carve([128, 2, 512], F32)
    for ri in range(2):
        pC = psv(7, [128, 4, 128])
        for gb in range(4):
            P.op("pe", lambda e, gb=gb, ri=ri, pC=pC: e.transpose(out=pC[0:64, gb, :], in_=Cn[:, ri, gb, :], identity=identf[:]),
                 Cn.s() + identf.s(), PS.s(7))
        P.op("dve", lambda e, ri=ri, pC=pC: e.tensor_copy(out=CT[0:64, ri, :], in_=pC[0:64].rearrange("p a b -> p (a b)")), PS.s(7), CT.s())
    Cre = carve([128, 16, 16], F32)
    Cim = carve([128, 16, 16], F32)
    for g2 in range(2):
        P.dma("sp", Cre[64 * g2:64 * (g2 + 1), :, :], CT[0:64, 0, :].rearrange("p (pair g2 c) -> p g2 pair c", g2=2, c=16)[:, g2], reads=CT.s(), writes=Cre.s())
        P.dma("sp", Cim[64 * g2:64 * (g2 + 1), :, :], CT[0:64, 1, :].rearrange("p (pair g2 c) -> p g2 pair c", g2=2, c=16)[:, g2], reads=CT.s(), writes=Cim.s())
    Dn = carve([128, 128], F32)
    P.dma("sp", Dn[0:4, :], ssm_d[:].rearrange("(gb gl) c -> gb (gl c)", gb=4), writes=Dn.s())
    pD = psv(7, [128, 4])
    P.op("pe", lambda e: e.transpose(out=pD, in_=Dn[0:4, :], identity=identf[0:4, 0:4]), Dn.s() + identf.s(), PS.s(7))
    P.op("dve", lambda e: e.tensor_copy(out=Dcol[:], in_=pD), PS.s(7), Dcol.s())

    SMs = sm.s()
    dtt, adt, ang, mag, sn_, cs_, lr, li, den, nr, fre, fim, r8, rr8, er, ei, t1s, t2s = [sm[:, i, :] for i in range(18)]
    P.op("act", lambda e: e.activation(out=dtt, in_=ldt, func=AF.Exp), prm.s(), SMs)
    TT("dve", adt, are, dtt, MUL, prm.s() + SMs, SMs)
    TT("dve", ang, aim, dtt, MUL, prm.s() + SMs, SMs)
    P.op("act", lambda e: e.activation(out=mag, in_=adt, func=AF.Exp), SMs, SMs)
    P.op("act", lambda e: e.activation(out=r8, in_=adt, func=AF.Exp, scale=8.0), SMs, SMs)
    P.op("act", lambda e: e.activation(out=rr8, in_=adt, func=AF.Exp, scale=-8.0), SMs, SMs)
    P.op("dve", lambda e: e.tensor_copy(out=angb[:], in_=ang), SMs, angb.s())
    sincos(angb, 16, snb, csb)
    TT("dve", lr, mag, csb[:], MUL, SMs + csb.s(), SMs)
    TT("dve", li, mag, snb[:], MUL, SMs + snb.s(), SMs)
    TT("dve", t1s, are, are, MUL, prm.s(), SMs)
    TT("dve", t2s, aim, aim, MUL, prm.s(), SMs)
    TT("dve", den, t1s, t2s, ADD, SMs, SMs)
    P.op("dve", lambda e: e.reciprocal(out=den, in_=den), SMs, SMs)
    TS("dve", nr, lr, -1.0, None, ADD, ALU.bypass, SMs, SMs)
    TT("dve", t1s, nr, are, MUL, SMs + prm.s(), SMs)
    TT("dve", t2s, li, aim, MUL, SMs + prm.s(), SMs)
    TT("dve", t1s, t1s, t2s, ADD, SMs, SMs)
    TT("dve", fre, t1s, den, MUL, SMs, SMs)
    TT("dve", t1s, li, are, MUL, SMs + prm.s(), SMs)
    TT("dve", t2s, nr, aim, MUL, SMs + prm.s(), SMs)
    TT("dve", t1s, t1s, t2s, SUB, SMs, SMs)
    TT("dve", fim, t1s, den, MUL, SMs, SMs)
    PWs = PWr.s() + PWi.s()
    P.op("dve", lambda e: e.memset(PWr[:, 0, :], 1.0), [], PWs)
    P.op("dve", lambda e: e.memset(PWi[:, 0, :], 0.0), [], PWs)
    for m in range(1, 9):
        TT("dve", t1s, PWr[:, m - 1, :], lr, MUL, PWs + SMs, SMs)
        TT("dve", t2s, PWi[:, m - 1, :], li, MUL, PWs + SMs, SMs)
        TT("dve", PWr[:, m, :], t1s, t2s, SUB, SMs, PWs)
        TT("dve", t1s, PWr[:, m - 1, :], li, MUL, PWs + SMs, SMs)
        TT("dve", t2s, PWi[:, m - 1, :], lr, MUL, PWs + SMs, SMs)
        TT("dve", PWi[:, m, :], t1s, t2s, ADD, SMs, PWs)
    TT("dve", er, PWr[:, 8, :], rr8, MUL, PWs + SMs, SMs)
    TT("dve", ei, PWi[:, 8, :], rr8, MUL, PWs + SMs, SMs)

    early_ops = P.rec
    P.rec = None
    hoist = [o for o in early_ops if o[4] and not o[2]]
    early_ops = [o for o in early_ops if not (o[4] and not o[2])]
    for o in hoist:
        P._add(*o)
    top_end = cur[0]
    cur[0] = b_cur
    ckb = [carve([128, 512], BF16) for _ in range(4)]

    def cache_load(idx):
        s, c = idx // 8, idx % 8
        P.dma("pool", ckb[idx % 4][:], cache_k[s, c * 128:(c + 1) * 128, :], writes=ckb[idx % 4].s())
        P.dma("pool", Vaugc[:, idx, :, 0:64], cache_v[s, c * 128:(c + 1) * 128, :].rearrange("p (h d) -> p h d", d=64),
              writes=Vaugc.s(idx))

    def cache_tr(idx):
        pT = transpose4(ckb[idx % 4], kTc, idx * 128, 6)
        P.op("act", lambda e, idx=idx, pT=pT: e.copy(out=kTc[:, :, idx * 128:(idx + 1) * 128], in_=pT), PS.s(6), kTc.s(idx))

    PT = [carve([128, 8, 128], BF16, 2) for _ in range(3)]
    attn_tm = [carve([128, 512], BF16) for _ in range(5)]
    qBD = [carve([128, 4, 512], BF16) for _ in range(2)]
    qBDs = [carve([128, 4, 256], BF16) for _ in range(2)]
    for qb_ in qBD + qBDs:
        P.op("pool", lambda e, qb_=qb_: e.memset(qb_[:], 0.0), [], qb_.s())
    Osb = [carve([128, 16, 65], F32, 2) for _ in range(3)]
    rl = carve([128, 16], F32)
    rl2n = carve([128, 8], F32)
    o1 = carve([128, 8, 64], F32)
    o2 = carve([128, 8, 64], F32)
    osq = o2
    ms = carve([128, 8], F32)
    rstd = carve([128, 8], F32)
    rti = carve([128, 8], I32)
    rtn = carve([128, 8], F32)
    SCALE = 32.0 ** -0.5
    assert cur[0] <= TOP_OFF, (cur[0], TOP_OFF)
    Oall = psv(4, [128, 8, 128], nb=2)

    qtiles = []
    for i in [15, 0, 14, 1, 13, 2, 12, 3, 11, 4, 10, 5, 9, 6, 8, 7]:
        specs = []
        for j in range(i + 1):
            specs.append(dict(nk=128, mask=(j == i), deps=kT.s(j) + Vaug.s(j),
                              kT=(lambda blk, j=j: kT[:, blk, j * 128:(j + 1) * 128]),
                              V=(lambda h, j=j: Vaug[:, j, h, :])))
        qtiles.append(dict(qcol0=i * 128, nq=128, qdeps=qT.s(i), specs=specs, dst=attnT.s(i)))
    for s in range(2):
        specs = []
        for c in range(8):
            idx = 8 * s + c
            specs.append(dict(nk=128, mask=False, deps=kTc.s(idx) + Vaugc.s(idx),
                              kT=(lambda blk, idx=idx: kTc[:, blk, idx * 128:(idx + 1) * 128]),
                              V=(lambda h, idx=idx: Vaugc[:, idx, h, :])))
        if s == 0:
            Vf = (lambda h: Vaug[0:64, 16, h, :])
            vd = Vaug.s(16)
        else:
            Vf = (lambda h: Vs1[0:64, h, :])
            vd = Vs1.s()
        specs.append(dict(nk=64, mask=False, deps=kT.s(16) + vd,
                          kT=(lambda blk, s=s: kT[:, blk, 2048 + 64 * s:2048 + 64 * (s + 1)]), V=Vf))
        qtiles.append(dict(qcol0=2048 + 64 * s, nq=64, qdeps=qT.s(16), specs=specs, dst=attnT.s(16)))

    steps = []
    for qi, qt in enumerate(qtiles):
        for hs in range(2):
            n = len(qt["specs"])
            for ji, ks in enumerate(qt["specs"]):
                steps.append(dict(qi=qi, qt=qt, hs=hs, ji=ji, ks=ks, first=(ji == 0), last=(ji == n - 1)))

    built_q = set()

    def build_q(qi):
        if qi in built_q:
            return
        built_q.add(qi)
        qt = qtiles[qi]
        nq = qt["nq"]
        qb_ = qBD[qi % 2] if nq == 128 else qBDs[qi % 2]
        for r in range(4):
            eng = "pool" if r % 2 else "dve"
            P.op(eng, lambda e, r=r, qb_=qb_, qt=qt, nq=nq: e.tensor_copy(
                out=qb_[32 * r:32 * r + 32, :, r * nq:(r + 1) * nq], in_=qT[32 * r:32 * r + 32, :, qt["qcol0"]:qt["qcol0"] + nq]),
                qt["qdeps"] + qb_.s(), qb_.s())

    def emit_qk(si, sp):
        qt, hs, ks = sp["qt"], sp["hs"], sp["ks"]
        nq, nk, qi = qt["nq"], ks["nk"], sp["qi"]
        qb_ = qBD[qi % 2] if nq == 128 else qBDs[qi % 2]
        build_q(qi)
        if sp["first"] and sp["hs"] == 0 and qi + 1 < len(qtiles):
            build_q(qi + 1)
        sb0 = 2 * (si % 2)
        for bp in range(2):
            blk = 2 * hs + bp
            P.op("pe", lambda e, ks=ks, blk=blk, bp=bp, nk=nk, nq=nq, qb_=qb_, sb0=sb0: e.matmul(
                psum[0:nk, sb0 + bp, 0:4 * nq], lhsT=ks["kT"](blk), rhs=qb_[:, blk, 0:4 * nq], start=True, stop=True),
                qb_.s() + ks["deps"], PS.s(sb0 + bp))
        pt = PT[si % 3]
        sp["pt"] = pt
        for bp in range(2):
            Sv = psum[0:nk, sb0 + bp, 0:4 * nq].rearrange("p (r q) -> p r q", r=4)
            P.op("act", lambda e, pt=pt, nk=nk, nq=nq, Sv=Sv, bp=bp: e.activation(
                out=pt[0:nk, 4 * bp:4 * bp + 4, 0:nq], in_=Sv, func=AF.Exp, scale=SCALE),
                PS.s(sb0 + bp) + pt.s(bp), pt.s(bp))
            if ks["mask"]:
                P.op("pool", lambda e, pt=pt, bp=bp: e.memset(pt[64:128, 4 * bp:4 * bp + 4, 0:64], 0.0), pt.s(bp), pt.s(bp))

    def emit_av(si, sp):
        qt, hs, ks, pt = sp["qt"], sp["hs"], sp["ks"], sp["pt"]
        nq, nk = qt["nq"], ks["nk"]
        for e8 in range(8):
            hh = 8 * hs + e8
            h = hh // 2
            P.op("pe", lambda e, ks=ks, pt=pt, h=h, e8=e8, nk=nk, nq=nq, sp=sp: e.matmul(
                Oall[0:nq, e8, 0:65], lhsT=pt[0:nk, e8, 0:nq], rhs=ks["V"](h),
                start=(sp["first"] and e8 % 4 == 0), stop=sp["last"], skip_group_check=True), pt.s(e8 // 4) + ks["deps"], PS.s(range(4, 6)))
        if sp["last"]:
            finalize(sp)

    fcnt = [0]

    def finalize(sp):
        qt, hs, qi = sp["qt"], sp["hs"], sp["qi"]
        nq = qt["nq"]
        am = attn_tm[qi % 5]
        ob = Osb[qi % 3]
        P.op("act", lambda e, hs=hs: e.copy(out=ob[0:nq, 8 * hs:8 * hs + 8, :], in_=Oall[0:nq, :, 0:65]), PS.s(range(4, 6)) + ob.s(hs), ob.s(hs))
        if hs == 0:
            return
        Ov = ob[:].rearrange("p (h m) d -> p h m d", m=2)
        P.op("dve", lambda e: e.reciprocal(out=rl[0:nq, :], in_=ob[0:nq, :, 64]), ob.s(), rl.s())
        rlv = rl[:].rearrange("p (h m) -> p h m", m=2)
        P.op("dve", lambda e: e.tensor_tensor(out=rl2n[0:nq, :], in0=rlv[0:nq, :, 1], in1=neglam[0:nq, 0:1].to_broadcast([nq, 8]), op=ALU.mult),
             rl.s() + neglam.s(), rl2n.s())
        P.op("dve", lambda e: e.tensor_tensor(out=o1[0:nq], in0=Ov[0:nq, :, 0, 0:64], in1=rlv[0:nq, :, 0].unsqueeze(2).to_broadcast([nq, 8, 64]), op=ALU.mult),
             ob.s() + rl.s(), o1.s())
        P.op("dve", lambda e: e.tensor_tensor(out=o2[0:nq], in0=Ov[0:nq, :, 1, 0:64], in1=rl2n[0:nq, :].unsqueeze(2).to_broadcast([nq, 8, 64]), op=ALU.mult),
             ob.s() + rl2n.s(), o2.s())
        P.op("dve", lambda e: e.tensor_tensor(out=o1[0:nq], in0=o1[0:nq], in1=o2[0:nq], op=ALU.add), o1.s() + o2.s(), o1.s())
        P.op("dve", lambda e: e.tensor_tensor(out=osq[0:nq], in0=o1[0:nq], in1=o1[0:nq], op=ALU.mult), o1.s(), osq.s())
        P.op("dve", lambda e: e.tensor_reduce(out=ms[0:nq, :], in_=osq[0:nq], op=ALU.add, axis=AX.X), osq.s(), ms.s())
        P.op("dve", lambda e: e.tensor_scalar(out=ms[0:nq, :], in0=ms[0:nq, :], scalar1=1.0 / 64.0, scalar2=1e-5, op0=ALU.mult, op1=ALU.add), ms.s(), ms.s())
        P.op("dve", lambda e: e.tensor_single_scalar(out=rti[0:nq, :], in_=ms[0:nq, :].bitcast(I32), scalar=1, op=ALU.logical_shift_right), ms.s(), rti.s())
        P.op("dve", lambda e: e.tensor_scalar(out=rti[0:nq, :], in0=rti[0:nq, :], scalar1=-1.0, scalar2=1597463007.0, op0=ALU.mult, op1=ALU.add), rti.s(), rti.s())
        for it in range(2):
            ysrc = rti[0:nq, :].bitcast(F32) if it == 0 else rstd[0:nq, :]
            P.op("dve", lambda e, ysrc=ysrc: e.tensor_tensor(out=rtn[0:nq, :], in0=ysrc, in1=ysrc, op=ALU.mult), rti.s() + rstd.s(), rtn.s())
            P.op("dve", lambda e: e.tensor_tensor(out=rtn[0:nq, :], in0=rtn[0:nq, :], in1=ms[0:nq, :], op=ALU.mult), rtn.s() + ms.s(), rtn.s())
            P.op("dve", lambda e: e.tensor_scalar(out=rtn[0:nq, :], in0=rtn[0:nq, :], scalar1=-0.5, scalar2=1.5, op0=ALU.mult, op1=ALU.add), rtn.s(), rtn.s())
            P.op("dve", lambda e, ysrc=ysrc: e.tensor_tensor(out=rstd[0:nq, :], in0=ysrc, in1=rtn[0:nq, :], op=ALU.mult), rtn.s() + rti.s() + rstd.s(), rstd.s())
        P.op("dve", lambda e: e.tensor_tensor(out=o1[0:nq], in0=o1[0:nq], in1=rstd[0:nq, :].unsqueeze(2).to_broadcast([nq, 8, 64]), op=ALU.mult),
             o1.s() + rstd.s(), o1.s())
        P.op("dve", lambda e: e.tensor_tensor(out=am[0:nq, :].rearrange("p (h d) -> p h d", d=64), in0=o1[0:nq],
                                              in1=gsub[0:nq, :].unsqueeze(1).to_broadcast([nq, 8, 64]), op=ALU.mult),
             o1.s() + gsub.s() + am.s(), am.s())
        if hs == 1:
            pendT.append((qt, am, nq, qi + 3))

    pendT = []
    cur_done = [-1]

    def emit_T(force=False):
        keep = []
        for item in pendT:
            qt, am, nq, due = item
            if not force and not (cur_done[0] >= due):
                keep.append(item)
                continue
            pT = psv(6, [128, 4, 128], BF16)
            for blk in range(4):
                P.op("pe", lambda e, blk=blk, am=am, nq=nq, pT=pT: e.transpose(out=pT[:, blk, 0:nq], in_=am[0:nq, blk * 128:(blk + 1) * 128], identity=ident[0:nq, 0:nq]),
                     am.s() + ident.s(), PS.s(6))
            P.op("act", lambda e, qt=qt, nq=nq, pT=pT: e.copy(out=attnT[:, :, qt["qcol0"]:qt["qcol0"] + nq], in_=pT[:, :, 0:nq]), PS.s(6), qt["dst"])
        pendT[:] = keep

    for idx in range(3):
        cache_load(idx)
    cdone = [0]
    for si, sp in enumerate(steps):
        emit_qk(si, sp)
        if si >= 1:
            emit_av(si - 1, steps[si - 1])
            pv_ = steps[si - 1]
            if pv_["last"] and pv_["hs"] == 0:
                cur_done[0] = pv_["qi"]
        emit_T()
        if si >= 2:
            P.flush(early_ops, 1)
        if sp["first"] and sp["hs"] == 0 and cdone[0] < 16 and sp["qi"] >= 1:
            cache_tr(cdone[0])
            if cdone[0] + 3 < 16:
                cache_load(cdone[0] + 3)
            cdone[0] += 1
    emit_av(len(steps) - 1, steps[-1])
    emit_T(force=True)
    P.flush(early_ops, len(early_ops))
    assert cdone[0] == 16

    if debug:
        dbg_attn = dout("dbg_attnT", [128, 4, NTOK], BF16)
        P.dma("sp", dbg_attn[:], attnT[:], reads=attnT.s(), writes=dbg_attn.s())
    if stage <= 2:
        P.emit(st)
        return

    P.barrier()
    cur[0] = keep_end
    BB = carve([128, 16, 2, 32], BF16)
    Cint = carve([128, 16, 2, 9, 32], BF16)
    Tz = carve([128, 4, 8, 128], BF16)
    Wb_off = cur[0]
    Wb = carve([128, 4, 2, 8, 128], BF16)
    lvl2_start = cur[0]

    def bc16(v):
        return v.unsqueeze(2).to_broadcast([128, 16, 16])

    Bbr = carve([128, 16, 16], F32)
    Bbi = carve([128, 16, 16], F32)
    T1 = carve([128, 16, 16], F32)
    T2 = carve([128, 16, 16], F32)
    TT("dve", T1[:], Bre[:], bc16(fre), MUL, Bre.s() + SMs, T1.s())
    TT("pool", T2[:], Bim[:], bc16(fim), MUL, Bim.s() + SMs, T2.s())
    TT("dve", Bbr[:], T1[:], T2[:], SUB, T1.s() + T2.s(), Bbr.s())
    TT("dve", T1[:], Bim[:], bc16(fre), MUL, Bim.s() + SMs, T1.s())
    TT("pool", T2[:], Bre[:], bc16(fim), MUL, Bre.s() + SMs, T2.s())
    TT("dve", Bbi[:], T1[:], T2[:], ADD, T1.s() + T2.s(), Bbi.s())
    P.op("pool", lambda e: e.memset(BB[:], 0.0), [], BB.s())
    for g2 in range(2):
        sl = slice(64 * g2, 64 * (g2 + 1))
        P.op("dve", lambda e, sl=sl, g2=g2: e.tensor_copy(out=BB[sl, :, 0, 16 * g2:16 * (g2 + 1)], in_=Bbr[sl]), Bbr.s() + BB.s(), BB.s())
        P.op("dve", lambda e, sl=sl, g2=g2: e.tensor_copy(out=BB[sl, :, 1, 16 * g2:16 * (g2 + 1)], in_=Bbi[sl]), Bbi.s() + BB.s(), BB.s())
    P.op("pool", lambda e: e.memset(Cint[:], 0.0), [], Cint.s())
    CW1 = carve([128, 16, 9, 16], F32)
    CW2 = carve([128, 16, 9, 16], F32)

    def bcC(v):
        return v.unsqueeze(2).to_broadcast([128, 16, 9, 16])

    def bcP(v):
        return v.rearrange("p m q -> p q m").unsqueeze(3).to_broadcast([128, 16, 9, 16])

    TT("dve", CW1[:], bcC(Cre[:]), bcP(PWr[:]), MUL, Cre.s() + PWs, CW1.s())
    TT("dve", CW2[:], bcC(Cim[:]), bcP(PWi[:]), MUL, Cim.s() + PWs, CW2.s())
    TT("dve", CW1[:], CW1[:], CW2[:], SUB, CW1.s() + CW2.s(), CW1.s())
    for g2 in range(2):
        sl = slice(64 * g2, 64 * (g2 + 1))
        P.op("act", lambda e, sl=sl, g2=g2: e.copy(out=Cint[sl, :, 0, :, 16 * g2:16 * (g2 + 1)], in_=CW1[sl]), CW1.s() + Cint.s(), Cint.s())
    CW3 = carve([128, 16, 9, 16], F32)
    TT("dve", CW3[:], bcC(Cre[:]), bcP(PWi[:]), MUL, Cre.s() + PWs, CW3.s())
    TT("dve", CW2[:], bcC(Cim[:]), bcP(PWr[:]), MUL, Cim.s() + PWs + CW2.s(), CW2.s())
    STT("dve", CW3[:], CW3[:], -1.0, CW2[:], MUL, SUB, CW3.s() + CW2.s(), CW3.s())
    for g2 in range(2):
        sl = slice(64 * g2, 64 * (g2 + 1))
        P.op("act", lambda e, sl=sl, g2=g2: e.copy(out=Cint[sl, :, 1, :, 16 * g2:16 * (g2 + 1)], in_=CW3[sl]), CW3.s() + Cint.s(), Cint.s())
    Tzf = carve([128, 4, 128], F32)
    P.op("pool", lambda e: e.memset(Tz[:], 0.0), [], Tz.s())
    P.op("pool", lambda e: e.memset(Tzf[:], 0.0), [], Tzf.s())
    pz = psv(5, [128, 32, 32], nb=2)
    for pair in range(16):
        gb, m = pair // 4, pair % 4
        for tau in range(8):
            for ri in range(2):
                P.op("pe", lambda e, pair=pair, gb=gb, m=m, tau=tau, ri=ri: e.matmul(
                    pz[32 * m:32 * (m + 1), gb * 8 + tau, :], lhsT=BB[:, pair, ri, :], rhs=Cint[:, pair, ri, tau, :],
                    start=(ri == 0), stop=(ri == 1), tile_position=(0, 32 * m), skip_group_check=True), BB.s() + Cint.s(), PS.s(range(5, 7)))
    pzv = pz.rearrange("p (gb t) c -> p gb t c", t=8)
    for m in range(4):
        sl = slice(32 * m, 32 * (m + 1))
        P.op("dve", lambda e, sl=sl: e.tensor_copy(out=Tz[sl, :, 1:8, sl], in_=pzv[sl, :, 1:8, :]), PS.s(range(5, 7)) + Tz.s(), Tz.s())
        P.op("dve", lambda e, sl=sl: e.tensor_copy(out=Tzf[sl, :, sl], in_=pzv[sl, :, 0, :]), PS.s(range(5, 7)) + Tzf.s(), Tzf.s())
    for gb in range(4):
        STT("dve", Tz[:, gb, 0, :], identf[:], Dcol[:, gb:gb + 1], Tzf[:, gb, :], MUL, ADD, identf.s() + Dcol.s() + Tzf.s() + Tz.s(), Tz.s())
    Nat = carve([128, 4, 2, 8, 128], F32)
    P.op("pool", lambda e: e.memset(Nat[:], 0.0), [], Nat.s())
    Natv = Nat[:].rearrange("p gb r j (m g c) -> p gb r j m g c", m=4, g=2)
    NW1 = carve([128, 16, 8, 16], F32)
    NW2 = carve([128, 16, 8, 16], F32)

    def bcB(v):
        return v.unsqueeze(2).to_broadcast([128, 16, 8, 16])

    def bcP8(v):
        return v[:, 0:8, :].rearrange("p m q -> p q m").unsqueeze(3).to_broadcast([128, 16, 8, 16])

    for ri in range(2):
        if ri == 0:
            TT("dve", NW1[:], bcB(Bbr[:]), bcP8(PWr), MUL, Bbr.s() + PWs + NW1.s(), NW1.s())
            TT("dve", NW2[:], bcB(Bbi[:]), bcP8(PWi), MUL, Bbi.s() + PWs + NW2.s(), NW2.s())
            TT("dve", NW1[:], NW1[:], NW2[:], SUB, NW1.s() + NW2.s(), NW1.s())
        else:
            TT("dve", NW1[:], bcB(Bbr[:]), bcP8(PWi), MUL, Bbr.s() + PWs + NW1.s(), NW1.s())
            TT("dve", NW2[:], bcB(Bbi[:]), bcP8(PWr), MUL, Bbi.s() + PWs + NW2.s(), NW2.s())
            TT("dve", NW1[:], NW1[:], NW2[:], ADD, NW1.s() + NW2.s(), NW1.s())
        for g2 in range(2):
            sl = slice(64 * g2, 64 * (g2 + 1))
            for gb in range(4):
                P.op("act", lambda e, sl=sl, g2=g2, gb=gb, ri=ri: e.copy(out=Natv[sl, gb, ri, :, :, g2, :],
                                                                        in_=NW1[sl, 4 * gb:4 * gb + 4, :, :].rearrange("p m q c -> p q m c")),
                     NW1.s() + Nat.s(), Nat.s())
    tcnt = 0
    for gb in range(4):
        for ri in range(2):
            for jh in range(2):
                bank = tcnt % 2
                tcnt += 1
                pW = psv(bank, [128, 4, 128])
                for jj in range(4):
                    j = jh * 4 + jj
                    P.op("pe", lambda e, gb=gb, ri=ri, j=j, jj=jj, pW=pW: e.transpose(out=pW[:, jj, :], in_=Nat[:, gb, ri, 7 - j, :], identity=identf[:]),
                         Nat.s() + identf.s(), PS.s(bank))
                P.op("act" if tcnt % 2 else "dve",
                     (lambda e, gb=gb, ri=ri, jh=jh, pW=pW: e.copy(out=Wb[:, gb, ri, 4 * jh:4 * jh + 4, :], in_=pW)) if tcnt % 2 else
                     (lambda e, gb=gb, ri=ri, jh=jh, pW=pW: e.tensor_copy(out=Wb[:, gb, ri, 4 * jh:4 * jh + 4, :], in_=pW)),
                     PS.s(bank), Wb.s())

    assert cur[0] <= TOP_OFF, (cur[0], TOP_OFF)
    P.barrier()
    cur[0] = lvl2_start
    Ec = carve([128, 8, KC], F32)
    Es = carve([128, 8, KC], F32)
    Sr = carve([128, 8, KC], F32)
    Si = carve([128, 8, KC], F32)
    Xr = carve([128, 8, KC], F32)
    Xi = carve([128, 8, KC], F32)
    U1 = carve([128, 8, KC], F32)
    U2 = carve([128, 8, KC], F32)
    Hbr = carve([128, 16, KC], BF16)
    Hbi = carve([128, 16, KC], BF16)
    Hfin = carve([128, 2, 3, 16], F32)
    assert cur[0] <= TOP_OFF, (cur[0], TOP_OFF)
    segs = [(0, 256, None), (256, 8, 0), (264, 8, 1)]
    for hf in range(2):
        ps8 = slice(8 * hf, 8 * hf + 8)
        Es_ = Ec.s() + Es.s()
        P.op("dve", lambda e, ps8=ps8: e.tensor_copy(out=Ec[:, :, 0], in_=er[:, ps8]), SMs + Es_, Es_)
        P.op("dve", lambda e, ps8=ps8: e.tensor_copy(out=Es[:, :, 0], in_=ei[:, ps8]), SMs + Es_, Es_)
        n = 1
        while n < 256:
            bc_c = Ec[:, :, n - 1:n].to_broadcast([128, 8, n])
            bc_s = Es[:, :, n - 1:n].to_broadcast([128, 8, n])
            a_c, a_s = Ec[:, :, 0:n], Es[:, :, 0:n]
            u1, u2 = U1[:, :, 0:n], U2[:, :, 0:n]
            TT("dve", u1, a_c, bc_c, MUL, Es_, U1.s())
            TT("dve", u2, a_s, bc_s, MUL, Es_, U2.s())
            TT("dve", Ec[:, :, n:2 * n], u1, u2, SUB, U1.s() + U2.s() + Es_, Es_)
            TT("dve", u1, a_c, bc_s, MUL, Es_, U1.s())
            TT("dve", u2, a_s, bc_c, MUL, Es_, U2.s())
            TT("dve", Es[:, :, n:2 * n], u1, u2, ADD, U1.s() + U2.s() + Es_, Es_)
            n *= 2
        for c0 in (256, 264):
            P.op("dve", lambda e, c0=c0: e.tensor_copy(out=Ec[:, :, c0:c0 + 8], in_=Ec[:, :, 0:8]), Es_, Es_)
            P.op("dve", lambda e, c0=c0: e.tensor_copy(out=Es[:, :, c0:c0 + 8], in_=Es[:, :, 0:8]), Es_, Es_)
        for gbl in range(2):
            gb = 2 * hf + gbl
            for ri in range(2):
                b0 = 4 * ((gbl * 2 + ri) % 2)
                for j in range(8):
                    for m in range(4):
                        P.op("pe", lambda e, gb=gb, ri=ri, j=j, m=m, b0=b0: e.matmul(
                            psum[:, b0 + m, 0:KC], lhsT=Wb[32 * m:32 * (m + 1), gb, ri, j, :], rhs=uD[32 * m:32 * (m + 1), gb, j, :],
                            start=(j == 0), stop=(j == 7), tile_position=(32 * m, 0)), Wb.s() + uD.s(), PS.s(range(b0, b0 + 4)))
                dst = Sr if ri == 0 else Si
                P.op("act", lambda e, dst=dst, gbl=gbl, b0=b0: e.copy(out=dst[:, 4 * gbl:4 * gbl + 4, :], in_=psum[:, b0:b0 + 4, 0:KC]),
                     PS.s(range(b0, b0 + 4)), dst.s())
        TT("dve", U1[:], Ec[:], Sr[:], MUL, Es_ + Sr.s(), U1.s())
        TT("dve", U2[:], Es[:], Si[:], MUL, Es_ + Si.s(), U2.s())
        TT("dve", Xr[:], U1[:], U2[:], ADD, U1.s() + U2.s(), Xr.s())
        TT("dve", U1[:], Ec[:], Si[:], MUL, Es_ + Si.s(), U1.s())
        TT("dve", U2[:], Es[:], Sr[:], MUL, Es_ + Sr.s(), U2.s())
        TT("dve", Xi[:], U1[:], U2[:], SUB, U1.s() + U2.s(), Xi.s())
        for pl in range(8):
            pair = 8 * hf + pl
            for (c0, n, sidx) in segs:
                for part, (src, dst) in enumerate(((Xr, Sr), (Xi, Si))):
                    if sidx is None:
                        init = 0.0
                        rdi = []
                    else:
                        init = h0[:, 2 * sidx + part, pair:pair + 1]
                        rdi = h0.s()
                    P.op("dve", lambda e, src=src, dst=dst, pl=pl, pair=pair, c0=c0, n=n, init=init: e.tensor_tensor_scan(
                        out=dst[:, pl, c0:c0 + n], data0=r8[:, pair:pair + 1].to_broadcast([128, n]), data1=src[:, pl, c0:c0 + n],
                        initial=init, op0=MUL, op1=ADD), src.s() + SMs + rdi + dst.s(), dst.s())
        TT("dve", U1[:], Ec[:], Sr[:], MUL, Es_ + Sr.s(), U1.s())
        TT("dve", U2[:], Es[:], Si[:], MUL, Es_ + Si.s(), U2.s())
        TT("dve", Xr[:], U1[:], U2[:], SUB, U1.s() + U2.s() + Xr.s(), Xr.s())
        TT("dve", U1[:], Ec[:], Si[:], MUL, Es_ + Si.s(), U1.s())
        TT("dve", U2[:], Es[:], Sr[:], MUL, Es_ + Sr.s(), U2.s())
        TT("dve", Xi[:], U1[:], U2[:], ADD, U1.s() + U2.s() + Xi.s(), Xi.s())
        for part, (X_, Hb) in enumerate(((Xr, Hbr), (Xi, Hbi))):
            for si, (c0, n, sidx) in enumerate(segs):
                P.op("pool", lambda e, X_=X_, part=part, si=si, c0=c0, n=n, ps8=ps8: e.tensor_copy(out=Hfin[:, part, si, ps8], in_=X_[:, :, c0 + n - 1]),
                     X_.s() + Hfin.s(), Hfin.s())
                P.op("act", lambda e, X_=X_, Hb=Hb, c0=c0, n=n, ps8=ps8: e.copy(out=Hb[:, ps8, c0 + 1:c0 + n], in_=X_[:, :, c0:c0 + n - 1]),
                     X_.s() + Hb.s(), Hb.s())
                if sidx is None:
                    P.op("pool", lambda e, Hb=Hb, c0=c0, ps8=ps8: e.memset(Hb[:, ps8, c0:c0 + 1], 0.0), Hb.s(), Hb.s())
                else:
                    P.op("pool", lambda e, Hb=Hb, c0=c0, ps8=ps8, sidx=sidx, part=part: e.tensor_copy(out=Hb[:, ps8, c0], in_=h0[:, 2 * sidx + part, ps8]),
                         h0.s() + Hb.s(), Hb.s())
    P.barrier()
    wglu_bf = carve([128, 4, 1024], BF16, at=Wb_off)
    wout_bf = carve([128, 8, 1024], BF16, at=lvl2_start)
    for gc in range(4):
        P.dma("pool", wglu_bf[:, gc, :], w_glu[gc * 128:(gc + 1) * 128, :], writes=wglu_bf.s())
    for kc in range(8):
        P.dma("pool", wout_bf[:, kc, :], w_out[kc * 128:(kc + 1) * 128, :], writes=wout_bf.s())
    pHf = psv(6, [128, 6, 128], nb=2)
    Hn2 = carve([128, 6, 128], F32)
    for part in range(2):
        for si in range(3):
            P.op("pe", lambda e, part=part, si=si: e.transpose(out=pHf[0:16, part * 3 + si, :], in_=Hfin[:, part, si, :], identity=identf[:]),
                 Hfin.s() + identf.s(), PS.s(7))
    P.op("dve", lambda e: e.tensor_copy(out=Hn2[0:16], in_=pHf[0:16]), PS.s(range(6, 8)), Hn2.s())
    for si in range(3):
        P.dma("sp", h_re_out[si].rearrange("(pair g2) p -> pair (g2 p)", g2=2), Hn2[0:16, si, :], reads=Hn2.s(), writes=h_re_out.s())
        P.dma("sp", h_im_out[si].rearrange("(pair g2) p -> pair (g2 p)", g2=2), Hn2[0:16, 3 + si, :], reads=Hn2.s(), writes=h_im_out.s())

    G = [carve([128, KC], F32) for _ in range(4)]
    gTv = gT[:].rearrange("p g (k b) -> p g k b", b=8)
    ycnt = 0
    for gb in range(4):
        for b in range(8):
            bank = ycnt % 4
            ycnt += 1
            py = psum[:, bank, 0:KC]
            for j in range(b + 1):
                P.op("pe", lambda e, gb=gb, b=b, j=j, py=py: e.matmul(py, lhsT=Tz[:, gb, b - j, :], rhs=uD[:, gb, j, :], start=(j == 0), stop=False,
                                                                    skip_group_check=True), Tz.s() + uD.s(), PS.s(bank))
            for m in range(4):
                pair = 4 * gb + m
                for ri, Hb in enumerate((Hbr, Hbi)):
                    P.op("pe", lambda e, m=m, pair=pair, ri=ri, Hb=Hb, b=b, bank=bank: e.matmul(
                        psum[32 * m:32 * (m + 1), bank, 0:KC], lhsT=Cint[:, pair, ri, b + 1, :], rhs=Hb[:, pair, :], start=False, stop=(ri == 1),
                        tile_position=(0, 32 * m), skip_group_check=True), Cint.s() + Hb.s(), PS.s(bank))
            if b % 2 == 1:
                b0 = bank - 1
                P.op("act", lambda e, gb=gb, b=b, b0=b0: e.activation(out=gTv[:, gb, :, b - 1:b + 1],
                                                                     in_=psum[:, b0:b0 + 2, 0:KC].rearrange("p j k -> p k j"),
                                                                     func=AF.Gelu_apprx_tanh), PS.s(range(b0, b0 + 2)), gT.s())

    if debug:
        dbg_g = dout("dbg_gT", [128, 4, NTOK], BF16)
        P.dma("sp", dbg_g[:], gT[:], reads=gT.s(), writes=dbg_g.s())
    if stage <= 3:
        P.emit(st)
        return

    P.barrier()
    cur[0] = keep_end
    W_UP_OFF = ARENA_BYTES - 64 * 1024
    w_up_bf = carve([128, 8, 4096], BF16, 8, at=W_UP_OFF)
    bgn = carve([128, 128], F32)
    P.dma("sp", bgn[0:8, :], b_glu[0:1, :].rearrange("a (f p) -> (a f) p", p=128), writes=bgn.s())
    pbg = psv(0, [128, 8])
    P.op("pe", lambda e: e.transpose(out=pbg, in_=bgn[0:8, :], identity=identf[0:8, 0:8]), bgn.s() + identf.s(), PS.s(0))
    bglu = carve([128, 8], F32)
    nbglu = carve([128, 8], F32)
    P.op("dve", lambda e: e.tensor_copy(out=bglu[:], in_=pbg), PS.s(0), bglu.s())
    TS("dve", nbglu[:], bglu[:], -1.0, None, MUL, ALU.bypass, bglu.s(), nbglu.s())
    def make_ln(gd, bd):
        L = {}
        L["g"] = carve([128, 1024], F32)
        L["b"] = carve([128, 1024], F32)
        P.dma("sp", L["g"][:], gd[0:1, :].to_broadcast([128, 1024]), writes=L["g"].s())
        P.dma("sp", L["b"][:], bd[0:1, :].to_broadcast([128, 1024]), writes=L["b"].s())
        L["stats"] = carve([128, 2, 6], F32)
        L["mv"] = carve([128, 2], F32)
        L["rstd"] = carve([128, 1], F32)
        L["nmr"] = carve([128, 1], F32)
        L["ti"] = carve([128, 1], I32)
        L["tn"] = carve([128, 2], F32)
        return L
    LN1 = make_ln(ln1_g, ln1_b)
    xs2 = [carve([128, 1024], F32) for _ in range(2)]
    eg = [carve([128, 4, 128], F32) for _ in range(2)]
    soT = [carve([128, 4, 128], BF16) for _ in range(2)]
    assert cur[0] <= Wb_off, (cur[0], Wb_off)
    cur[0] = Wb_off + 8 * 1024
    y1 = [carve([128, 1024], F32, 2) for _ in range(2)]
    assert cur[0] <= lvl2_start, (cur[0], lvl2_start)
    NFA = 12
    cur[0] = lvl2_start + 16 * 1024
    wdnA_off = cur[0]
    wdnA = carve([128, NFA, 1024], BF16, NFA)
    assert cur[0] <= W_UP_OFF, cur[0]

    def layer_norm(src, dst, L):
        stats, mv, rstd1, nmr, ti, tn = L["stats"], L["mv"], L["rstd"], L["nmr"], L["ti"], L["tn"]
        for hh_ in range(2):
            P.op("dve", lambda e, hh_=hh_: e.bn_stats(out=stats[:, hh_, :], in_=src[:, 512 * hh_:512 * (hh_ + 1)]), src.s(hh_) + stats.s(), stats.s())
        P.op("dve", lambda e: e.bn_aggr(out=mv[:], in_=stats[:].rearrange("p a b -> p (a b)")), stats.s(), mv.s())
        TS("dve", tn[:, 0:1], mv[:, 1:2], 1e-5, None, ADD, ALU.bypass, mv.s(), tn.s())
        P.op("dve", lambda e: e.tensor_single_scalar(out=ti[:], in_=tn[:, 0:1].bitcast(I32), scalar=1, op=ALU.logical_shift_right), tn.s(), ti.s())
        TS("dve", ti[:], ti[:], -1.0, 1597463007.0, MUL, ADD, ti.s(), ti.s())
        for it in range(2):
            ysrc = ti[:].bitcast(F32) if it == 0 else rstd1[:]
            TT("dve", tn[:, 1:2], ysrc, ysrc, MUL, ti.s() + rstd1.s(), tn.s())
            TT("dve", tn[:, 1:2], tn[:, 1:2], tn[:, 0:1], MUL, tn.s(), tn.s())
            TS("dve", tn[:, 1:2], tn[:, 1:2], -0.5, 1.5, MUL, ADD, tn.s(), tn.s())
            TT("dve", rstd1[:], ysrc, tn[:, 1:2], MUL, tn.s() + ti.s() + rstd1.s(), rstd1.s())
        STT("dve", nmr[:], mv[:, 0:1], -1.0, rstd1[:], MUL, MUL, mv.s() + rstd1.s(), nmr.s())
        for hh_ in range(2):
            sl = slice(512 * hh_, 512 * (hh_ + 1))
            P.op("act", lambda e, sl=sl: e.activation(out=dst[:, sl], in_=src[:, sl], func=AF.Identity, scale=rstd1[:, 0:1], bias=nmr[:, 0:1]),
                 src.s(hh_) + rstd1.s() + nmr.s() + dst.s(hh_), dst.s(hh_))
            TT("dve", dst[:, sl], dst[:, sl], L["g"][:, sl], MUL, dst.s(hh_) + L["g"].s(), dst.s(hh_))
            TT("dve", dst[:, sl], dst[:, sl], L["b"][:, sl], ADD, dst.s(hh_) + L["b"].s(), dst.s(hh_))

    def c1_G(t):
        b = t % 2
        cols = slice(t * 128, (t + 1) * 128)
        P.dma("sp", xs2[b][:], x_all[t * 128:(t + 1) * 128, :], writes=xs2[b].s())
        zb = 2 * b
        pz_ = psv(zb, [128, 8, 128], nb=2)
        for fb in range(8):
            for gc in range(4):
                P.op("pe", lambda e, fb=fb, gc=gc, cols=cols, pz_=pz_: e.matmul(pz_[:, fb, :], lhsT=wglu_bf[:, gc, fb * 128:(fb + 1) * 128], rhs=gT[:, gc, cols],
                                                                               start=(gc == 0), stop=(gc == 3), skip_group_check=True), wglu_bf.s() + gT.s(), PS.s(range(zb, zb + 2)))
        for f4 in range(4):
            P.op("act", lambda e, f4=f4, b=b, pz_=pz_: e.activation(out=eg[b][:, f4, :], in_=pz_[:, 4 + f4, :], func=AF.Sigmoid, bias=bglu[:, 4 + f4:5 + f4]),
                 PS.s(range(zb, zb + 2)) + bglu.s() + eg[b].s(), eg[b].s())
        for f4 in range(4):
            STT("dve", soT[b][:, f4, :], pz_[:, f4, :], bglu[:, f4:f4 + 1], eg[b][:, f4, :], ADD, MUL, PS.s(range(zb, zb + 2)) + bglu.s() + eg[b].s() + soT[b].s(), soT[b].s())

    def c1_M(t):
        b = t % 2
        cols = slice(t * 128, (t + 1) * 128)
        mb = 4 + 2 * b
        pm_ = psv(mb, [128, 1024], nb=2)
        for hh_ in range(2):
            for kc in range(8):
                lh = attnT[:, kc, cols] if kc < 4 else soT[b][:, kc - 4, :]
                rd = attnT.s(t) if kc < 4 else soT[b].s()
                P.op("pe", lambda e, hh_=hh_, kc=kc, lh=lh, pm_=pm_: e.matmul(pm_[:, 512 * hh_:512 * (hh_ + 1)], lhsT=lh, rhs=wout_bf[:, kc, 512 * hh_:512 * (hh_ + 1)],
                                                                             start=(kc == 0), stop=(kc == 7)), rd + wout_bf.s(), PS.s(mb + hh_))
        for hh_ in range(2):
            STT("dve", y1[b][:, 512 * hh_:512 * (hh_ + 1)], xs2[b][:, 512 * hh_:512 * (hh_ + 1)], ALPHA, pm_[:, 512 * hh_:512 * (hh_ + 1)], MUL, ADD,
                xs2[b].s() + PS.s(mb + hh_) + y1[b].s(hh_), y1[b].s(hh_))
        layer_norm(y1[b], y1[b], LN1)
        P.dma("sp", x1_scr[t * 128:(t + 1) * 128, :], y1[b][:], reads=y1[b].s(), writes=x1_scr.s(t))

    c1_G(0)
    for t in range(NT):
        if t + 1 < NT:
            c1_G(t + 1)
        c1_M(t)
        if t == 1:
            for dc in range(8):
                for hh_ in range(2):
                    P.dma("pool", w_up_bf[:, dc, 2048 * hh_:2048 * (hh_ + 1)], w_up[dc * 128:(dc + 1) * 128, 2048 * hh_:2048 * (hh_ + 1)], writes=w_up_bf.s(dc))
            for fc in range(0, NFA, 2):
                P.dma("pool", wdnA[:, fc:fc + 2, :], w_down[fc * 128:(fc + 2) * 128, :].rearrange("(a p) n -> p a n", p=128), writes=wdnA.s(range(fc, fc + 2)))


    if debug:
        dbg_x1 = dout("dbg_x1", [NTOK, 1024], F32)
        P.dma("sp", dbg_x1[:], x1_scr[:], reads=x1_scr.s(), writes=dbg_x1.s())

    P.barrier(include_dma=("sp",))
    cur[0] = uD_start
    LN2 = make_ln(ln2_g, ln2_b)
    wdnB = carve([128, 32 - NFA, 1024], BF16, 32 - NFA)

    def wdn(fc):
        if fc < NFA:
            return wdnA[:, fc, :], wdnA.s(fc)
        return wdnB[:, fc - NFA, :], wdnB.s(fc - NFA)

    x1g = [carve([128, 2, 1024], F32) for _ in range(2)]
    x1b = [carve([128, 2, 1024], BF16) for _ in range(2)]
    x1T = [carve([128, 8, 256], BF16) for _ in range(2)]
    hT = carve([128, 32, 256], BF16)
    sq = [carve([128, 256], F32) for _ in range(2)]
    y2 = [carve([128, 1024], F32, 2) for _ in range(2)]
    assert cur[0] <= wdnA_off, (cur[0], wdnA_off)
    groups = [(2 * g, 2) for g in range(8)] + [(16, 1)]
    ucnt = [0]
    ycnt2 = [0]

    def c2_X(gi):
        t0, ntl = groups[gi]
        b = gi % 2
        for tl in range(ntl):
            P.dma("sp", x1g[b][:, tl, :], x1_scr[(t0 + tl) * 128:(t0 + tl + 1) * 128, :], reads=x1_scr.s(t0 + tl), writes=x1g[b].s())
            P.dma("pool", x1b[b][:, tl, :], x1_scr[(t0 + tl) * 128:(t0 + tl + 1) * 128, :], reads=x1_scr.s(t0 + tl), writes=x1b[b].s())
        for tl in range(ntl):
            pxT = psv(7, [128, 8, 128], BF16)
            for dc in range(8):
                P.op("pe", lambda e, dc=dc, tl=tl, pxT=pxT, b=b: e.transpose(out=pxT[:, dc, :], in_=x1b[b][:, tl, dc * 128:(dc + 1) * 128], identity=ident[:]),
                     x1b[b].s() + ident.s(), PS.s(7))
            P.op("act", lambda e, b=b, tl=tl, pxT=pxT: e.copy(out=x1T[b][:, :, tl * 128:(tl + 1) * 128], in_=pxT), PS.s(7) + x1T[b].s(), x1T[b].s())

    def c2_U(gi):
        t0, ntl = groups[gi]
        b = gi % 2
        ntok = 128 * ntl
        for fc in range(32):
            bank = ucnt[0] % 3
            ucnt[0] += 1
            ph = psum[:, bank, 0:ntok]
            for dc in range(8):
                P.op("pe", lambda e, fc=fc, dc=dc, b=b, ph=ph, ntok=ntok: e.matmul(ph, lhsT=w_up_bf[:, dc, fc * 128:(fc + 1) * 128], rhs=x1T[b][:, dc, 0:ntok],
                                                                                  start=(dc == 0), stop=(dc == 7)), w_up_bf.s(dc) + x1T[b].s(), PS.s(bank))
            sqb = sq[fc % 2]
            P.op("act", lambda e, sqb=sqb, ph=ph, ntok=ntok: e.activation(out=sqb[:, 0:ntok], in_=ph, func=AF.Square), PS.s(bank), sqb.s())
            STT("dve", hT[:, fc, 0:ntok], ph, 0.0, sqb[:, 0:ntok], ALU.is_gt, MUL, PS.s(bank) + sqb.s() + hT.s(), hT.s())

    pend_ln = []

    def c2_D(gi):
        t0, ntl = groups[gi]
        b = gi % 2
        while pend_ln:
            pend_ln.pop(0)()
        for tl in range(ntl):
            t = t0 + tl
            yb = y2[ycnt2[0] % 2]
            ycnt2[0] += 1
            for hh_ in range(2):
                bank = 3 + 2 * tl + hh_
                pd = psv(bank, [128, 512])
                for fc in range(32):
                    wv, wr = wdn(fc)
                    P.op("pe", lambda e, fc=fc, tl=tl, hh_=hh_, pd=pd, wv=wv: e.matmul(pd, lhsT=hT[:, fc, tl * 128:(tl + 1) * 128], rhs=wv[:, 512 * hh_:512 * (hh_ + 1)],
                                                                                       start=(fc == 0), stop=(fc == 31)), hT.s() + wr, PS.s(bank))
                STT("dve", yb[:, 512 * hh_:512 * (hh_ + 1)], x1g[b][:, tl, 512 * hh_:512 * (hh_ + 1)], ALPHA, pd, MUL, ADD,
                    x1g[b].s() + PS.s(bank) + yb.s(hh_), yb.s(hh_))
            def tail(yb=yb, t=t):
                layer_norm(yb, yb, LN2)
                P.dma("sp", y_out[t * 128:(t + 1) * 128, :], yb[:], reads=yb.s(), writes=y_out.s())
            pend_ln.append(tail)

    c2_X(0)
    for fc in range(NFA, 32, 2):
        P.dma("pool", wdnB[:, fc - NFA:fc - NFA + 2, :], w_down[fc * 128:(fc + 2) * 128, :].rearrange("(a p) n -> p a n", p=128),
              writes=wdnB.s(range(fc - NFA, fc - NFA + 2)))
    for gi in range(len(groups)):
        c2_U(gi)
        if gi + 1 < len(groups):
            c2_X(gi + 1)
        c2_D(gi)
    while pend_ln:
        pend_ln.pop(0)()

    P.emit(st)


def make_in_maps(inp):
    f = lambda a: np.ascontiguousarray(a, dtype=np.float32)
    maps = []
    lam_p = np.stack([inp["lambda_q1"][0], inp["lambda_k1"][0], inp["lambda_q2"][0], inp["lambda_k2"][0]])[None]
    for c in range(8):
        xa = np.concatenate([inp["x_prompt"][c], inp["x_sample"][2 * c], inp["x_sample"][2 * c + 1]], axis=0)
        m = {
            "x_all": f(xa),
            "cache_k": f(inp["cache_k"][0, 2 * c:2 * c + 2].reshape(2, 1024, 512)),
            "cache_v": f(inp["cache_v"][0, 2 * c:2 * c + 2].reshape(2, 1024, 512)),
            "st_re": f(inp["state_ssm_re"][0, 2 * c:2 * c + 2]),
            "st_im": f(inp["state_ssm_im"][0, 2 * c:2 * c + 2]),
            "w_in": f(inp["w_in"][0]),
            "lam_p": f(lam_p),
            "subln_g": f(inp["subln_g"]),
            "a_re": f(inp["ssm_a_re"][0]),
            "a_im": f(inp["ssm_a_im"][0]),
            "log_dt": f(inp["ssm_log_dt"]),
            "b_re": f(inp["ssm_b_re"][0]),
            "b_im": f(inp["ssm_b_im"][0]),
            "c_re": f(inp["ssm_c_re"][0]),
            "c_im": f(inp["ssm_c_im"][0]),
            "ssm_d": f(inp["ssm_d"][0]),
            "w_glu": f(inp["w_glu"][0]),
            "b_glu": f(inp["b_glu"]),
            "w_out": f(inp["w_out"][0]),
            "ln1_g": f(inp["ln1_g"]),
            "ln1_b": f(inp["ln1_b"]),
            "w_up": f(inp["w_up"][0]),
            "w_down": f(inp["w_down"][0]),
            "ln2_g": f(inp["ln2_g"]),
            "ln2_b": f(inp["ln2_b"]),
        }
        maps.append(m)
    return maps


_NC_CACHE = {}


def kernel(**inp):
    inp = {k: np.asarray(v) for k, v in inp.items()}
    if "nc" not in _NC_CACHE:
        _NC_CACHE["nc"] = build_nc()
    nc = _NC_CACHE["nc"]
    maps = make_in_maps(inp)
    res = run_bass_kernel_spmd(nc, maps, core_ids=list(range(8)))
    R = res.results
    y_prompt = np.stack([R[c]["y_out"][:2048] for c in range(8)])
    y_sample = np.stack([R[c]["y_out"][2048 + 64 * s:2048 + 64 * (s + 1)] for c in range(8) for s in range(2)])
    k_prompt = np.stack([R[c]["k_out"][:2048].reshape(2048, 8, 64) for c in range(8)])[None]
    v_prompt = np.stack([R[c]["v_out"][:2048].reshape(2048, 8, 64) for c in range(8)])[None]
    k_sample = np.stack([R[c]["k_out"][2048 + 64 * s:2048 + 64 * (s + 1)].reshape(64, 8, 64) for c in range(8) for s in range(2)])[None]
    v_sample = np.stack([R[c]["v_out"][2048 + 64 * s:2048 + 64 * (s + 1)].reshape(64, 8, 64) for c in range(8) for s in range(2)])[None]
    re_p = np.stack([R[c]["h_re_out"][0] for c in range(8)])[None]
    im_p = np.stack([R[c]["h_im_out"][0] for c in range(8)])[None]
    re_s = np.stack([R[c]["h_re_out"][1 + s] for c in range(8) for s in range(2)])[None]
    im_s = np.stack([R[c]["h_im_out"][1 + s] for c in range(8) for s in range(2)])[None]
    f = lambda a: np.ascontiguousarray(a, dtype=np.float32)
    return (f(y_prompt), f(y_sample), f(k_prompt), f(v_prompt), f(re_p), f(im_p),
            f(k_sample), f(v_sample), f(re_s), f(im_s))
```

```python
import math
from contextlib import ExitStack
import numpy as np
import concourse.bass as bass
import concourse.mybir as mybir
from concourse.bass_utils import run_bass_kernel_spmd

F32 = mybir.dt.float32
BF16 = mybir.dt.bfloat16
I32 = mybir.dt.int32
U8 = mybir.dt.uint8
ALU = mybir.AluOpType
AF = mybir.ActivationFunctionType
AX = mybir.AxisListType

ENG = ("pe", "act", "dve", "pool", "sp")

NT = 17
NTOK = NT * 128
KC = 272
LAM_INIT = 0.8 - 0.6 * math.exp(-0.3 * 0)
ALPHA = 2.0 ** 0.25
ROPE_INV = [500000.0 ** (-(2 * i) / 8.0) for i in range(4)]
TWO_PI = 2.0 * math.pi


class Res:
    __slots__ = ("w", "rd")

    def __init__(self):
        self.w = None
        self.rd = []


class Buf:
    def __init__(self, t, nslots=1):
        self.t = t
        self.res = [Res() for _ in range(nslots)]

    def __getitem__(self, k):
        return self.t[k]

    def s(self, slot=None):
        if slot is None:
            return list(self.res)
        if isinstance(slot, (list, tuple, range)):
            return [self.res[i] for i in slot]
        return [self.res[slot]]


class Op:
    __slots__ = ("eng", "fn", "deps", "signal", "semval", "dma", "dsem", "dval", "prev_same_sem")

    def __init__(self, eng, fn, dma):
        self.eng = eng
        self.fn = fn
        self.deps = []
        self.signal = False
        self.semval = None
        self.dma = dma
        self.dsem = None
        self.dval = None
        self.prev_same_sem = None


class Prog:
    def __init__(self, nc, n_dma_sems=(12, 4, 40)):
        self.nc = nc
        self.ops = {e: [] for e in ENG}
        self.n_dma = {"sp": n_dma_sems[0], "act": n_dma_sems[1], "pool": n_dma_sems[2]}
        self.dma_cnt = {"sp": 0, "act": 0, "pool": 0}
        self.dma_last = {}
        self.dma_uses = {}
        self.bar = {e: [] for e in ENG}

    def _add(self, eng, fn, reads, writes, dma=False):
        op = Op(eng, fn, dma)
        deps = list(self.bar[eng])
        self.bar[eng] = []
        for r in reads:
            if r.w is not None:
                deps.append(r.w)
        for w in writes:
            if w.w is not None:
                deps.append(w.w)
            deps.extend(w.rd)
        seen = set()
        for d in deps:
            if d is op or id(d) in seen:
                continue
            seen.add(id(d))
            if (not d.dma) and d.eng == "pe" and eng == "pe" and not dma:
                continue
            op.deps.append(d)
            d.signal = True
        for r in reads:
            r.rd.append(op)
        for w in writes:
            w.w = op
            w.rd = []
        self.ops[eng].append(op)
        if dma:
            k = self.dma_cnt[eng] % self.n_dma[eng]
            self.dma_cnt[eng] += 1
            key = (eng, k)
            op.prev_same_sem = self.dma_last.get(key)
            self.dma_uses[key] = self.dma_uses.get(key, 0) + 1
            op.dsem = key
            op.dval = 16 * self.dma_uses[key]
            self.dma_last[key] = op
        return op

    rec = None

    def op(self, eng, fn, reads=(), writes=()):
        if self.rec is not None:
            self.rec.append((eng, fn, list(reads), list(writes), False))
            return None
        return self._add(eng, fn, list(reads), list(writes), False)

    def dma(self, eng, out, in_, reads=(), writes=(), **kw):
        def fn(e):
            return e.dma_start(out=out, in_=in_, **kw)
        if self.rec is not None:
            self.rec.append((eng, fn, list(reads), list(writes), True))
            return None
        return self._add(eng, fn, list(reads), list(writes), True)

    def flush(self, lst, n):
        for _ in range(min(n, len(lst))):
            self._add(*lst.pop(0))

    def barrier(self, include_dma=True):
        lasts = []
        for e in ENG:
            if self.ops[e]:
                lasts.append(self.ops[e][-1])
        if include_dma:
            for key, op in self.dma_last.items():
                if include_dma is True or key[0] in include_dma:
                    lasts.append(op)
        for e in ENG:
            self.bar[e] = list(lasts)

    def emit(self, stack):
        nc = self.nc
        sems = {e: stack.enter_context(nc.semaphore("s_" + e)) for e in ENG}
        dsems = {}
        for e, n in self.n_dma.items():
            for k in range(n):
                dsems[(e, k)] = stack.enter_context(nc.semaphore("d_%s%d" % (e, k)))
        for e in ENG:
            c = 0
            for op in self.ops[e]:
                if op.dma:
                    continue
                if op.signal:
                    c += 1
                    op.semval = c
        block = stack.enter_context(nc.Block())
        prog = self

        def body(ename):
            def f(eng):
                waited = {}

                def wait(sem, key, val):
                    if waited.get(key, 0) >= val:
                        return
                    waited[key] = val
                    eng.wait_ge(sem, val)

                for op in prog.ops[ename]:
                    for d in op.deps:
                        if d.dma:
                            wait(dsems[d.dsem], d.dsem, d.dval)
                        else:
                            wait(sems[d.eng], d.eng, d.semval)
                    if op.dma:
                        p = op.prev_same_sem
                        if p is not None:
                            wait(dsems[p.dsem], p.dsem, p.dval)
                        inst = op.fn(eng)
                        inst.then_inc(dsems[op.dsem], 16)
                    else:
                        inst = op.fn(eng)
                        if op.signal:
                            inst.then_inc(sems[ename], 1)
                if ename == "sp":
                    for key, last in prog.dma_last.items():
                        wait(dsems[key], key, last.dval)
            return f

        block.tensor(body("pe"))
        block.scalar(body("act"))
        block.vector(body("dve"))
        block.gpsimd(body("pool"))
        block.sync(body("sp"))


class K:
    pass


def build_nc(stage=99, debug=False):
    nc = bass.Bass("TRN2", target_bir_lowering=False)
    st = ExitStack()
    with st:
        _build(nc, st, stage, debug)
    return nc


def _build(nc, st, stage, debug):
    P = Prog(nc)
    D = {}

    def din(name, shape, dt=F32):
        D[name] = Buf(nc.dram_tensor(name, list(shape), dt, kind="ExternalInput").ap())
        return D[name]

    def dout(name, shape, dt=F32):
        D[name] = Buf(nc.dram_tensor(name, list(shape), dt, kind="ExternalOutput").ap())
        return D[name]

    x_all = din("x_all", [NTOK, 1024])
    cache_k = din("cache_k", [2, 1024, 512])
    cache_v = din("cache_v", [2, 1024, 512])
    st_re = din("st_re", [2, 32, 64])
    st_im = din("st_im", [2, 32, 64])
    w_in = din("w_in", [1024, 2048])
    lam_p = din("lam_p", [1, 4, 32])
    subln_g = din("subln_g", [1, 64])
    a_re = din("a_re", [32, 64])
    a_im = din("a_im", [32, 64])
    log_dt = din("log_dt", [1, 32])
    b_re = din("b_re", [32, 64, 16])
    b_im = din("b_im", [32, 64, 16])
    c_re = din("c_re", [32, 16, 64])
    c_im = din("c_im", [32, 16, 64])
    ssm_d = din("ssm_d", [32, 16])
    w_glu = din("w_glu", [512, 1024])
    b_glu = din("b_glu", [1, 1024])
    w_out = din("w_out", [1024, 1024])
    ln1_g = din("ln1_g", [1, 1024])
    ln1_b = din("ln1_b", [1, 1024])
    w_up = din("w_up", [1024, 4096])
    w_down = din("w_down", [4096, 1024])
    ln2_g = din("ln2_g", [1, 1024])
    ln2_b = din("ln2_b", [1, 1024])

    y_out = dout("y_out", [NTOK, 1024])
    k_out = dout("k_out", [NTOK, 512])
    v_out = dout("v_out", [NTOK, 512])
    h_re_out = dout("h_re_out", [3, 32, 64])
    h_im_out = dout("h_im_out", [3, 32, 64])
    x1_scr = Buf(nc.dram_tensor("x1_scr", [NTOK, 1024], F32, kind="Internal").ap(), NT)

    ARENA_BYTES = 206 * 1024
    arena = st.enter_context(nc.sbuf_tensor("arena", [128, ARENA_BYTES], U8))
    psum = st.enter_context(nc.psum_tensor("psum", [128, 8, 512], F32))
    PS = Buf(psum, 8)

    cur = [0]

    def carve(shape, dt, nslots=1, at=None):
        esz = {F32: 4, BF16: 2, I32: 4, U8: 1}[dt]
        n = 1
        for d in shape[1:]:
            n *= d
        nb = n * esz
        nb_al = (nb + 31) // 32 * 32
        if at is None:
            off = cur[0]
            cur[0] += nb_al
        else:
            off = at
        assert off + nb <= ARENA_BYTES, ("arena overflow", off, nb)
        v = arena[:, off:off + nb]
        if dt != U8:
            v = v.bitcast(dt)
        if len(shape) > 2:
            names = " ".join("d%d" % i for i in range(1, len(shape)))
            kw = {"d%d" % i: shape[i] for i in range(2, len(shape))}
            v = v.rearrange("p (%s) -> p %s" % (names, names), **kw)
        return Buf(v, nslots)

    def psv(bank, shape, dt=F32, nb=1):
        if nb == 1:
            v = psum[:, bank, :]
        else:
            v = psum[:, bank:bank + nb, :].rearrange("p a b -> p (a b)")
        if dt != F32:
            v = v.bitcast(dt)
        n = 1
        for d in shape[1:]:
            n *= d
        v = v[:, 0:n]
        if len(shape) > 2:
            names = " ".join("d%d" % i for i in range(1, len(shape)))
            kw = {"d%d" % i: shape[i] for i in range(2, len(shape))}
            v = v.rearrange("p (%s) -> p %s" % (names, names), **kw)
        return v

    identf = carve([128, 128], F32)
    ident = carve([128, 128], BF16)
    P.op("pool", lambda e: e.memset(identf[:], 1.0), [], identf.s())
    P.op("pool", lambda e: e.affine_select(out=identf[:], in_=identf[:], pattern=[[-1, 128]], base=0,
                                           channel_multiplier=1, compare_op=ALU.is_equal, fill=0.0),
         identf.s(), identf.s())
    P.op("dve", lambda e: e.tensor_copy(out=ident[:], in_=identf[:]), identf.s(), ident.s())
    ones_bf = carve([128, 128], BF16)
    P.op("pool", lambda e: e.memset(ones_bf[:], 1.0), [], ones_bf.s())
    ones_f = carve([128, 128], F32)
    P.op("pool", lambda e: e.memset(ones_f[:], 1.0), [], ones_f.s())

    def sincos(ang, n, sin_out, cos_out, eng="dve"):
        tmp = carve([128, n], F32)
        ki = carve([128, n], I32)
        kf = carve([128, n], F32)
        r = carve([128, n], F32)
        C1 = 6.28125
        C2 = TWO_PI - C1
        for which, outb, shift in (("s", sin_out, 0.0), ("c", cos_out, math.pi / 2)):
            P.op(eng, lambda e, shift=shift: e.tensor_scalar(out=tmp[:], in0=ang[:], scalar1=shift, scalar2=1.0 / TWO_PI,
                                                             op0=ALU.add, op1=ALU.mult), ang.s(), tmp.s())
            P.op(eng, lambda e: e.tensor_copy(out=ki[:], in_=tmp[:]), tmp.s(), ki.s())
            P.op(eng, lambda e: e.tensor_copy(out=kf[:], in_=ki[:]), ki.s(), kf.s())
            P.op(eng, lambda e, shift=shift: e.tensor_scalar(out=tmp[:], in0=ang[:], scalar1=shift, scalar2=None,
                                                             op0=ALU.add), ang.s(), tmp.s())
            P.op(eng, lambda e: e.scalar_tensor_tensor(out=r[:], in0=kf[:], scalar=-C1, in1=tmp[:],
                                                       op0=ALU.mult, op1=ALU.add), kf.s() + tmp.s(), r.s())
            P.op(eng, lambda e: e.scalar_tensor_tensor(out=tmp[:], in0=kf[:], scalar=-C2, in1=r[:],
                                                       op0=ALU.mult, op1=ALU.add), kf.s() + r.s(), tmp.s())
            P.op(eng, lambda e: e.tensor_scalar(out=r[:], in0=tmp[:], scalar1=-math.pi, scalar2=math.pi,
                                                op0=ALU.max, op1=ALU.min), tmp.s(), r.s())
            P.op("act", lambda e, outb=outb: e.activation(out=outb[:], in_=r[:], func=AF.Sin), r.s(), outb.s())

    posi = carve([128, NT], I32)
    posf = carve([128, NT], F32)
    P.op("pool", lambda e: e.iota(posi[:], pattern=[[128, NT]], base=0, channel_multiplier=1), [], posi.s())
    P.op("pool", lambda e: e.iota(posi[0:64, 16:17], pattern=[[0, 1]], base=1024, channel_multiplier=1), posi.s(), posi.s())
    P.op("pool", lambda e: e.iota(posi[64:128, 16:17], pattern=[[0, 1]], base=1024, channel_multiplier=1), posi.s(), posi.s())
    P.op("dve", lambda e: e.tensor_copy(out=posf[:], in_=posi[:]), posi.s(), posf.s())
    rang = carve([128, NT, 4], F32)
    for i in range(4):
        P.op("dve", lambda e, i=i: e.tensor_scalar(out=rang[:, :, i], in0=posf[:], scalar1=float(ROPE_INV[i]), scalar2=None,
                                                   op0=ALU.mult), posf.s(), rang.s())
    rsin = carve([128, NT, 4], F32)
    rcos = carve([128, NT, 4], F32)
    rang2 = Buf(rang[:].rearrange("p a b -> p (a b)"))
    rang2.res = rang.res
    rsin2 = Buf(rsin[:].rearrange("p a b -> p (a b)")); rsin2.res = rsin.res
    rcos2 = Buf(rcos[:].rearrange("p a b -> p (a b)")); rcos2.res = rcos.res
    sincos(rang2, NT * 4, rsin2, rcos2)

    uD_start = cur[0]
    uD = carve([128, 4, 8, KC], BF16, NT)
    attnT = carve([128, 4, NTOK], BF16, NT)
    gT = carve([128, 4, NTOK], BF16, 1)
    keep_end = cur[0]
    qT = carve([128, 4, NTOK], BF16, NT)
    kT = carve([128, 4, NTOK], BF16, NT)
    Vaug = carve([128, NT, 8, 65], BF16, NT)
    kTc = carve([128, 4, 2048], BF16, 16)
    Vaugc = carve([128, 16, 8, 65], BF16, 16)
    pers_end = cur[0]

    P.op("pool", lambda e: e.memset(Vaug[:, :, :, 64:65], 1.0), [], Vaug.s())
    P.op("pool", lambda e: e.memset(Vaugc[:, :, :, 64:65], 1.0), [], Vaugc.s())

    w_in_bf = carve([128, 8, 2048], BF16, 32)
    xb = [carve([128, 1024], BF16) for _ in range(3)]
    xT = [carve([128, 8, 128], BF16) for _ in range(3)]
    qf = [carve([128, 512], F32) for _ in range(2)] * 2
    kf = [carve([128, 512], F32) for _ in range(2)] * 2
    vf = [carve([128, 512], F32) for _ in range(2)] * 2
    qb = [carve([128, 512], BF16) for _ in range(3)]
    kb = [carve([128, 512], BF16) for _ in range(3)]
    rt = [carve([128, 16, 4], F32) for _ in range(4)]

    def rope(buf, tile):
        v = buf[:].rearrange("p (h d) -> p h d", d=32)
        x1 = v[:, :, 0:4]
        x2 = v[:, :, 4:8]
        cs = rcos[:, tile, :].unsqueeze(1).to_broadcast([128, 16, 4])
        sn = rsin[:, tile, :].unsqueeze(1).to_broadcast([128, 16, 4])
        rd = buf.s() + rcos.s() + rsin.s()
        P.op("dve", lambda e: e.tensor_tensor(out=rt[0][:], in0=x1, in1=cs, op=ALU.mult), rd, rt[0].s())
        P.op("dve", lambda e: e.tensor_tensor(out=rt[1][:], in0=x2, in1=sn, op=ALU.mult), rd, rt[1].s())
        P.op("dve", lambda e: e.tensor_tensor(out=rt[2][:], in0=x2, in1=cs, op=ALU.mult), rd, rt[2].s())
        P.op("dve", lambda e: e.tensor_tensor(out=rt[3][:], in0=x1, in1=sn, op=ALU.mult), rd, rt[3].s())
        P.op("dve", lambda e: e.tensor_tensor(out=x1, in0=rt[0][:], in1=rt[1][:], op=ALU.subtract),
             rt[0].s() + rt[1].s() + rt[2].s() + rt[3].s(), buf.s())
        P.op("dve", lambda e: e.tensor_tensor(out=x2, in0=rt[2][:], in1=rt[3][:], op=ALU.add),
             rt[2].s() + rt[3].s(), buf.s())

    def transpose4(src, dstT, col0, bank, ncols=128):
        pv = psv(bank, [128, 4, 128], BF16)
        for blk in range(4):
            P.op("pe", lambda e, blk=blk: e.transpose(out=pv[:, blk, :], in_=src[:, blk * 128:(blk + 1) * 128], identity=ident[:]),
                 src.s() + ident.s(), PS.s(bank))
        return pv

    NB = 3

    def stageL(t):
        b = t % NB
        P.dma("pool", xb[b][:], x_all[t * 128:(t + 1) * 128, :], writes=xb[b].s())

    def stageX(t):
        b = t % NB
        bank = t % 2
        pxT = psv(bank, [128, 8, 128], BF16)
        for dc in range(8):
            P.op("pe", lambda e, dc=dc, b=b: e.transpose(out=pxT[:, dc, :], in_=xb[b][:, dc * 128:(dc + 1) * 128], identity=ident[:]),
                 xb[b].s() + ident.s(), PS.s(bank))
        P.op("act", lambda e, b=b: e.copy(out=xT[b][:], in_=pxT), PS.s(bank), xT[b].s())

    def stageM(t):
        b = t % NB
        b2 = t % 2
        for j, bank in ((0, 2), (1, 3), (2, 4)):
            pv_ = psv(bank, [128, 512])
            for dc in range(8):
                P.op("pe", lambda e, dc=dc, b=b, j=j, pv_=pv_: e.matmul(pv_, lhsT=xT[b][:, dc, :], rhs=w_in_bf[:, dc, j * 512:(j + 1) * 512],
                                                                       start=(dc == 0), stop=(dc == 7)),
                     xT[b].s() + w_in_bf.s(j * 8 + dc), PS.s(bank))
        pu = psv(5, [128, 4, 128])
        for gb in range(4):
            for dc in range(8):
                P.op("pe", lambda e, dc=dc, b=b, gb=gb: e.matmul(pu[:, gb, :], lhsT=w_in_bf[:, dc, 1536 + gb * 128:1536 + (gb + 1) * 128],
                                                                 rhs=xT[b][:, dc, :], start=(dc == 0), stop=(dc == 7)),
                     xT[b].s() + w_in_bf.s(24 + dc), PS.s(5))
        pq, pk, pvv = psv(2, [128, 512]), psv(3, [128, 512]), psv(4, [128, 512])
        P.op("act", lambda e, b2=b2: e.copy(out=qf[b2][:], in_=pq), PS.s(2), qf[b2].s())
        P.op("act", lambda e, b2=b2: e.copy(out=kf[b2][:], in_=pk), PS.s(3), kf[b2].s())
        P.op("dve", lambda e, b2=b2: e.tensor_copy(out=vf[b2][:], in_=pvv), PS.s(4), vf[b2].s())
        P.op("dve", lambda e, t=t: e.tensor_copy(out=uD[:, :, :, 16 * t:16 * (t + 1)],
                                                 in_=pu.rearrange("p g (k j) -> p g j k", j=8)), PS.s(5), uD.s(t))
        rope(qf[b2], t)
        rope(kf[b2], t)
        P.dma("sp", k_out[t * 128:(t + 1) * 128, :], kf[b2][:], reads=kf[b2].s(), writes=k_out.s())
        P.dma("sp", v_out[t * 128:(t + 1) * 128, :], vf[b2][:], reads=vf[b2].s(), writes=v_out.s())
        P.op("pool", lambda e, b=b, b2=b2: e.tensor_copy(out=qb[b][:], in_=qf[b2][:]), qf[b2].s(), qb[b].s())
        P.op("act", lambda e, b=b, b2=b2: e.copy(out=kb[b][:], in_=kf[b2][:]), kf[b2].s(), kb[b].s())
        P.op("pool", lambda e, b2=b2, t=t: e.tensor_copy(out=Vaug[:, t, :, 0:64], in_=vf[b2][:].rearrange("p (h d) -> p h d", d=64)),
             vf[b2].s(), Vaug.s(t))

    def stageT(t):
        b = t % NB
        pqT = transpose4(qb[b], qT, t * 128, 6)
        P.op("dve", lambda e, t=t, pqT=pqT: e.tensor_copy(out=qT[:, :, t * 128:(t + 1) * 128], in_=pqT), PS.s(6), qT.s(t))
        pkT = transpose4(kb[b], kT, t * 128, 7)
        P.op("act", lambda e, t=t, pkT=pkT: e.copy(out=kT[:, :, t * 128:(t + 1) * 128], in_=pkT), PS.s(7), kT.s(t))

    stageL(0)
    stageL(1)
    stageL(2)
    for j in range(4):
        for dc in range(0, 8, 2):
            P.dma("pool", w_in_bf[:, dc:dc + 2, j * 512:(j + 1) * 512],
                  w_in[dc * 128:(dc + 2) * 128, j * 512:(j + 1) * 512].rearrange("(a p) n -> p a n", p=128),
                  writes=w_in_bf.s(range(j * 8 + dc, j * 8 + dc + 2)))
    stageX(0)
    for t in range(NT):
        if t >= 1 and t + 2 < NT:
            stageL(t + 2)
        if t + 1 < NT:
            stageX(t + 1)
        stageM(t)
        if t >= 1:
            stageT(t - 1)
    stageT(NT - 1)

    if stage <= 1:
        P.emit(st)
        return

    rr = [0]

    def alt():
        rr[0] += 1
        return "dve" if rr[0] % 2 else "pool"

    def TT(eng, o, a, b, op, rd, wr):
        P.op(eng, lambda e: e.tensor_tensor(out=o, in0=a, in1=b, op=op), rd, wr)

    def TS(eng, o, a, s1, s2, op0, op1, rd, wr):
        P.op(eng, lambda e: e.tensor_scalar(out=o, in0=a, scalar1=s1, scalar2=s2, op0=op0, op1=op1), rd, wr)

    def STT(eng, o, a, sc, b, op0, op1, rd, wr):
        P.op(eng, lambda e: e.scalar_tensor_tensor(out=o, in0=a, scalar=sc, in1=b, op0=op0, op1=op1), rd, wr)

    MUL, ADD, SUB = ALU.mult, ALU.add, ALU.subtract
    P.barrier()
    cur[0] = pers_end
    lamb = carve([128, 128], F32)
    P.dma("sp", lamb[:], lam_p[0:1, :, :].rearrange("a b c -> a (b c)").to_broadcast([128, 128]), writes=lamb.s())
    lt = carve([128, 64], F32)
    ls = carve([128, 2], F32)
    le = carve([128, 2], F32)
    neglam = carve([128, 1], F32)
    P.op("dve", lambda e: e.tensor_tensor(out=lt[:].rearrange("p (a b) -> p a b", a=2), in0=lamb[:].rearrange("p (a k b) -> p a k b", a=2, k=2)[:, :, 0, :],
                                          in1=lamb[:].rearrange("p (a k b) -> p a k b", a=2, k=2)[:, :, 1, :], op=ALU.mult), lamb.s(), lt.s())
    P.op("dve", lambda e: e.tensor_reduce(out=ls[:], in_=lt[:].rearrange("p (a b) -> p a b", a=2), op=ALU.add, axis=AX.X), lt.s(), ls.s())
    P.op("act", lambda e: e.activation(out=le[:], in_=ls[:], func=AF.Exp), ls.s(), le.s())
    P.op("dve", lambda e: e.scalar_tensor_tensor(out=neglam[:], in0=le[:, 1:2], scalar=-LAM_INIT, in1=le[:, 0:1],
                                                 op0=ALU.add, op1=ALU.subtract), le.s(), neglam.s())
    gsub = carve([128, 64], F32)
    P.dma("sp", gsub[:], subln_g[0:1, :].to_broadcast([128, 64]), writes=gsub.s())
    P.op("dve", lambda e: e.tensor_scalar(out=gsub[:], in0=gsub[:], scalar1=1.0 - LAM_INIT, scalar2=None, op0=ALU.mult), gsub.s(), gsub.s())
    mhalf = carve([128, 8], F32)
    P.op("pool", lambda e: e.memset(mhalf[:], -0.5), [], mhalf.s())
    Vs1 = carve([64, 8, 65], BF16)
    P.dma("sp", Vs1[0:64], Vaug[64:128, 16, :, :], reads=Vaug.s(16), writes=Vs1.s())

    TOP_OFF = ARENA_BYTES - 21 * 1024
    assert cur[0] <= TOP_OFF, cur[0]
    b_cur = cur[0]
    cur[0] = TOP_OFF
    prm = carve([128, 3, 16], F32)
    h0 = carve([128, 4, 16], F32)
    Dcol = carve([128, 4], F32)
    sm = carve([128, 24, 16], F32)
    angb = carve([128, 16], F32)
    snb = carve([128, 16], F32)
    csb = carve([128, 16], F32)
    PWr = carve([128, 9, 16], F32)
    PWi = carve([128, 9, 16], F32)
    P.rec = []
    An = carve([128, 3, 128], F32)
    ldn = carve([128, 2], F32)
    P.dma("sp", An[0:16, 0, :], a_re[:, :].rearrange("(pair g2) p -> pair (g2 p)", g2=2), writes=An.s())
    P.dma("sp", An[0:16, 1, :], a_im[:, :].rearrange("(pair g2) p -> pair (g2 p)", g2=2), writes=An.s())
    P.dma("sp", ldn[0:16, :], log_dt[0:1, :].rearrange("a (pair g2) -> (a pair) g2", g2=2), writes=ldn.s())
    P.op("dve", lambda e: e.tensor_copy(out=An[0:16, 2, :].rearrange("p (g q) -> p g q", g=2),
                                        in_=ldn[0:16, :].unsqueeze(2).to_broadcast([16, 2, 64])), ldn.s() + An.s(), An.s())
    pA = psv(7, [128, 3, 16])
    for i in range(3):
        P.op("pe", lambda e, i=i: e.transpose(out=pA[:, i, :], in_=An[0:16, i, :], identity=identf[0:16, 0:16]), An.s() + identf.s(), PS.s(7))
    P.op("dve", lambda e: e.tensor_copy(out=prm[:], in_=pA), PS.s(7), prm.s())
    are, aim, ldt = prm[:, 0, :], prm[:, 1, :], prm[:, 2, :]
    Hn = carve([128, 4, 128], F32)
    for s in range(2):
        P.dma("sp", Hn[0:16, 2 * s, :], st_re[s].rearrange("(pair g2) p -> pair (g2 p)", g2=2), writes=Hn.s())
        P.dma("sp", Hn[0:16, 2 * s + 1, :], st_im[s].rearrange("(pair g2) p -> pair (g2 p)", g2=2), writes=Hn.s())
    pH0 = psv(7, [128, 4, 16])
    for i in range(4):
        P.op("pe", lambda e, i=i: e.transpose(out=pH0[:, i, :], in_=Hn[0:16, i, :], identity=identf[0:16, 0:16]), Hn.s() + identf.s(), PS.s(7))
    P.op("dve", lambda e: e.tensor_copy(out=h0[:], in_=pH0), PS.s(7), h0.s())
    Bre = carve([128, 16, 16], F32)
    Bim = carve([128, 16, 16], F32)
    for g2 in range(2):
        P.dma("sp", Bre[64 * g2:64 * (g2 + 1), :, :], b_re[:].rearrange("(pair g2) p c -> g2 p pair c", g2=2)[g2], writes=Bre.s())
        P.dma("sp", Bim[64 * g2:64 * (g2 + 1), :, :], b_im[:].rearrange("(pair g2) p c -> g2 p pair c", g2=2)[g2], writes=Bim.s())
    Cn = carve([128, 2, 4, 64], F32)
    P.dma("sp", Cn[:, 0, :, :], c_re[:].rearrange("(gb gl) c p -> (gl c) gb p", gb=4), writes=Cn.s())
    P.dma("sp", Cn[:, 1, :, :], c_im[:].rearrange("(gb gl) c p -> (gl c) gb p", gb=4), writes=Cn.s())
    CT = carve([128, 2, 512], F32)
    for ri in range(2):
        pC = psv(7, [128, 4, 128])
        for gb in range(4):
            P.op("pe", lambda e, gb=gb, ri=ri, pC=pC: e.transpose(out=pC[0:64, gb, :], in_=Cn[:, ri, gb, :], identity=identf[:]),
                 Cn.s() + identf.s(), PS.s(7))
        P.op("dve", lambda e, ri=ri, pC=pC: e.tensor_copy(out=CT[0:64, ri, :], in_=pC[0:64].rearrange("p a b -> p (a b)")), PS.s(7), CT.s())
    Cre = carve([128, 16, 16], F32)
    Cim = carve([128, 16, 16], F32)
    for g2 in range(2):
        P.dma("sp", Cre[64 * g2:64 * (g2 + 1), :, :], CT[0:64, 0, :].rearrange("p (pair g2 c) -> p g2 pair c", g2=2, c=16)[:, g2], reads=CT.s(), writes=Cre.s())
        P.dma("sp", Cim[64 * g2:64 * (g2 + 1), :, :], CT[0:64, 1, :].rearrange("p (pair g2 c) -> p g2 pair c", g2=2, c=16)[:, g2], reads=CT.s(), writes=Cim.s())
    Dn = carve([128, 128], F32)
    P.dma("sp", Dn[0:4, :], ssm_d[:].rearrange("(gb gl) c -> gb (gl c)", gb=4), writes=Dn.s())
    pD = psv(7, [128, 4])
    P.op("pe", lambda e: e.transpose(out=pD, in_=Dn[0:4, :], identity=identf[0:4, 0:4]), Dn.s() + identf.s(), PS.s(7))
    P.op("dve", lambda e: e.tensor_copy(out=Dcol[:], in_=pD), PS.s(7), Dcol.s())

    SMs = sm.s()
    dtt, adt, ang, mag, sn_, cs_, lr, li, den, nr, fre, fim, r8, rr8, er, ei, t1s, t2s = [sm[:, i, :] for i in range(18)]
    P.op("act", lambda e: e.activation(out=dtt, in_=ldt, func=AF.Exp), prm.s(), SMs)
    TT("dve", adt, are, dtt, MUL, prm.s() + SMs, SMs)
    TT("dve", ang, aim, dtt, MUL, prm.s() + SMs, SMs)
    P.op("act", lambda e: e.activation(out=mag, in_=adt, func=AF.Exp), SMs, SMs)
    P.op("act", lambda e: e.activation(out=r8, in_=adt, func=AF.Exp, scale=8.0), SMs, SMs)
    P.op("act", lambda e: e.activation(out=rr8, in_=adt, func=AF.Exp, scale=-8.0), SMs, SMs)
    P.op("dve", lambda e: e.tensor_copy(out=angb[:], in_=ang), SMs, angb.s())
    sincos(angb, 16, snb, csb)
    TT("dve", lr, mag, csb[:], MUL, SMs + csb.s(), SMs)
    TT("dve", li, mag, snb[:], MUL, SMs + snb.s(), SMs)
    TT("dve", t1s, are, are, MUL, prm.s(), SMs)
    TT("dve", t2s, aim, aim, MUL, prm.s(), SMs)
    TT("dve", den, t1s, t2s, ADD, SMs, SMs)
    P.op("dve", lambda e: e.reciprocal(out=den, in_=den), SMs, SMs)
    TS("dve", nr, lr, -1.0, None, ADD, ALU.bypass, SMs, SMs)
    TT("dve", t1s, nr, are, MUL, SMs + prm.s(), SMs)
    TT("dve", t2s, li, aim, MUL, SMs + prm.s(), SMs)
    TT("dve", t1s, t1s, t2s, ADD, SMs, SMs)
    TT("dve", fre, t1s, den, MUL, SMs, SMs)
    TT("dve", t1s, li, are, MUL, SMs + prm.s(), SMs)
    TT("dve", t2s, nr, aim, MUL, SMs + prm.s(), SMs)
    TT("dve", t1s, t1s, t2s, SUB, SMs, SMs)
    TT("dve", fim, t1s, den, MUL, SMs, SMs)
    PWs = PWr.s() + PWi.s()
    P.op("dve", lambda e: e.memset(PWr[:, 0, :], 1.0), [], PWs)
    P.op("dve", lambda e: e.memset(PWi[:, 0, :], 0.0), [], PWs)
    for m in range(1, 9):
        TT("dve", t1s, PWr[:, m - 1, :], lr, MUL, PWs + SMs, SMs)
        TT("dve", t2s, PWi[:, m - 1, :], li, MUL, PWs + SMs, SMs)
        TT("dve", PWr[:, m, :], t1s, t2s, SUB, SMs, PWs)
        TT("dve", t1s, PWr[:, m - 1, :], li, MUL, PWs + SMs, SMs)
        TT("dve", t2s, PWi[:, m - 1, :], lr, MUL, PWs + SMs, SMs)
        TT("dve", PWi[:, m, :], t1s, t2s, ADD, SMs, PWs)
    TT("dve", er, PWr[:, 8, :], rr8, MUL, PWs + SMs, SMs)
    TT("dve", ei, PWi[:, 8, :], rr8, MUL, PWs + SMs, SMs)

    early_ops = P.rec
    P.rec = None
    hoist = [o for o in early_ops if o[4] and not o[2]]
    early_ops = [o for o in early_ops if not (o[4] and not o[2])]
    for o in hoist:
        P._add(*o)
    top_end = cur[0]
    cur[0] = b_cur
    ckb = [carve([128, 512], BF16) for _ in range(4)]

    def cache_load(idx):
        s, c = idx // 8, idx % 8
        P.dma("pool", ckb[idx % 4][:], cache_k[s, c * 128:(c + 1) * 128, :], writes=ckb[idx % 4].s())
        P.dma("pool", Vaugc[:, idx, :, 0:64], cache_v[s, c * 128:(c + 1) * 128, :].rearrange("p (h d) -> p h d", d=64),
              writes=Vaugc.s(idx))

    def cache_tr(idx):
        pT = transpose4(ckb[idx % 4], kTc, idx * 128, 6)
        P.op("act", lambda e, idx=idx, pT=pT: e.copy(out=kTc[:, :, idx * 128:(idx + 1) * 128], in_=pT), PS.s(6), kTc.s(idx))

    PT = [carve([128, 8, 128], BF16, 2) for _ in range(3)]
    attn_tm = [carve([128, 512], BF16) for _ in range(5)]
    qBD = [carve([128, 4, 512], BF16) for _ in range(2)]
    qBDs = [carve([128, 4, 256], BF16) for _ in range(2)]
    for qb_ in qBD + qBDs:
        P.op("pool", lambda e, qb_=qb_: e.memset(qb_[:], 0.0), [], qb_.s())
    Osb = [carve([128, 16, 65], F32, 2) for _ in range(3)]
    rl = carve([128, 16], F32)
    rl2n = carve([128, 8], F32)
    o1 = carve([128, 8, 64], F32)
    o2 = carve([128, 8, 64], F32)
    osq = o2
    ms = carve([128, 8], F32)
    rstd = carve([128, 8], F32)
    rti = carve([128, 8], I32)
    rtn = carve([128, 8], F32)
    SCALE = 32.0 ** -0.5
    assert cur[0] <= TOP_OFF, (cur[0], TOP_OFF)
    Oall = psv(4, [128, 8, 128], nb=2)

    qtiles = []
    for i in [15, 0, 14, 1, 13, 2, 12, 3, 11, 4, 10, 5, 9, 6, 8, 7]:
        specs = []
        for j in range(i + 1):
            specs.append(dict(nk=128, mask=(j == i), deps=kT.s(j) + Vaug.s(j),
                              kT=(lambda blk, j=j: kT[:, blk, j * 128:(j + 1) * 128]),
                              V=(lambda h, j=j: Vaug[:, j, h, :])))
        qtiles.append(dict(qcol0=i * 128, nq=128, qdeps=qT.s(i), specs=specs, dst=attnT.s(i)))
    for s in range(2):
        specs = []
        for c in range(8):
            idx = 8 * s + c
            specs.append(dict(nk=128, mask=False, deps=kTc.s(idx) + Vaugc.s(idx),
                              kT=(lambda blk, idx=idx: kTc[:, blk, idx * 128:(idx + 1) * 128]),
                              V=(lambda h, idx=idx: Vaugc[:, idx, h, :])))
        if s == 0:
            Vf = (lambda h: Vaug[0:64, 16, h, :])
            vd = Vaug.s(16)
        else:
            Vf = (lambda h: Vs1[0:64, h, :])
            vd = Vs1.s()
        specs.append(dict(nk=64, mask=False, deps=kT.s(16) + vd,
                          kT=(lambda blk, s=s: kT[:, blk, 2048 + 64 * s:2048 + 64 * (s + 1)]), V=Vf))
        qtiles.append(dict(qcol0=2048 + 64 * s, nq=64, qdeps=qT.s(16), specs=specs, dst=attnT.s(16)))

    steps = []
    for qi, qt in enumerate(qtiles):
        for hs in range(2):
            n = len(qt["specs"])
            for ji, ks in enumerate(qt["specs"]):
                steps.append(dict(qi=qi, qt=qt, hs=hs, ji=ji, ks=ks, first=(ji == 0), last=(ji == n - 1)))

    built_q = set()

    def build_q(qi):
        if qi in built_q:
            return
        built_q.add(qi)
        qt = qtiles[qi]
        nq = qt["nq"]
        qb_ = qBD[qi % 2] if nq == 128 else qBDs[qi % 2]
        for r in range(4):
            eng = "pool" if r % 2 else "dve"
            P.op(eng, lambda e, r=r, qb_=qb_, qt=qt, nq=nq: e.tensor_copy(
                out=qb_[32 * r:32 * r + 32, :, r * nq:(r + 1) * nq], in_=qT[32 * r:32 * r + 32, :, qt["qcol0"]:qt["qcol0"] + nq]),
                qt["qdeps"] + qb_.s(), qb_.s())

    def emit_qk(si, sp):
        qt, hs, ks = sp["qt"], sp["hs"], sp["ks"]
        nq, nk, qi = qt["nq"], ks["nk"], sp["qi"]
        qb_ = qBD[qi % 2] if nq == 128 else qBDs[qi % 2]
        build_q(qi)
        if sp["first"] and sp["hs"] == 0 and qi + 1 < len(qtiles):
            build_q(qi + 1)
        sb0 = 2 * (si % 2)
        for bp in range(2):
            blk = 2 * hs + bp
            P.op("pe", lambda e, ks=ks, blk=blk, bp=bp, nk=nk, nq=nq, qb_=qb_, sb0=sb0: e.matmul(
                psum[0:nk, sb0 + bp, 0:4 * nq], lhsT=ks["kT"](blk), rhs=qb_[:, blk, 0:4 * nq], start=True, stop=True),
                qb_.s() + ks["deps"], PS.s(sb0 + bp))
        pt = PT[si % 3]
        sp["pt"] = pt
        for bp in range(2):
            Sv = psum[0:nk, sb0 + bp, 0:4 * nq].rearrange("p (r q) -> p r q", r=4)
            P.op("act", lambda e, pt=pt, nk=nk, nq=nq, Sv=Sv, bp=bp: e.activation(
                out=pt[0:nk, 4 * bp:4 * bp + 4, 0:nq], in_=Sv, func=AF.Exp, scale=SCALE),
                PS.s(sb0 + bp) + pt.s(bp), pt.s(bp))
            if ks["mask"]:
                P.op("pool", lambda e, pt=pt, bp=bp: e.memset(pt[64:128, 4 * bp:4 * bp + 4, 0:64], 0.0), pt.s(bp), pt.s(bp))

    def emit_av(si, sp):
        qt, hs, ks, pt = sp["qt"], sp["hs"], sp["ks"], sp["pt"]
        nq, nk = qt["nq"], ks["nk"]
        for e8 in range(8):
            hh = 8 * hs + e8
            h = hh // 2
            P.op("pe", lambda e, ks=ks, pt=pt, h=h, e8=e8, nk=nk, nq=nq, sp=sp: e.matmul(
                Oall[0:nq, e8, 0:65], lhsT=pt[0:nk, e8, 0:nq], rhs=ks["V"](h),
                start=(sp["first"] and e8 % 4 == 0), stop=sp["last"], skip_group_check=True), pt.s(e8 // 4) + ks["deps"], PS.s(range(4, 6)))
        if sp["last"]:
            finalize(sp)

    fcnt = [0]

    def finalize(sp):
        qt, hs, qi = sp["qt"], sp["hs"], sp["qi"]
        nq = qt["nq"]
        am = attn_tm[qi % 5]
        ob = Osb[qi % 3]
        P.op("act", lambda e, hs=hs: e.copy(out=ob[0:nq, 8 * hs:8 * hs + 8, :], in_=Oall[0:nq, :, 0:65]), PS.s(range(4, 6)) + ob.s(hs), ob.s(hs))
        if hs == 0:
            return
        Ov = ob[:].rearrange("p (h m) d -> p h m d", m=2)
        P.op("dve", lambda e: e.reciprocal(out=rl[0:nq, :], in_=ob[0:nq, :, 64]), ob.s(), rl.s())
        rlv = rl[:].rearrange("p (h m) -> p h m", m=2)
        P.op("dve", lambda e: e.tensor_tensor(out=rl2n[0:nq, :], in0=rlv[0:nq, :, 1], in1=neglam[0:nq, 0:1].to_broadcast([nq, 8]), op=ALU.mult),
             rl.s() + neglam.s(), rl2n.s())
        P.op("dve", lambda e: e.tensor_tensor(out=o1[0:nq], in0=Ov[0:nq, :, 0, 0:64], in1=rlv[0:nq, :, 0].unsqueeze(2).to_broadcast([nq, 8, 64]), op=ALU.mult),
             ob.s() + rl.s(), o1.s())
        P.op("dve", lambda e: e.tensor_tensor(out=o2[0:nq], in0=Ov[0:nq, :, 1, 0:64], in1=rl2n[0:nq, :].unsqueeze(2).to_broadcast([nq, 8, 64]), op=ALU.mult),
             ob.s() + rl2n.s(), o2.s())
        P.op("dve", lambda e: e.tensor_tensor(out=o1[0:nq], in0=o1[0:nq], in1=o2[0:nq], op=ALU.add), o1.s() + o2.s(), o1.s())
        P.op("dve", lambda e: e.tensor_tensor(out=osq[0:nq], in0=o1[0:nq], in1=o1[0:nq], op=ALU.mult), o1.s(), osq.s())
        P.op("dve", lambda e: e.tensor_reduce(out=ms[0:nq, :], in_=osq[0:nq], op=ALU.add, axis=AX.X), osq.s(), ms.s())
        P.op("dve", lambda e: e.tensor_scalar(out=ms[0:nq, :], in0=ms[0:nq, :], scalar1=1.0 / 64.0, scalar2=1e-5, op0=ALU.mult, op1=ALU.add), ms.s(), ms.s())
        P.op("dve", lambda e: e.tensor_single_scalar(out=rti[0:nq, :], in_=ms[0:nq, :].bitcast(I32), scalar=1, op=ALU.logical_shift_right), ms.s(), rti.s())
        P.op("dve", lambda e: e.tensor_scalar(out=rti[0:nq, :], in0=rti[0:nq, :], scalar1=-1.0, scalar2=1597463007.0, op0=ALU.mult, op1=ALU.add), rti.s(), rti.s())
        for it in range(2):
            ysrc = rti[0:nq, :].bitcast(F32) if it == 0 else rstd[0:nq, :]
            P.op("dve", lambda e, ysrc=ysrc: e.tensor_tensor(out=rtn[0:nq, :], in0=ysrc, in1=ysrc, op=ALU.mult), rti.s() + rstd.s(), rtn.s())
            P.op("dve", lambda e: e.tensor_tensor(out=rtn[0:nq, :], in0=rtn[0:nq, :], in1=ms[0:nq, :], op=ALU.mult), rtn.s() + ms.s(), rtn.s())
            P.op("dve", lambda e: e.tensor_scalar(out=rtn[0:nq, :], in0=rtn[0:nq, :], scalar1=-0.5, scalar2=1.5, op0=ALU.mult, op1=ALU.add), rtn.s(), rtn.s())
            P.op("dve", lambda e, ysrc=ysrc: e.tensor_tensor(out=rstd[0:nq, :], in0=ysrc, in1=rtn[0:nq, :], op=ALU.mult), rtn.s() + rti.s() + rstd.s(), rstd.s())
        P.op("dve", lambda e: e.tensor_tensor(out=o1[0:nq], in0=o1[0:nq], in1=rstd[0:nq, :].unsqueeze(2).to_broadcast([nq, 8, 64]), op=ALU.mult),
             o1.s() + rstd.s(), o1.s())
        P.op("dve", lambda e: e.tensor_tensor(out=am[0:nq, :].rearrange("p (h d) -> p h d", d=64), in0=o1[0:nq],
                                              in1=gsub[0:nq, :].unsqueeze(1).to_broadcast([nq, 8, 64]), op=ALU.mult),
             o1.s() + gsub.s() + am.s(), am.s())
        if hs == 1:
            pendT.append((qt, am, nq, qi + 3))

    pendT = []
    cur_done = [-1]

    def emit_T(force=False):
        keep = []
        for item in pendT:
            qt, am, nq, due = item
            if not force and not (cur_done[0] >= due):
                keep.append(item)
                continue
            pT = psv(6, [128, 4, 128], BF16)
            for blk in range(4):
                P.op("pe", lambda e, blk=blk, am=am, nq=nq, pT=pT: e.transpose(out=pT[:, blk, 0:nq], in_=am[0:nq, blk * 128:(blk + 1) * 128], identity=ident[0:nq, 0:nq]),
                     am.s() + ident.s(), PS.s(6))
            P.op("act", lambda e, qt=qt, nq=nq, pT=pT: e.copy(out=attnT[:, :, qt["qcol0"]:qt["qcol0"] + nq], in_=pT[:, :, 0:nq]), PS.s(6), qt["dst"])
        pendT[:] = keep

    for idx in range(3):
        cache_load(idx)
    cdone = [0]
    for si, sp in enumerate(steps):
        emit_qk(si, sp)
        if si >= 1:
            emit_av(si - 1, steps[si - 1])
            pv_ = steps[si - 1]
            if pv_["last"] and pv_["hs"] == 0:
                cur_done[0] = pv_["qi"]
        emit_T()
        if si >= 2:
            P.flush(early_ops, 1)
        if sp["first"] and sp["hs"] == 0 and cdone[0] < 16 and sp["qi"] >= 1:
            cache_tr(cdone[0])
            if cdone[0] + 3 < 16:
                cache_load(cdone[0] + 3)
            cdone[0] += 1
    emit_av(len(steps) - 1, steps[-1])
    emit_T(force=True)
    P.flush(early_ops, len(early_ops))
    assert cdone[0] == 16

    if debug:
        dbg_attn = dout("dbg_attnT", [128, 4, NTOK], BF16)
        P.dma("sp", dbg_attn[:], attnT[:], reads=attnT.s(), writes=dbg_attn.s())
    if stage <= 2:
        P.emit(st)
        return

    P.barrier()
    cur[0] = keep_end
    BB = carve([128, 16, 2, 32], BF16)
    Cint = carve([128, 16, 2, 9, 32], BF16)
    Tz = carve([128, 4, 8, 128], BF16)
    Wb_off = cur[0]
    Wb = carve([128, 4, 2, 8, 128], BF16)
    lvl2_start = cur[0]

    def bc16(v):
        return v.unsqueeze(2).to_broadcast([128, 16, 16])

    Bbr = carve([128, 16, 16], F32)
    Bbi = carve([128, 16, 16], F32)
    T1 = carve([128, 16, 16], F32)
    T2 = carve([128, 16, 16], F32)
    TT("dve", T1[:], Bre[:], bc16(fre), MUL, Bre.s() + SMs, T1.s())
    TT("pool", T2[:], Bim[:], bc16(fim), MUL, Bim.s() + SMs, T2.s())
    TT("dve", Bbr[:], T1[:], T2[:], SUB, T1.s() + T2.s(), Bbr.s())
    TT("dve", T1[:], Bim[:], bc16(fre), MUL, Bim.s() + SMs, T1.s())
    TT("pool", T2[:], Bre[:], bc16(fim), MUL, Bre.s() + SMs, T2.s())
    TT("dve", Bbi[:], T1[:], T2[:], ADD, T1.s() + T2.s(), Bbi.s())
    P.op("pool", lambda e: e.memset(BB[:], 0.0), [], BB.s())
    for g2 in range(2):
        sl = slice(64 * g2, 64 * (g2 + 1))
        P.op("dve", lambda e, sl=sl, g2=g2: e.tensor_copy(out=BB[sl, :, 0, 16 * g2:16 * (g2 + 1)], in_=Bbr[sl]), Bbr.s() + BB.s(), BB.s())
        P.op("dve", lambda e, sl=sl, g2=g2: e.tensor_copy(out=BB[sl, :, 1, 16 * g2:16 * (g2 + 1)], in_=Bbi[sl]), Bbi.s() + BB.s(), BB.s())
    P.op("act", lambda e: e.memzero(Cint[:]), [], Cint.s())
    CW1 = carve([128, 16, 9, 16], F32)
    CW2 = carve([128, 16, 9, 16], F32)

    def bcC(v):
        return v.unsqueeze(2).to_broadcast([128, 16, 9, 16])

    def bcP(v):
        return v.rearrange("p m q -> p q m").unsqueeze(3).to_broadcast([128, 16, 9, 16])

    TT("dve", CW1[:], bcC(Cre[:]), bcP(PWr[:]), MUL, Cre.s() + PWs, CW1.s())
    TT("dve", CW2[:], bcC(Cim[:]), bcP(PWi[:]), MUL, Cim.s() + PWs, CW2.s())
    TT("dve", CW1[:], CW1[:], CW2[:], SUB, CW1.s() + CW2.s(), CW1.s())
    for g2 in range(2):
        sl = slice(64 * g2, 64 * (g2 + 1))
        P.op("act", lambda e, sl=sl, g2=g2: e.copy(out=Cint[sl, :, 0, :, 16 * g2:16 * (g2 + 1)], in_=CW1[sl]), CW1.s() + Cint.s(), Cint.s())
    CW3 = carve([128, 16, 9, 16], F32)
    TT("dve", CW3[:], bcC(Cre[:]), bcP(PWi[:]), MUL, Cre.s() + PWs, CW3.s())
    TT("dve", CW2[:], bcC(Cim[:]), bcP(PWr[:]), MUL, Cim.s() + PWs + CW2.s(), CW2.s())
    STT("dve", CW3[:], CW3[:], -1.0, CW2[:], MUL, SUB, CW3.s() + CW2.s(), CW3.s())
    for g2 in range(2):
        sl = slice(64 * g2, 64 * (g2 + 1))
        P.op("act", lambda e, sl=sl, g2=g2: e.copy(out=Cint[sl, :, 1, :, 16 * g2:16 * (g2 + 1)], in_=CW3[sl]), CW3.s() + Cint.s(), Cint.s())
    Tzf = carve([128, 4, 128], F32)
    P.op("pool", lambda e: e.memset(Tz[:], 0.0), [], Tz.s())
    P.op("pool", lambda e: e.memset(Tzf[:], 0.0), [], Tzf.s())
    pz = psv(5, [128, 32, 32], nb=2)
    for pair in range(16):
        gb, m = pair // 4, pair % 4
        for tau in range(8):
            for ri in range(2):
                P.op("pe", lambda e, pair=pair, gb=gb, m=m, tau=tau, ri=ri: e.matmul(
                    pz[32 * m:32 * (m + 1), gb * 8 + tau, :], lhsT=BB[:, pair, ri, :], rhs=Cint[:, pair, ri, tau, :],
                    start=(ri == 0), stop=(ri == 1), tile_position=(0, 32 * m), skip_group_check=True), BB.s() + Cint.s(), PS.s(range(5, 7)))
    pzv = pz.rearrange("p (gb t) c -> p gb t c", t=8)
    for m in range(4):
        sl = slice(32 * m, 32 * (m + 1))
        P.op("dve", lambda e, sl=sl: e.tensor_copy(out=Tz[sl, :, 1:8, sl], in_=pzv[sl, :, 1:8, :]), PS.s(range(5, 7)) + Tz.s(), Tz.s())
        P.op("dve", lambda e, sl=sl: e.tensor_copy(out=Tzf[sl, :, sl], in_=pzv[sl, :, 0, :]), PS.s(range(5, 7)) + Tzf.s(), Tzf.s())
    for gb in range(4):
        STT("dve", Tz[:, gb, 0, :], identf[:], Dcol[:, gb:gb + 1], Tzf[:, gb, :], MUL, ADD, identf.s() + Dcol.s() + Tzf.s() + Tz.s(), Tz.s())
    Nat = carve([128, 4, 2, 8, 128], F32)
    P.op("pool", lambda e: e.memset(Nat[:], 0.0), [], Nat.s())
    Natv = Nat[:].rearrange("p gb r j (m g c) -> p gb r j m g c", m=4, g=2)
    NW1 = carve([128, 16, 8, 16], F32)
    NW2 = carve([128, 16, 8, 16], F32)

    def bcB(v):
        return v.unsqueeze(2).to_broadcast([128, 16, 8, 16])

    def bcP8(v):
        return v[:, 0:8, :].rearrange("p m q -> p q m").unsqueeze(3).to_broadcast([128, 16, 8, 16])

    for ri in range(2):
        if ri == 0:
            TT("dve", NW1[:], bcB(Bbr[:]), bcP8(PWr), MUL, Bbr.s() + PWs + NW1.s(), NW1.s())
            TT("dve", NW2[:], bcB(Bbi[:]), bcP8(PWi), MUL, Bbi.s() + PWs + NW2.s(), NW2.s())
            TT("dve", NW1[:], NW1[:], NW2[:], SUB, NW1.s() + NW2.s(), NW1.s())
        else:
            TT("dve", NW1[:], bcB(Bbr[:]), bcP8(PWi), MUL, Bbr.s() + PWs + NW1.s(), NW1.s())
            TT("dve", NW2[:], bcB(Bbi[:]), bcP8(PWr), MUL, Bbi.s() + PWs + NW2.s(), NW2.s())
            TT("dve", NW1[:], NW1[:], NW2[:], ADD, NW1.s() + NW2.s(), NW1.s())
        for g2 in range(2):
            sl = slice(64 * g2, 64 * (g2 + 1))
            for gb in range(4):
                P.op("act", lambda e, sl=sl, g2=g2, gb=gb, ri=ri: e.copy(out=Natv[sl, gb, ri, :, :, g2, :],
                                                                        in_=NW1[sl, 4 * gb:4 * gb + 4, :, :].rearrange("p m q c -> p q m c")),
                     NW1.s() + Nat.s(), Nat.s())
    tcnt = 0
    for gb in range(4):
        for ri in range(2):
            for jh in range(2):
                bank = tcnt % 2
                tcnt += 1
                pW = psv(bank, [128, 4, 128])
                for jj in range(4):
                    j = jh * 4 + jj
                    P.op("pe", lambda e, gb=gb, ri=ri, j=j, jj=jj, pW=pW: e.transpose(out=pW[:, jj, :], in_=Nat[:, gb, ri, 7 - j, :], identity=identf[:]),
                         Nat.s() + identf.s(), PS.s(bank))
                P.op("act" if tcnt % 2 else "dve",
                     (lambda e, gb=gb, ri=ri, jh=jh, pW=pW: e.copy(out=Wb[:, gb, ri, 4 * jh:4 * jh + 4, :], in_=pW)) if tcnt % 2 else
                     (lambda e, gb=gb, ri=ri, jh=jh, pW=pW: e.tensor_copy(out=Wb[:, gb, ri, 4 * jh:4 * jh + 4, :], in_=pW)),
                     PS.s(bank), Wb.s())

    assert cur[0] <= TOP_OFF, (cur[0], TOP_OFF)
    P.barrier()
    cur[0] = lvl2_start
    Ec = carve([128, 8, KC], F32)
    Es = carve([128, 8, KC], F32)
    Sr = carve([128, 8, KC], F32)
    Si = carve([128, 8, KC], F32)
    Xr = carve([128, 8, KC], F32)
    Xi = carve([128, 8, KC], F32)
    U1 = carve([128, 8, KC], F32)
    U2 = carve([128, 8, KC], F32)
    Hbr = carve([128, 16, KC], BF16)
    Hbi = carve([128, 16, KC], BF16)
    Hfin = carve([128, 2, 3, 16], F32)
    assert cur[0] <= TOP_OFF, (cur[0], TOP_OFF)
    segs = [(0, 256, None), (256, 8, 0), (264, 8, 1)]
    for hf in range(2):
        ps8 = slice(8 * hf, 8 * hf + 8)
        Es_ = Ec.s() + Es.s()
        P.op("dve", lambda e, ps8=ps8: e.tensor_copy(out=Ec[:, :, 0], in_=er[:, ps8]), SMs + Es_, Es_)
        P.op("dve", lambda e, ps8=ps8: e.tensor_copy(out=Es[:, :, 0], in_=ei[:, ps8]), SMs + Es_, Es_)
        n = 1
        while n < 256:
            bc_c = Ec[:, :, n - 1:n].to_broadcast([128, 8, n])
            bc_s = Es[:, :, n - 1:n].to_broadcast([128, 8, n])
            a_c, a_s = Ec[:, :, 0:n], Es[:, :, 0:n]
            u1, u2 = U1[:, :, 0:n], U2[:, :, 0:n]
            TT("dve", u1, a_c, bc_c, MUL, Es_, U1.s())
            TT("dve", u2, a_s, bc_s, MUL, Es_, U2.s())
            TT("dve", Ec[:, :, n:2 * n], u1, u2, SUB, U1.s() + U2.s() + Es_, Es_)
            TT("dve", u1, a_c, bc_s, MUL, Es_, U1.s())
            TT("dve", u2, a_s, bc_c, MUL, Es_, U2.s())
            TT("dve", Es[:, :, n:2 * n], u1, u2, ADD, U1.s() + U2.s() + Es_, Es_)
            n *= 2
        for c0 in (256, 264):
            P.op("dve", lambda e, c0=c0: e.tensor_copy(out=Ec[:, :, c0:c0 + 8], in_=Ec[:, :, 0:8]), Es_, Es_)
            P.op("dve", lambda e, c0=c0: e.tensor_copy(out=Es[:, :, c0:c0 + 8], in_=Es[:, :, 0:8]), Es_, Es_)
        for gbl in range(2):
            gb = 2 * hf + gbl
            for ri in range(2):
                b0 = 4 * ((gbl * 2 + ri) % 2)
                for j in range(8):
                    for m in range(4):
                        P.op("pe", lambda e, gb=gb, ri=ri, j=j, m=m, b0=b0: e.matmul(
                            psum[:, b0 + m, 0:KC], lhsT=Wb[32 * m:32 * (m + 1), gb, ri, j, :], rhs=uD[32 * m:32 * (m + 1), gb, j, :],
                            start=(j == 0), stop=(j == 7), tile_position=(32 * m, 0)), Wb.s() + uD.s(), PS.s(range(b0, b0 + 4)))
                dst = Sr if ri == 0 else Si
                P.op("act", lambda e, dst=dst, gbl=gbl, b0=b0: e.copy(out=dst[:, 4 * gbl:4 * gbl + 4, :], in_=psum[:, b0:b0 + 4, 0:KC]),
                     PS.s(range(b0, b0 + 4)), dst.s())
        TT("dve", U1[:], Ec[:], Sr[:], MUL, Es_ + Sr.s(), U1.s())
        TT("dve", U2[:], Es[:], Si[:], MUL, Es_ + Si.s(), U2.s())
        TT("dve", Xr[:], U1[:], U2[:], ADD, U1.s() + U2.s(), Xr.s())
        TT("dve", U1[:], Ec[:], Si[:], MUL, Es_ + Si.s(), U1.s())
        TT("dve", U2[:], Es[:], Sr[:], MUL, Es_ + Sr.s(), U2.s())
        TT("dve", Xi[:], U1[:], U2[:], SUB, U1.s() + U2.s(), Xi.s())
        for pl in range(8):
            pair = 8 * hf + pl
            for (c0, n, sidx) in segs:
                for part, (src, dst) in enumerate(((Xr, Sr), (Xi, Si))):
                    if sidx is None:
                        init = 0.0
                        rdi = []
                    else:
                        init = h0[:, 2 * sidx + part, pair:pair + 1]
                        rdi = h0.s()
                    P.op("dve", lambda e, src=src, dst=dst, pl=pl, pair=pair, c0=c0, n=n, init=init: e.tensor_tensor_scan(
                        out=dst[:, pl, c0:c0 + n], data0=r8[:, pair:pair + 1].to_broadcast([128, n]), data1=src[:, pl, c0:c0 + n],
                        initial=init, op0=MUL, op1=ADD), src.s() + SMs + rdi + dst.s(), dst.s())
        TT("dve", U1[:], Ec[:], Sr[:], MUL, Es_ + Sr.s(), U1.s())
        TT("dve", U2[:], Es[:], Si[:], MUL, Es_ + Si.s(), U2.s())
        TT("dve", Xr[:], U1[:], U2[:], SUB, U1.s() + U2.s() + Xr.s(), Xr.s())
        TT("dve", U1[:], Ec[:], Si[:], MUL, Es_ + Si.s(), U1.s())
        TT("dve", U2[:], Es[:], Sr[:], MUL, Es_ + Sr.s(), U2.s())
        TT("dve", Xi[:], U1[:], U2[:], ADD, U1.s() + U2.s() + Xi.s(), Xi.s())
        for part, (X_, Hb) in enumerate(((Xr, Hbr), (Xi, Hbi))):
            for si, (c0, n, sidx) in enumerate(segs):
                P.op("pool", lambda e, X_=X_, part=part, si=si, c0=c0, n=n, ps8=ps8: e.tensor_copy(out=Hfin[:, part, si, ps8], in_=X_[:, :, c0 + n - 1]),
                     X_.s() + Hfin.s(), Hfin.s())
                P.op("act", lambda e, X_=X_, Hb=Hb, c0=c0, n=n, ps8=ps8: e.copy(out=Hb[:, ps8, c0 + 1:c0 + n], in_=X_[:, :, c0:c0 + n - 1]),
                     X_.s() + Hb.s(), Hb.s())
                if sidx is None:
                    P.op("pool", lambda e, Hb=Hb, c0=c0, ps8=ps8: e.memset(Hb[:, ps8, c0:c0 + 1], 0.0), Hb.s(), Hb.s())
                else:
                    P.op("pool", lambda e, Hb=Hb, c0=c0, ps8=ps8, sidx=sidx, part=part: e.tensor_copy(out=Hb[:, ps8, c0], in_=h0[:, 2 * sidx + part, ps8]),
                         h0.s() + Hb.s(), Hb.s())
    P.barrier()
    wglu_bf = carve([128, 4, 1024], BF16, at=Wb_off)
    wout_bf = carve([128, 8, 1024], BF16, at=lvl2_start)
    for gc in range(4):
        P.dma("pool", wglu_bf[:, gc, :], w_glu[gc * 128:(gc + 1) * 128, :], writes=wglu_bf.s())
    for kc in range(8):
        P.dma("pool", wout_bf[:, kc, :], w_out[kc * 128:(kc + 1) * 128, :], writes=wout_bf.s())
    pHf = psv(6, [128, 6, 128], nb=2)
    Hn2 = carve([128, 6, 128], F32)
    for part in range(2):
        for si in range(3):
            P.op("pe", lambda e, part=part, si=si: e.transpose(out=pHf[0:16, part * 3 + si, :], in_=Hfin[:, part, si, :], identity=identf[:]),
                 Hfin.s() + identf.s(), PS.s(7))
    P.op("dve", lambda e: e.tensor_copy(out=Hn2[0:16], in_=pHf[0:16]), PS.s(range(6, 8)), Hn2.s())
    for si in range(3):
        P.dma("sp", h_re_out[si].rearrange("(pair g2) p -> pair (g2 p)", g2=2), Hn2[0:16, si, :], reads=Hn2.s(), writes=h_re_out.s())
        P.dma("sp", h_im_out[si].rearrange("(pair g2) p -> pair (g2 p)", g2=2), Hn2[0:16, 3 + si, :], reads=Hn2.s(), writes=h_im_out.s())

    G = [carve([128, KC], F32) for _ in range(4)]
    gTv = gT[:].rearrange("p g (k b) -> p g k b", b=8)
    ycnt = 0
    for gb in range(4):
        for b in range(8):
            bank = ycnt % 4
            ycnt += 1
            py = psum[:, bank, 0:KC]
            for j in range(b + 1):
                P.op("pe", lambda e, gb=gb, b=b, j=j, py=py: e.matmul(py, lhsT=Tz[:, gb, b - j, :], rhs=uD[:, gb, j, :], start=(j == 0), stop=False,
                                                                    skip_group_check=True), Tz.s() + uD.s(), PS.s(bank))
            for m in range(4):
                pair = 4 * gb + m
                for ri, Hb in enumerate((Hbr, Hbi)):
                    P.op("pe", lambda e, m=m, pair=pair, ri=ri, Hb=Hb, b=b, bank=bank: e.matmul(
                        psum[32 * m:32 * (m + 1), bank, 0:KC], lhsT=Cint[:, pair, ri, b + 1, :], rhs=Hb[:, pair, :], start=False, stop=(ri == 1),
                        tile_position=(0, 32 * m), skip_group_check=True), Cint.s() + Hb.s(), PS.s(bank))
            if b % 2 == 1:
                b0 = bank - 1
                P.op("act", lambda e, gb=gb, b=b, b0=b0: e.activation(out=gTv[:, gb, :, b - 1:b + 1],
                                                                     in_=psum[:, b0:b0 + 2, 0:KC].rearrange("p j k -> p k j"),
                                                                     func=AF.Gelu_apprx_tanh), PS.s(range(b0, b0 + 2)), gT.s())

    if debug:
        dbg_g = dout("dbg_gT", [128, 4, NTOK], BF16)
        P.dma("sp", dbg_g[:], gT[:], reads=gT.s(), writes=dbg_g.s())
    if stage <= 3:
        P.emit(st)
        return

    P.barrier()
    cur[0] = keep_end
    W_UP_OFF = ARENA_BYTES - 64 * 1024
    w_up_bf = carve([128, 8, 4096], BF16, 8, at=W_UP_OFF)
    bgn = carve([128, 128], F32)
    P.dma("sp", bgn[0:8, :], b_glu[0:1, :].rearrange("a (f p) -> (a f) p", p=128), writes=bgn.s())
    pbg = psv(0, [128, 8])
    P.op("pe", lambda e: e.transpose(out=pbg, in_=bgn[0:8, :], identity=identf[0:8, 0:8]), bgn.s() + identf.s(), PS.s(0))
    bglu = carve([128, 8], F32)
    nbglu = carve([128, 8], F32)
    P.op("dve", lambda e: e.tensor_copy(out=bglu[:], in_=pbg), PS.s(0), bglu.s())
    TS("dve", nbglu[:], bglu[:], -1.0, None, MUL, ALU.bypass, bglu.s(), nbglu.s())
    def make_ln(gd, bd):
        L = {}
        L["g"] = carve([128, 1024], F32)
        L["b"] = carve([128, 1024], F32)
        P.dma("sp", L["g"][:], gd[0:1, :].to_broadcast([128, 1024]), writes=L["g"].s())
        P.dma("sp", L["b"][:], bd[0:1, :].to_broadcast([128, 1024]), writes=L["b"].s())
        L["stats"] = carve([128, 2, 6], F32)
        L["mv"] = carve([128, 2], F32)
        L["rstd"] = carve([128, 1], F32)
        L["nmr"] = carve([128, 1], F32)
        L["ti"] = carve([128, 1], I32)
        L["tn"] = carve([128, 2], F32)
        return L
    LN1 = make_ln(ln1_g, ln1_b)
    xs2 = [carve([128, 1024], F32) for _ in range(2)]
    eg = [carve([128, 4, 128], F32) for _ in range(2)]
    soT = [carve([128, 4, 128], BF16) for _ in range(2)]
    assert cur[0] <= Wb_off, (cur[0], Wb_off)
    cur[0] = Wb_off + 8 * 1024
    y1 = [carve([128, 1024], F32, 2) for _ in range(2)]
    assert cur[0] <= lvl2_start, (cur[0], lvl2_start)
    NFA = 12
    cur[0] = lvl2_start + 16 * 1024
    wdnA_off = cur[0]
    wdnA = carve([128, NFA, 1024], BF16, NFA)
    assert cur[0] <= W_UP_OFF, cur[0]

    def layer_norm(src, dst, L):
        stats, mv, rstd1, nmr, ti, tn = L["stats"], L["mv"], L["rstd"], L["nmr"], L["ti"], L["tn"]
        for hh_ in range(2):
            P.op("dve", lambda e, hh_=hh_: e.bn_stats(out=stats[:, hh_, :], in_=src[:, 512 * hh_:512 * (hh_ + 1)]), src.s(hh_) + stats.s(), stats.s())
        P.op("dve", lambda e: e.bn_aggr(out=mv[:], in_=stats[:].rearrange("p a b -> p (a b)")), stats.s(), mv.s())
        TS("dve", tn[:, 0:1], mv[:, 1:2], 1e-5, None, ADD, ALU.bypass, mv.s(), tn.s())
        P.op("dve", lambda e: e.tensor_single_scalar(out=ti[:], in_=tn[:, 0:1].bitcast(I32), scalar=1, op=ALU.logical_shift_right), tn.s(), ti.s())
        TS("dve", ti[:], ti[:], -1.0, 1597463007.0, MUL, ADD, ti.s(), ti.s())
        for it in range(2):
            ysrc = ti[:].bitcast(F32) if it == 0 else rstd1[:]
            STT("dve", tn[:, 1:2], ysrc, tn[:, 0:1], ysrc, MUL, MUL, ti.s() + rstd1.s() + tn.s(), tn.s())
            TS("dve", tn[:, 1:2], tn[:, 1:2], -0.5, 1.5, MUL, ADD, tn.s(), tn.s())
            TT("dve", rstd1[:], ysrc, tn[:, 1:2], MUL, tn.s() + ti.s() + rstd1.s(), rstd1.s())
        STT("dve", nmr[:], mv[:, 0:1], -1.0, rstd1[:], MUL, MUL, mv.s() + rstd1.s(), nmr.s())
        for hh_ in range(2):
            sl = slice(512 * hh_, 512 * (hh_ + 1))
            P.op("act", lambda e, sl=sl: e.activation(out=dst[:, sl], in_=src[:, sl], func=AF.Identity, scale=rstd1[:, 0:1], bias=nmr[:, 0:1]),
                 src.s(hh_) + rstd1.s() + nmr.s() + dst.s(hh_), dst.s(hh_))
            TT("dve", dst[:, sl], dst[:, sl], L["g"][:, sl], MUL, dst.s(hh_) + L["g"].s(), dst.s(hh_))
            TT("dve", dst[:, sl], dst[:, sl], L["b"][:, sl], ADD, dst.s(hh_) + L["b"].s(), dst.s(hh_))

    def c1_G(t):
        b = t % 2
        cols = slice(t * 128, (t + 1) * 128)
        P.dma("sp", xs2[b][:], x_all[t * 128:(t + 1) * 128, :], writes=xs2[b].s())
        zb = 2 * b
        pz_ = psv(zb, [128, 8, 128], nb=2)
        for fb in range(8):
            for gc in range(4):
                P.op("pe", lambda e, fb=fb, gc=gc, cols=cols, pz_=pz_: e.matmul(pz_[:, fb, :], lhsT=wglu_bf[:, gc, fb * 128:(fb + 1) * 128], rhs=gT[:, gc, cols],
                                                                               start=(gc == 0), stop=(gc == 3), skip_group_check=True), wglu_bf.s() + gT.s(), PS.s(range(zb, zb + 2)))
        for f4 in range(4):
            P.op("act", lambda e, f4=f4, b=b, pz_=pz_: e.activation(out=eg[b][:, f4, :], in_=pz_[:, 4 + f4, :], func=AF.Sigmoid, bias=bglu[:, 4 + f4:5 + f4]),
                 PS.s(range(zb, zb + 2)) + bglu.s() + eg[b].s(), eg[b].s())
        for f4 in range(4):
            STT("dve", soT[b][:, f4, :], pz_[:, f4, :], bglu[:, f4:f4 + 1], eg[b][:, f4, :], ADD, MUL, PS.s(range(zb, zb + 2)) + bglu.s() + eg[b].s() + soT[b].s(), soT[b].s())

    def c1_M(t):
        b = t % 2
        cols = slice(t * 128, (t + 1) * 128)
        mb = 4 + 2 * b
        pm_ = psv(mb, [128, 1024], nb=2)
        for hh_ in range(2):
            for kc in range(8):
                lh = attnT[:, kc, cols] if kc < 4 else soT[b][:, kc - 4, :]
                rd = attnT.s(t) if kc < 4 else soT[b].s()
                P.op("pe", lambda e, hh_=hh_, kc=kc, lh=lh, pm_=pm_: e.matmul(pm_[:, 512 * hh_:512 * (hh_ + 1)], lhsT=lh, rhs=wout_bf[:, kc, 512 * hh_:512 * (hh_ + 1)],
                                                                             start=(kc == 0), stop=(kc == 7)), rd + wout_bf.s(), PS.s(mb + hh_))
        for hh_ in range(2):
            STT("dve", y1[b][:, 512 * hh_:512 * (hh_ + 1)], xs2[b][:, 512 * hh_:512 * (hh_ + 1)], ALPHA, pm_[:, 512 * hh_:512 * (hh_ + 1)], MUL, ADD,
                xs2[b].s() + PS.s(mb + hh_) + y1[b].s(hh_), y1[b].s(hh_))
        layer_norm(y1[b], y1[b], LN1)
        P.dma("sp", x1_scr[t * 128:(t + 1) * 128, :], y1[b][:], reads=y1[b].s(), writes=x1_scr.s(t))

    c1_G(0)
    for t in range(NT):
        if t + 1 < NT:
            c1_G(t + 1)
        c1_M(t)
        if t == 1:
            for dc in range(8):
                for hh_ in range(2):
                    P.dma("pool", w_up_bf[:, dc, 2048 * hh_:2048 * (hh_ + 1)], w_up[dc * 128:(dc + 1) * 128, 2048 * hh_:2048 * (hh_ + 1)], writes=w_up_bf.s(dc))
            for fc in range(0, NFA, 2):
                P.dma("pool", wdnA[:, fc:fc + 2, :], w_down[fc * 128:(fc + 2) * 128, :].rearrange("(a p) n -> p a n", p=128), writes=wdnA.s(range(fc, fc + 2)))


    if debug:
        dbg_x1 = dout("dbg_x1", [NTOK, 1024], F32)
        P.dma("sp", dbg_x1[:], x1_scr[:], reads=x1_scr.s(), writes=dbg_x1.s())

    P.barrier(include_dma=("sp",))
    cur[0] = uD_start
    LN2 = make_ln(ln2_g, ln2_b)
    wdnB = carve([128, 32 - NFA, 1024], BF16, 32 - NFA)

    def wdn(fc):
        if fc < NFA:
            return wdnA[:, fc, :], wdnA.s(fc)
        return wdnB[:, fc - NFA, :], wdnB.s(fc - NFA)

    x1g = [carve([128, 2, 1024], F32) for _ in range(2)]
    x1b = [carve([128, 2, 1024], BF16) for _ in range(2)]
    x1T = [carve([128, 8, 256], BF16) for _ in range(2)]
    hT = carve([128, 32, 256], BF16)
    sq = [carve([128, 256], F32) for _ in range(2)]
    y2 = [carve([128, 1024], F32, 2) for _ in range(2)]
    assert cur[0] <= wdnA_off, (cur[0], wdnA_off)
    groups = [(2 * g, 2) for g in range(8)] + [(16, 1)]
    ucnt = [0]
    ycnt2 = [0]

    def c2_X(gi):
        t0, ntl = groups[gi]
        b = gi % 2
        for tl in range(ntl):
            P.dma("sp", x1g[b][:, tl, :], x1_scr[(t0 + tl) * 128:(t0 + tl + 1) * 128, :], reads=x1_scr.s(t0 + tl), writes=x1g[b].s())
            P.dma("pool", x1b[b][:, tl, :], x1_scr[(t0 + tl) * 128:(t0 + tl + 1) * 128, :], reads=x1_scr.s(t0 + tl), writes=x1b[b].s())
        for tl in range(ntl):
            pxT = psv(7, [128, 8, 128], BF16)
            for dc in range(8):
                P.op("pe", lambda e, dc=dc, tl=tl, pxT=pxT, b=b: e.transpose(out=pxT[:, dc, :], in_=x1b[b][:, tl, dc * 128:(dc + 1) * 128], identity=ident[:]),
                     x1b[b].s() + ident.s(), PS.s(7))
            P.op("act", lambda e, b=b, tl=tl, pxT=pxT: e.copy(out=x1T[b][:, :, tl * 128:(tl + 1) * 128], in_=pxT), PS.s(7) + x1T[b].s(), x1T[b].s())

    def c2_U(gi):
        t0, ntl = groups[gi]
        b = gi % 2
        ntok = 128 * ntl
        for fc in range(32):
            bank = ucnt[0] % 3
            ucnt[0] += 1
            ph = psum[:, bank, 0:ntok]
            for dc in range(8):
                P.op("pe", lambda e, fc=fc, dc=dc, b=b, ph=ph, ntok=ntok: e.matmul(ph, lhsT=w_up_bf[:, dc, fc * 128:(fc + 1) * 128], rhs=x1T[b][:, dc, 0:ntok],
                                                                                  start=(dc == 0), stop=(dc == 7)), w_up_bf.s(dc) + x1T[b].s(), PS.s(bank))
            sqb = sq[fc % 2]
            P.op("act", lambda e, sqb=sqb, ph=ph, ntok=ntok: e.activation(out=sqb[:, 0:ntok], in_=ph, func=AF.Square), PS.s(bank), sqb.s())
            STT("dve", hT[:, fc, 0:ntok], ph, 0.0, sqb[:, 0:ntok], ALU.is_gt, MUL, PS.s(bank) + sqb.s() + hT.s(), hT.s())

    pend_ln = []

    def c2_D(gi):
        t0, ntl = groups[gi]
        b = gi % 2
        while pend_ln:
            pend_ln.pop(0)()
        for tl in range(ntl):
            t = t0 + tl
            yb = y2[ycnt2[0] % 2]
            ycnt2[0] += 1
            for hh_ in range(2):
                bank = 3 + 2 * tl + hh_
                pd = psv(bank, [128, 512])
                for fc in range(32):
                    wv, wr = wdn(fc)
                    P.op("pe", lambda e, fc=fc, tl=tl, hh_=hh_, pd=pd, wv=wv: e.matmul(pd, lhsT=hT[:, fc, tl * 128:(tl + 1) * 128], rhs=wv[:, 512 * hh_:512 * (hh_ + 1)],
                                                                                       start=(fc == 0), stop=(fc == 31)), hT.s() + wr, PS.s(bank))
                STT("dve", yb[:, 512 * hh_:512 * (hh_ + 1)], x1g[b][:, tl, 512 * hh_:512 * (hh_ + 1)], ALPHA, pd, MUL, ADD,
                    x1g[b].s() + PS.s(bank) + yb.s(hh_), yb.s(hh_))
            def tail(yb=yb, t=t):
                layer_norm(yb, yb, LN2)
                P.dma("sp", y_out[t * 128:(t + 1) * 128, :], yb[:], reads=yb.s(), writes=y_out.s())
            pend_ln.append(tail)

    c2_X(0)
    for fc in range(NFA, 32, 2):
        P.dma("pool", wdnB[:, fc - NFA:fc - NFA + 2, :], w_down[fc * 128:(fc + 2) * 128, :].rearrange("(a p) n -> p a n", p=128),
              writes=wdnB.s(range(fc - NFA, fc - NFA + 2)))
    for gi in range(len(groups)):
        c2_U(gi)
        if gi + 1 < len(groups):
            c2_X(gi + 1)
        c2_D(gi)
    while pend_ln:
        pend_ln.pop(0)()

    P.emit(st)


def make_in_maps(inp):
    f = lambda a: np.ascontiguousarray(a, dtype=np.float32)
    maps = []
    lam_p = np.stack([inp["lambda_q1"][0], inp["lambda_k1"][0], inp["lambda_q2"][0], inp["lambda_k2"][0]])[None]
    for c in range(8):
        xa = np.concatenate([inp["x_prompt"][c], inp["x_sample"][2 * c], inp["x_sample"][2 * c + 1]], axis=0)
        m = {
            "x_all": f(xa),
            "cache_k": f(inp["cache_k"][0, 2 * c:2 * c + 2].reshape(2, 1024, 512)),
            "cache_v": f(inp["cache_v"][0, 2 * c:2 * c + 2].reshape(2, 1024, 512)),
            "st_re": f(inp["state_ssm_re"][0, 2 * c:2 * c + 2]),
            "st_im": f(inp["state_ssm_im"][0, 2 * c:2 * c + 2]),
            "w_in": f(inp["w_in"][0]),
            "lam_p": f(lam_p),
            "subln_g": f(inp["subln_g"]),
            "a_re": f(inp["ssm_a_re"][0]),
            "a_im": f(inp["ssm_a_im"][0]),
            "log_dt": f(inp["ssm_log_dt"]),
            "b_re": f(inp["ssm_b_re"][0]),
            "b_im": f(inp["ssm_b_im"][0]),
            "c_re": f(inp["ssm_c_re"][0]),
            "c_im": f(inp["ssm_c_im"][0]),
            "ssm_d": f(inp["ssm_d"][0]),
            "w_glu": f(inp["w_glu"][0]),
            "b_glu": f(inp["b_glu"]),
            "w_out": f(inp["w_out"][0]),
            "ln1_g": f(inp["ln1_g"]),
            "ln1_b": f(inp["ln1_b"]),
            "w_up": f(inp["w_up"][0]),
            "w_down": f(inp["w_down"][0]),
            "ln2_g": f(inp["ln2_g"]),
            "ln2_b": f(inp["ln2_b"]),
        }
        maps.append(m)
    return maps


_NC_CACHE = {}


def kernel(**inp):
    inp = {k: np.asarray(v) for k, v in inp.items()}
    if "nc" not in _NC_CACHE:
        _NC_CACHE["nc"] = build_nc()
    nc = _NC_CACHE["nc"]
    maps = make_in_maps(inp)
    res = run_bass_kernel_spmd(nc, maps, core_ids=list(range(8)))
    R = res.results
    y_prompt = np.stack([R[c]["y_out"][:2048] for c in range(8)])
    y_sample = np.stack([R[c]["y_out"][2048 + 64 * s:2048 + 64 * (s + 1)] for c in range(8) for s in range(2)])
    k_prompt = np.stack([R[c]["k_out"][:2048].reshape(2048, 8, 64) for c in range(8)])[None]
    v_prompt = np.stack([R[c]["v_out"][:2048].reshape(2048, 8, 64) for c in range(8)])[None]
    k_sample = np.stack([R[c]["k_out"][2048 + 64 * s:2048 + 64 * (s + 1)].reshape(64, 8, 64) for c in range(8) for s in range(2)])[None]
    v_sample = np.stack([R[c]["v_out"][2048 + 64 * s:2048 + 64 * (s + 1)].reshape(64, 8, 64) for c in range(8) for s in range(2)])[None]
    re_p = np.stack([R[c]["h_re_out"][0] for c in range(8)])[None]
    im_p = np.stack([R[c]["h_im_out"][0] for c in range(8)])[None]
    re_s = np.stack([R[c]["h_re_out"][1 + s] for c in range(8) for s in range(2)])[None]
    im_s = np.stack([R[c]["h_im_out"][1 + s] for c in range(8) for s in range(2)])[None]
    f = lambda a: np.ascontiguousarray(a, dtype=np.float32)
    return (f(y_prompt), f(y_sample), f(k_prompt), f(v_prompt), f(re_p), f(im_p),
            f(k_sample), f(v_sample), f(re_s), f(im_s))
```
